# Optimizing a Trainium2 kernel written in Bass

```python
import math
import jax, jax.numpy as jnp
from jax import lax
import numpy as np

D_MODEL = 2048
BATCH = 2
SEQ = 4096
DEPTH = 2

GRID_W = 64
D_MIX = D_MODEL
N_MIXERS = 4
D_GROUP = D_MIX // N_MIXERS
HEAD_DIM = 64
N_NA_HEADS = D_GROUP // HEAD_DIM
N_RET_HEADS = D_GROUP // HEAD_DIM
N_FFT_GROUPS = 4
FFT_GROUP_DIM = D_GROUP // N_FFT_GROUPS
NA_KH_MAX = 8
NA_KW = 16
RET_CHUNK = 128
CONV_WIDTH = 31
ROPE_BASE = 10000.0
EPS = 1e-6
N_IN_PIECES = 13
D_IN = N_IN_PIECES * D_GROUP

kernel_name = "hybrid_parallel_fourier_natten_retnet_conformer"


def _rmsnorm(x, g):
    xf = x.astype(jnp.float32)
    ms = jnp.mean(xf * xf, axis=-1, keepdims=True)
    return (xf * lax.rsqrt(ms + EPS)).astype(x.dtype) * g


def _fourier_mix(u, w_fft):
    B, S, _ = u.shape
    uf = u.astype(jnp.float32).reshape(B, S, N_FFT_GROUPS, FFT_GROUP_DIM)
    y = jnp.real(jnp.fft.fft2(uf, axes=(1, 3), norm="ortho"))
    return y.reshape(B, S, D_GROUP).astype(u.dtype) @ w_fft


def _neighbourhood_attention(q, k, v, rel_bias):
    B, S, _ = q.shape
    rows = S // GRID_W
    kh = min(NA_KH_MAX, rows)

    def grid(t):
        return t.reshape(B, rows, GRID_W, N_NA_HEADS, HEAD_DIM)

    qg = grid(q) * (HEAD_DIM ** -0.5)
    kg, vg = grid(k), grid(v)
    r = jnp.arange(rows)
    row_start = jnp.clip(r - kh // 2, 0, rows - kh)
    key_rows = row_start[:, None] + jnp.arange(kh)[None, :]
    kb = kg[:, key_rows]
    vb = vg[:, key_rows]
    col = jnp.arange(GRID_W)
    col_start = jnp.clip(col - NA_KW // 2, 0, GRID_W - NA_KW)
    rel_c = col[None, :] - col_start[:, None]
    col_in = (rel_c >= 0) & (rel_c < NA_KW)
    dc = col[None, :] - col[:, None]
    dr = key_rows - r[:, None]
    bias = rel_bias[:, (dr + NA_KH_MAX - 1)[:, :, None, None],
                    jnp.clip(dc + NA_KW - 1, 0, 2 * NA_KW - 2)[None, None]]
    bias = bias.transpose(0, 1, 3, 2, 4)
    s = jnp.einsum('brqhd,brakhd->bhrqak', qg, kb).astype(jnp.float32)
    s = s + bias[None].astype(jnp.float32)
    s = jnp.where(col_in[:, None, :], s, -jnp.inf)
    p = jax.nn.softmax(s.reshape(B, N_NA_HEADS, rows, GRID_W, kh * GRID_W), axis=-1)
    p = p.reshape(B, N_NA_HEADS, rows, GRID_W, kh, GRID_W).astype(v.dtype)
    o = jnp.einsum('bhrqak,brakhd->brqhd', p, vb)
    return o.reshape(B, S, D_GROUP)


def _rotary(t):
    S = t.shape[2]
    half = HEAD_DIM // 2
    inv = ROPE_BASE ** (-jnp.arange(half, dtype=jnp.float32) / half)
    ang = jnp.arange(S, dtype=jnp.float32)[:, None] * inv[None, :]
    cos, sin = jnp.cos(ang), jnp.sin(ang)
    t1, t2 = t[..., :half], t[..., half:]
    return jnp.concatenate([t1 * cos - t2 * sin, t1 * sin + t2 * cos], axis=-1)


def _retention_direction(q, k, v, log_gamma, include_diag):
    B, H, S, Dh = q.shape
    C = RET_CHUNK
    n = S // C
    qc = q.reshape(B, H, n, C, Dh)
    kc = k.reshape(B, H, n, C, Dh)
    vc = v.reshape(B, H, n, C, Dh)
    idx = jnp.arange(C, dtype=jnp.float32)
    diff = idx[:, None] - idx[None, :]
    mask = (diff >= 0) if include_diag else (diff > 0)
    lg = log_gamma[:, None, None]
    d_intra = jnp.where(mask, jnp.exp(lg * jnp.where(mask, diff, 0.0)), 0.0)
    scores = jnp.einsum('bhnid,bhnjd->bhnij', qc, kc) * d_intra[None, :, None]
    o_intra = jnp.einsum('bhnij,bhnjd->bhnid', scores, vc)
    k_decay = jnp.exp(log_gamma[:, None] * (C - 1 - idx)[None, :])
    kv = jnp.einsum('bhnjd,bhnje->nbhde', kc * k_decay[None, :, None, :, None], vc)
    chunk_decay = jnp.exp(log_gamma * C)[None, :, None, None]

    def step(state, kv_n):
        return chunk_decay * state + kv_n, state

    _, prev = lax.scan(step, jnp.zeros((B, H, Dh, Dh), jnp.float32), kv)
    q_decay = jnp.exp(log_gamma[:, None] * (idx + 1.0)[None, :])
    o_cross = jnp.einsum('bhnid,nbhde->bhnie', qc * q_decay[None, :, None, :, None], prev)
    return (o_intra + o_cross).reshape(B, H, S, Dh)


def _retention(q, k, v, logit_fwd, logit_bwd):
    B, S, _ = q.shape

    def heads(t):
        return t.astype(jnp.float32).reshape(B, S, N_RET_HEADS, HEAD_DIM).transpose(0, 2, 1, 3)

    qh = _rotary(heads(q)) * (HEAD_DIM ** -0.5)
    kh = _rotary(heads(k))
    vh = heads(v)
    lf = jax.nn.log_sigmoid(logit_fwd.astype(jnp.float32))
    lb = jax.nn.log_sigmoid(logit_bwd.astype(jnp.float32))
    fwd = _retention_direction(qh, kh, vh, lf, True)
    bwd = _retention_direction(qh[:, :, ::-1], kh[:, :, ::-1], vh[:, :, ::-1], lb, False)[:, :, ::-1]
    o = fwd + bwd
    o = o * lax.rsqrt(jnp.mean(o * o, axis=-1, keepdims=True) + EPS)
    return o.transpose(0, 2, 1, 3).reshape(B, S, D_GROUP).astype(q.dtype)


def _conformer_conv(a, b, conv_w, conv_b, ln_g, ln_b, w_pw):
    u = a * jax.nn.sigmoid(b)
    y = lax.conv_general_dilated(
        u, conv_w[:, None, :].astype(u.dtype), window_strides=(1,),
        padding=[(CONV_WIDTH // 2, CONV_WIDTH // 2)],
        dimension_numbers=('NWC', 'WIO', 'NWC'), feature_group_count=D_GROUP) + conv_b
    yf = y.astype(jnp.float32)
    mu = jnp.mean(yf, axis=-1, keepdims=True)
    var = jnp.mean((yf - mu) ** 2, axis=-1, keepdims=True)
    y = ((yf - mu) * lax.rsqrt(var + EPS)).astype(u.dtype) * ln_g + ln_b
    return jax.nn.silu(y) @ w_pw


def setup_inputs(seed: int = 0) -> dict:
    key = jax.random.key(seed)
    ks = jax.random.split(key, 17)
    f32 = jnp.float32
    nrm = lambda k, shape, s: jax.random.normal(k, shape, f32) * s
    gam = 1.0 - 2.0 ** (-5.0 - jnp.arange(N_RET_HEADS, dtype=f32))
    ret_logit0 = jnp.log(gam) - jnp.log1p(-gam)
    return {
        "x": nrm(ks[0], (BATCH, SEQ, D_MODEL), 1.0),
        "c": nrm(ks[1], (BATCH, D_MODEL), 1.0),
        "norm_g": 1.0 + nrm(ks[2], (DEPTH, D_MODEL), 0.01),
        "w_ada": nrm(ks[3], (DEPTH, D_MODEL, 3 * D_MODEL), 0.5 * D_MODEL ** -0.5),
        "b_ada": nrm(ks[4], (DEPTH, 3 * D_MODEL), 0.01),
        "w_in": nrm(ks[5], (DEPTH, D_MODEL, D_IN), D_MODEL ** -0.5),
        "w_fft": nrm(ks[6], (DEPTH, D_GROUP, D_GROUP), D_GROUP ** -0.5),
        "na_rel_bias": nrm(ks[7], (DEPTH, N_NA_HEADS, 2 * NA_KH_MAX - 1, 2 * NA_KW - 1), 0.02),
        "ret_logit_fwd": ret_logit0[None] + nrm(ks[8], (DEPTH, N_RET_HEADS), 0.01),
        "ret_logit_bwd": ret_logit0[None] + nrm(ks[9], (DEPTH, N_RET_HEADS), 0.01),
        "conv_w": nrm(ks[10], (DEPTH, CONV_WIDTH, D_GROUP), CONV_WIDTH ** -0.5),
        "conv_b": nrm(ks[11], (DEPTH, D_GROUP), 0.01),
        "conv_ln_g": 1.0 + nrm(ks[12], (DEPTH, D_GROUP), 0.01),
        "conv_ln_b": nrm(ks[13], (DEPTH, D_GROUP), 0.01),
        "conv_w_pw": nrm(ks[14], (DEPTH, D_GROUP, D_GROUP), D_GROUP ** -0.5),
        "w_out": nrm(ks[15], (DEPTH, D_MIX, D_MODEL), D_MIX ** -0.5),
        "final_g": 1.0 + nrm(ks[16], (D_MODEL,), 0.01),
    }


def reference(x, c, norm_g, w_ada, b_ada, w_in, w_fft, na_rel_bias, ret_logit_fwd, ret_logit_bwd,
              conv_w, conv_b, conv_ln_g, conv_ln_b, conv_w_pw, w_out, final_g):
    c_act = jax.nn.silu(c)
    for l in range(DEPTH):
        mod = c_act @ w_ada[l] + b_ada[l]
        shift, scale, gate = jnp.split(mod, 3, axis=-1)
        h = _rmsnorm(x, norm_g[l]) * (1.0 + scale[:, None, :]) + shift[:, None, :]
        z = h @ w_in[l]
        (f_x, f_g, na_q, na_k, na_v, na_g, r_q, r_k, r_v, r_g,
         cv_a, cv_b, cv_g) = jnp.split(z, N_IN_PIECES, axis=-1)
        o_fft = _fourier_mix(f_x, w_fft[l]) * jax.nn.silu(f_g)
        o_na = _neighbourhood_attention(na_q, na_k, na_v, na_rel_bias[l]) * jax.nn.silu(na_g)
        o_ret = _retention(r_q, r_k, r_v, ret_logit_fwd[l], ret_logit_bwd[l]) * jax.nn.silu(r_g)
        o_cv = _conformer_conv(cv_a, cv_b, conv_w[l], conv_b[l], conv_ln_g[l], conv_ln_b[l],
                               conv_w_pw[l]) * jax.nn.silu(cv_g)
        y = jnp.concatenate([o_fft, o_na, o_ret, o_cv], axis=-1) @ w_out[l]
        x = x + gate[:, None, :] * y
    return _rmsnorm(x, final_g)
```

```python
import numpy as np
import ml_dtypes
from contextlib import ExitStack
import concourse.bass as bass
import concourse.mybir as mybir
from concourse.bass_utils import run_bass_kernel_spmd

F32 = mybir.dt.float32
BF16 = mybir.dt.bfloat16
AF = mybir.ActivationFunctionType
ALU = mybir.AluOpType
AX = mybir.AxisListType

D = 2048
SEQ = 4096
NB = 2
DEPTH = 2
DIN = 6656
NTOK = 1024
NCORE = 8
EPS = 1e-6
NBLK1 = 13
BLK = 128 * 1024
B_P, B_Q, B_FG, B_NQ, B_NK, B_NV, B_NG, B_RQ, B_RK, B_RV, B_RG, B_CA, B_CB = range(13)


PROFILE_SCOPES = False


class Buf:
    __slots__ = ("name", "w", "r")

    def __init__(self, name=""):
        self.name = name
        self.w = None
        self.r = {}


class Sched:
    COMPUTE = ("pe", "act", "dve", "pool")
    NDMA = 12

    def __init__(self, nc, es):
        self.nc = nc
        self.es = es
        self.eng = {"pe": nc.tensor, "act": nc.scalar, "dve": nc.vector, "pool": nc.gpsimd, "sp": nc.sync}
        self.streams = {e: [] for e in self.eng}
        self.sems = {}
        self.cnt = {}
        self.waited = {e: {} for e in self.eng}
        for e in self.COMPUTE:
            self.sems[e] = es.enter_context(nc.semaphore("s_" + e))
            self.cnt[e] = 0
        self.dq = {}
        for q in ("sp", "pool", "act"):
            lst = []
            for i in range(self.NDMA):
                k = "d_%s_%d" % (q, i)
                self.sems[k] = es.enter_context(nc.semaphore(k))
                self.cnt[k] = 0
                lst.append(k)
            self.dq[q] = [lst, 0]
        self.scope = None
        self.psum = [es.enter_context(nc.psum_tensor("psb%d" % i, [128, 512], F32)) for i in range(8)]
        self.pbuf = [Buf("ps%d" % i) for i in range(8)]

    def sb(self, es, name, shape, dt):
        return es.enter_context(self.nc.sbuf_tensor(name, list(shape), dt))

    def _deps(self, reads, writes):
        deps = {}

        def add(k, v):
            if deps.get(k, 0) < v:
                deps[k] = v
        for b in reads:
            if b.w is not None:
                add(*b.w)
        for b in writes:
            if b.w is not None:
                add(*b.w)
            for k, v in b.r.items():
                add(k, v)
        return deps

    def _emit_waits(self, eng, deps):
        for k, v in deps.items():
            if eng == "pe" and k == "pe":
                continue
            if self.waited[eng].get(k, 0) < v:
                self.waited[eng][k] = v
                self.streams[eng].append(("wait", k, v))

    def _mark(self, tok, reads, writes):
        k, v = tok
        for b in reads:
            if b.r.get(k, 0) < v:
                b.r[k] = v
        for b in writes:
            b.w = tok
            b.r = {}

    def op(self, eng, fn, reads=(), writes=()):
        self._emit_waits(eng, self._deps(reads, writes))
        self.cnt[eng] += 1
        tok = (eng, self.cnt[eng])
        self.streams[eng].append(("op", fn, eng, 1, self.scope))
        self._mark(tok, reads, writes)
        return tok

    def dma(self, q, out, in_, reads=(), writes=(), slow=False):
        lst, idx = self.dq[q]
        k = lst[idx % len(lst)]
        self.dq[q][1] = idx + 1
        deps = self._deps(reads, writes)
        if self.cnt[k] > 0:
            deps[k] = max(deps.get(k, 0), self.cnt[k])
        self._emit_waits(q, deps)
        self.cnt[k] += 16
        tok = (k, self.cnt[k])
        if slow:
            self.streams[q].append(("op", lambda e: e.dma_start(out=out, in_=in_, allow_slow_non_contiguous=True), k, 16, self.scope))
        else:
            self.streams[q].append(("op", lambda e: e.dma_start(out=out, in_=in_), k, 16, self.scope))
        self._mark(tok, reads, writes)
        return tok

    def coll(self, kind, ins, outs, groups, reads=(), writes=()):
        q = "pool"
        lst, idx = self.dq[q]
        k = lst[idx % len(lst)]
        self.dq[q][1] = idx + 1
        deps = self._deps(reads, writes)
        if self.cnt[k] > 0:
            deps[k] = max(deps.get(k, 0), self.cnt[k])
        self._emit_waits(q, deps)
        self.cnt[k] += 16
        tok = (k, self.cnt[k])
        self.streams[q].append(("op", lambda e: e.collective_compute(kind, ALU.bypass, replica_groups=groups, ins=ins, outs=outs), k, 16, self.scope))
        self._mark(tok, reads, writes)
        return tok

    def barrier(self):
        allv = {k: v for k, v in self.cnt.items() if v > 0}
        for e in self.eng:
            self._emit_waits(e, dict(allv))

    def run(self):
        nc = self.nc
        self.barrier()
        with nc.Block() as block:
            def make(ename):
                def body(e):
                    cur, ctx = None, None
                    for it in self.streams[ename]:
                        if it[0] == "wait":
                            e.wait_ge(self.sems[it[1]], it[2])
                        else:
                            _, fn, k, inc, sc = it
                            if PROFILE_SCOPES and sc != cur:
                                if ctx is not None:
                                    ctx.__exit__(None, None, None)
                                    ctx = None
                                if sc is not None:
                                    ctx = nc.named_scope(sc)
                                    ctx.__enter__()
                                cur = sc
                            fn(e).then_inc(self.sems[k], inc)
                    if ctx is not None:
                        ctx.__exit__(None, None, None)
                return body
            block.sync(make("sp"))
            block.tensor(make("pe"))
            block.scalar(make("act"))
            block.vector(make("dve"))
            block.gpsimd(make("pool"))

    def mm(self, out, lhsT, rhs, start, stop, r, w):
        return self.op("pe", lambda e: e.matmul(out, lhsT=lhsT, rhs=rhs, start=start, stop=stop), r, w)

    def tr(self, out, in_, ident, r, w):
        return self.op("pe", lambda e: e.transpose(out=out, in_=in_, identity=ident), r, w)

    def act(self, out, in_, func, r, w, bias=None, scale=None, accum=None):
        kw = {}
        if bias is not None:
            kw["bias"] = bias
        if scale is not None:
            kw["scale"] = scale
        if accum is not None:
            kw["accum_out"] = accum
        return self.op("act", lambda e: e.activation(out=out, in_=in_, func=func, **kw), r, w)

    def tt(self, eng, out, in0, in1, op, r, w):
        return self.op(eng, lambda e: e.tensor_tensor(out=out, in0=in0, in1=in1, op=op), r, w)

    def ts(self, eng, out, in0, s1, op0, r, w, s2=None, op1=None):
        if op1 is None:
            return self.op(eng, lambda e: e.tensor_scalar(out=out, in0=in0, scalar1=s1, scalar2=None, op0=op0), r, w)
        return self.op(eng, lambda e: e.tensor_scalar(out=out, in0=in0, scalar1=s1, scalar2=s2, op0=op0, op1=op1), r, w)

    def stt(self, out, in0, scalar, in1, op0, op1, r, w):
        return self.op("dve", lambda e: e.scalar_tensor_tensor(out=out, in0=in0, scalar=scalar, in1=in1, op0=op0, op1=op1), r, w)

    def copy(self, eng, out, in_, r, w):
        if eng == "act":
            return self.op("act", lambda e: e.copy(out=out, in_=in_), r, w)
        return self.op(eng, lambda e: e.tensor_copy(out=out, in_=in_), r, w)

    def red(self, out, in_, op, r, w, negate=False):
        return self.op("dve", lambda e: e.tensor_reduce(out=out, in_=in_, axis=AX.X, op=op, negate=negate), r, w)

    def memset(self, eng, ap, val, w):
        return self.op(eng, lambda e: e.memset(ap, val), (), w)


def bc(ap, pos, count):
    lst = [list(x) for x in ap.ap]
    lst.insert(pos, [0, count])
    return bass.AP(ap.tensor, ap.offset, lst)


def dram_in(nc, name, shape, dt=F32):
    return nc.dram_tensor(name, list(shape), dt, kind="ExternalInput").ap()


def dram_out(nc, name, shape, dt=F32):
    return nc.dram_tensor(name, list(shape), dt, kind="ExternalOutput").ap()


def dram_tmp(nc, name, shape, dt=F32):
    return nc.dram_tensor(name, list(shape), dt, kind="Internal").ap()


_CONST = None


def consts():
    global _CONST
    if _CONST is not None:
        return _CONST
    c = {}
    c["ident"] = np.eye(128, dtype=np.float32)
    i128 = np.arange(128, dtype=np.float64)
    ang = 2 * np.pi * np.outer(i128, i128) / 128.0
    ccsc = np.stack([np.cos(ang), np.sin(ang)], axis=1) / np.sqrt(128.0)
    c["ccsc"] = ccsc.astype(np.float32)
    s1 = np.arange(64)[:, None, None]
    s2 = np.arange(64)[None, :, None]
    k1 = np.arange(64)[None, None, :]
    th = 2 * np.pi * (k1 * s1 / 64.0 + k1 * s2 / 4096.0)
    m1 = np.zeros((128, 64, 128), np.float64)
    m1[0:64, :, 0:64] = np.cos(th)
    m1[0:64, :, 64:128] = np.sin(th)
    m1[64:128, :, 0:64] = -np.sin(th)
    m1[64:128, :, 64:128] = np.cos(th)
    c["m1"] = m1.astype(ml_dtypes.bfloat16)
    ph = 2 * np.pi * np.outer(np.arange(64), np.arange(64)) / 64.0
    m3 = np.concatenate([np.cos(ph), -np.sin(ph)], axis=0) / 64.0
    c["m3"] = m3.astype(ml_dtypes.bfloat16)
    col = np.arange(64)
    cs = np.clip(col - 8, 0, 48)
    rel = col[None, :] - cs[:, None]
    m = np.where((rel >= 0) & (rel < 16), 0.0, -30000.0).astype(np.float32)
    c["namask"] = np.concatenate([m, m], axis=0)
    half = 32
    inv = 10000.0 ** (-np.arange(half, dtype=np.float32) / half)
    angr = np.arange(SEQ, dtype=np.float32)[:, None] * inv[None, :]
    cosr = np.cos(angr.astype(np.float64)).T
    sinr = np.sin(angr.astype(np.float64)).T
    cos64 = np.concatenate([cosr, cosr], axis=0)
    sin64 = np.concatenate([-sinr, sinr], axis=0)
    c["rcos"] = np.concatenate([cos64, cos64], axis=0).astype(np.float32)
    c["rsin"] = np.concatenate([sin64, sin64], axis=0).astype(np.float32)
    pm = np.zeros((128, 128), np.float32)
    for mm_ in range(128):
        pm[mm_ ^ 32, mm_] = 1.0
    c["perm"] = pm.astype(ml_dtypes.bfloat16)
    j = np.arange(128)[:, None]
    i = np.arange(128)[None, :]
    BIG = 1.0e7
    distf = np.where(i >= j, (i - j), BIG).astype(np.float32)
    distb = np.where(j > i, (j - i), BIG).astype(np.float32)
    io1 = np.broadcast_to(np.arange(1, 129, dtype=np.float32)[None, :], (128, 128))
    io2 = np.broadcast_to((128 - np.arange(128, dtype=np.float32))[None, :], (128, 128))
    c["rtab"] = np.ascontiguousarray(np.stack([distf, distb, io1, io2], axis=1)).astype(np.float32)
    ce = np.stack([127 - np.arange(128), np.arange(128)], axis=1).astype(np.float32)
    c["cexp"] = ce
    _CONST = c
    return c


CONST_SHAPES = {"ident": [128, 128], "ccsc": [128, 2, 128], "m1": [128, 64, 128], "m3": [128, 64],
                "namask": [128, 64], "rcos": [128, SEQ], "rsin": [128, SEQ], "perm": [128, 128],
                "rtab": [128, 4, 128], "cexp": [128, 2]}


def emit_mod(S, io):
    nc = S.nc
    with ExitStack() as es:
        cT = S.sb(es, "m_cT", [128, 16, 2], F32)
        bcT = Buf()
        ba = S.sb(es, "m_ba", [2, 1536], F32)
        bba = Buf()
        res = S.sb(es, "m_res", [2, 1536], F32)
        bres = Buf()
        wt = [S.sb(es, "m_w%d" % i, [128, 16, 512], F32) for i in range(3)]
        bw = [Buf() for _ in range(3)]
        S.dma("sp", cT[:], io["cT"], (), [bcT])
        S.dma("sp", ba[:], io["bada"], (), [bba])
        for n in range(3):
            S.dma("sp", wt[n][:], io["wada"][:, n * 512:(n + 1) * 512].rearrange("(k p) c -> p k c", p=128), (), [bw[n]])
        S.act(cT[:], cT[:], AF.Silu, [bcT], [bcT])
        cR = S.sb(es, "m_cR", [128, 16, 64, 2], F32)
        bcR = Buf()
        S.copy("dve", cR[:], bc(cT[:], 2, 64), [bcT], [bcR])
        for n in range(3):
            ps = S.psum[n]
            pb = S.pbuf[n]
            for k in range(16):
                S.mm(ps[:, :], cR[:, k, :, :].rearrange("p r b -> p (r b)"), wt[n][:, k, :], k == 0, k == 15, [bcR, bw[n]], [pb])
            S.tt("dve", res[:, n * 512:(n + 1) * 512], ps[0:2, :], ba[:, n * 512:(n + 1) * 512], ALU.add, [pb, bba], [bres])
        S.dma("sp", io["mod"], res[:], [bres], [Buf()])
    S.barrier()


def emit_A(S, io, layer_tag=""):
    nc = S.nc
    with ExitStack() as es:
        ident = S.sb(es, "a_ident", [128, 128], F32)
        b_ident = Buf()
        S.dma("sp", ident[:], io["ident"], (), [b_ident])
        gT = S.sb(es, "a_gT", [128, 16], F32)
        scT = S.sb(es, "a_scT", [128, 16], F32)
        shT = S.sb(es, "a_shT", [128, 16], F32)
        gsT = S.sb(es, "a_gsT", [128, 16], F32)
        b_mod = Buf()
        b_gs = Buf()
        S.dma("sp", gT[:], io["norm_g"].rearrange("(c p) -> p c", p=128), (), [b_mod], slow=True)
        S.dma("sp", scT[:], io["scale"].rearrange("(c p) -> p c", p=128), (), [b_mod], slow=True)
        S.dma("sp", shT[:], io["shift"].rearrange("(c p) -> p c", p=128), (), [b_mod], slow=True)
        S.stt(gsT[:], scT[:], 1.0, gT[:], ALU.add, ALU.mult, [b_mod], [b_gs])

        ccsc = S.sb(es, "a_ccsc", [128, 2, 128], F32)
        b_cc = Buf()
        S.dma("sp", ccsc[:], io["ccsc"], (), [b_cc])
        wf = S.sb(es, "a_wf", [128, 4, 512], F32)
        b_wf = Buf()
        S.dma("sp", wf[:], io["w_fft"].rearrange("(g p) c -> p g c", p=128), (), [b_wf])
        mcs = S.sb(es, "a_mcs", [128, 2, 4, 512], BF16)
        b_mcs = Buf()
        for cs in range(2):
            for g in range(4):
                pi = (cs * 4 + g) % 4
                ps, pb = S.psum[pi], S.pbuf[pi]
                S.mm(ps[:, :], ccsc[:, cs, :], wf[:, g, :], True, True, [b_cc, b_wf], [pb])
                S.copy("act" if g % 2 else "dve", mcs[:, cs, g, :], ps[:, :], [pb], [b_mcs])

        hT = S.sb(es, "a_hT", [128, 16, NTOK], BF16)
        b_hT = [Buf() for _ in range(2)]
        xt = [S.sb(es, "a_x%d" % i, [128, D], F32) for i in range(4)]
        b_xt = [Buf() for _ in range(4)]
        junk = S.sb(es, "a_junk", [128, D], F32)
        b_junk = Buf()
        ss = S.sb(es, "a_ss", [128, 8], F32)
        rs = S.sb(es, "a_rs", [128, 8], F32)
        b_ss = [Buf() for _ in range(8)]
        for grp in range(2):
            for tt_ in range(4):
                t = grp * 4 + tt_
                S.dma("sp", xt[tt_][:], io["x"][t * 128:(t + 1) * 128, :], (), [b_xt[tt_]])
                S.act(junk[:], xt[tt_][:], AF.Square, [b_xt[tt_]], [b_junk, b_ss[t]], accum=ss[:, t:t + 1])
                S.ts("dve", rs[:, t:t + 1], ss[:, t:t + 1], 1.0 / D, ALU.mult, [b_ss[t]], [b_ss[t]], s2=EPS, op1=ALU.add)
                S.act(rs[:, t:t + 1], rs[:, t:t + 1], AF.Sqrt, [b_ss[t]], [b_ss[t]])
                S.op("dve", lambda e, o=rs[:, t:t + 1]: e.reciprocal(out=o, in_=o), [b_ss[t]], [b_ss[t]])
                S.ts("dve", xt[tt_][:], xt[tt_][:], rs[:, t:t + 1], ALU.mult, [b_xt[tt_], b_ss[t]], [b_xt[tt_]])
            for k in range(16):
                pi = 4 + (k % 4)
                ps, pb = S.psum[pi], S.pbuf[pi]
                for tt_ in range(4):
                    S.tr(ps[:, tt_ * 128:(tt_ + 1) * 128], xt[tt_][:, k * 128:(k + 1) * 128], ident[:], [b_xt[tt_], b_ident], [pb])
                S.ts("dve", hT[:, k, grp * 512:(grp + 1) * 512], ps[:, :], gsT[:, k:k + 1], ALU.mult,
                     [pb, b_gs, b_mod], [b_hT[grp]], s2=shT[:, k:k + 1], op1=ALU.add)

        wb = [S.sb(es, "a_w%d" % i, [128, 16, 512], BF16) for i in range(3)]
        b_wb = [Buf() for _ in range(3)]
        st_fm = [S.sb(es, "a_sfm%d" % i, [128, NTOK], BF16) for i in range(2)]
        b_sfm = [Buf() for _ in range(2)]
        st_tm = [S.sb(es, "a_stm%d" % i, [128, 8, 512], BF16) for i in range(2)]
        b_stm = [Buf() for _ in range(2)]
        fxT = S.sb(es, "a_fxT", [128, 4, NTOK], BF16)
        b_fx = Buf()
        send1 = io["send1"]

        def wload(p):
            slot = p % 3
            S.dma("pool", wb[slot][:], io["w_in"][:, p * 512:(p + 1) * 512].rearrange("(k p) c -> p k c", p=128),
                  (), [b_wb[slot]])

        cnt = {"fm": 0, "tm": 0, "ev": 0, "ps": 0}

        def evac(out, in_, r, w, scale=None):
            cnt["ev"] += 1
            if cnt["ev"] % 2:
                if scale is None:
                    S.copy("act", out, in_, r, w)
                else:
                    S.act(out, in_, AF.Copy, r, w, scale=scale)
            else:
                if scale is None:
                    S.copy("dve", out, in_, r, w)
                else:
                    S.ts("dve", out, in_, scale, ALU.mult, r, w)

        def nextps():
            cnt["ps"] += 1
            i = cnt["ps"] % 4
            return S.psum[i], S.pbuf[i]

        def fm_piece(p, dst_fn, scale=None):
            slot = p % 3
            for j in range(4):
                dram_ap, sb_ap = dst_fn(j)
                if sb_ap is None:
                    si = cnt["fm"] % 2
                    cnt["fm"] += 1
                    stage, bst = st_fm[si], b_sfm[si]
                    tgt = stage
                else:
                    tgt, bst = sb_ap, b_fx
                for h in range(2):
                    ps, pb = nextps()
                    for k in range(16):
                        S.mm(ps[:, :], wb[slot][:, k, j * 128:(j + 1) * 128], hT[:, k, h * 512:(h + 1) * 512],
                             k == 0, k == 15, [b_wb[slot], b_hT[h]], [pb])
                    if sb_ap is None:
                        evac(tgt[:, h * 512:(h + 1) * 512], ps[:, :], [pb], [bst], scale)
                    else:
                        evac(tgt[:, j, h * 512:(h + 1) * 512], ps[:, :], [pb], [bst], scale)
                if dram_ap is not None:
                    S.dma("sp", dram_ap, tgt[:], [bst], [Buf()])

        def tm_from(lhs_fn, nk, rhs_fn, rbufs, blk):
            si = cnt["tm"] % 2
            cnt["tm"] += 1
            stage, bst = st_tm[si], b_stm[si]
            for t in range(8):
                ps, pb = nextps()
                for k in range(nk):
                    S.mm(ps[:, :], lhs_fn(k, t), rhs_fn(k), k == 0, k == nk - 1, rbufs(t), [pb])
                evac(stage[:, t, :], ps[:, :], [pb], [bst])
            for j in range(4):
                dst = send1[j, blk, :].rearrange("(t p c) -> p t c", p=128, c=128)
                S.dma("sp", dst, stage[:, :, j * 128:(j + 1) * 128], [bst], [Buf()])

        def send_fm(blk):
            return lambda j: (send1[j, blk, :].rearrange("(p t) -> p t", p=128), None)

        wload(0)
        wload(1)
        for p in range(13):
            if p + 2 < 13:
                wload(p + 2)
            slot = p % 3
            if p == 0:
                fm_piece(0, lambda j: (None, fxT))
                for cs, blk in ((0, B_P), (1, B_Q)):
                    tm_from(lambda k, t: fxT[:, k, t * 128:(t + 1) * 128], 4,
                            lambda k, cs=cs: mcs[:, cs, k, :], lambda t: [b_fx, b_mcs], blk)
            elif p == 1:
                fm_piece(1, send_fm(B_FG))
            elif p == 2:
                fm_piece(2, send_fm(B_NQ), scale=0.125)
            elif p == 3:
                fm_piece(3, send_fm(B_NK))
            elif p in (4, 8, 9):
                blk = {4: B_NV, 8: B_RV, 9: B_RG}[p]
                tm_from(lambda k, t: hT[:, k, t * 128:(t + 1) * 128], 16,
                        lambda k, slot=slot: wb[slot][:, k, :], lambda t, slot=slot: [b_hT[t // 4], b_wb[slot]], blk)
            elif p == 5:
                fm_piece(5, send_fm(B_NG))
            elif p == 6:
                fm_piece(6, send_fm(B_RQ), scale=0.125)
            elif p == 7:
                fm_piece(7, send_fm(B_RK))
            elif p == 10:
                fm_piece(10, send_fm(B_CA))
            elif p == 11:
                fm_piece(11, send_fm(B_CB))
            elif p == 12:
                fm_piece(12, lambda j: (io["cvg"][j, :, :], None))
    S.barrier()


def _row_start(r):
    return min(max(r - 4, 0), 56)


def emit_B(S, io):
    nc = S.nc
    recv1 = io["recv1"]
    send2 = io["send2"]

    def load_fm(es, name, blk):
        t = S.sb(es, name, [128, SEQ], BF16)
        b = Buf()
        for i in range(4):
            S.dma(("sp", "act")[i % 2], t[:, i * 1024:(i + 1) * 1024], recv1[i, blk, :].rearrange("(p t) -> p t", p=128), (), [b])
        return t, b

    def load_tm(es, name, blk):
        t = S.sb(es, name, [128, 32, 128], BF16)
        b = Buf()
        for i in range(4):
            S.dma(("sp", "act")[i % 2], t[:, i * 8:(i + 1) * 8, :], recv1[i, blk, :].rearrange("(t p c) -> p t c", p=128, c=128), (), [b])
        return t, b

    def store_o(t, b, blk):
        for i in range(4):
            S.dma("sp", send2[i, blk, :, :], t[:, i * 1024:(i + 1) * 1024], [b], [Buf()])

    with ExitStack() as es0:
        identf = S.sb(es0, "b_identf", [128, 128], F32)
        identb = S.sb(es0, "b_identb", [128, 128], BF16)
        b_id = Buf()
        S.dma("sp", identf[:], io["ident"], (), [b_id])
        S.copy("dve", identb[:], identf[:], [b_id], [b_id])

        S.scope = "fft"
        with ExitStack() as es:
            pq = S.sb(es, "f_pq", [128, 64, 128], BF16)
            b_pq = Buf()
            for i in range(4):
                for c_, blk in ((0, B_P), (1, B_Q)):
                    S.dma("sp", pq[c_ * 64 + i * 16:c_ * 64 + (i + 1) * 16, :, :],
                          recv1[i, blk, :].rearrange("(a s c) -> a s c", a=16, c=128), (), [b_pq])
            m1 = S.sb(es, "f_m1", [128, 64, 128], BF16)
            b_m1 = Buf()
            for h in range(4):
                S.dma("act", m1[:, h * 16:(h + 1) * 16, :], io["m1"][:, h * 16:(h + 1) * 16, :], (), [b_m1])
            m3 = S.sb(es, "f_m3", [128, 64], BF16)
            b_m3 = Buf()
            S.dma("act", m3[:], io["m3"], (), [b_m3])
            fg, b_fg = load_fm(es, "f_fg", B_FG)
            gf = S.sb(es, "f_gate", [128, SEQ], BF16)
            b_gf = Buf()
            S.act(gf[:], fg[:], AF.Silu, [b_fg], [b_gf])
            S.scope = "conv"
            ca, b_ca = load_fm(es, "c_a", B_CA)
            cbt, b_cb = load_fm(es, "c_b", B_CB)
            S.act(cbt[:], cbt[:], AF.Sigmoid, [b_cb], [b_cb])
            up = S.sb(es, "c_up", [128, SEQ + 32], BF16)
            b_up = Buf()
            S.memset("pool", up[:, 0:15], 0.0, [b_up])
            S.memset("pool", up[:, 15 + SEQ:SEQ + 32], 0.0, [b_up])
            S.tt("dve", up[:, 15:15 + SEQ], ca[:], cbt[:], ALU.mult, [b_ca, b_cb], [b_up])
            cw = S.sb(es, "c_w", [128, 31], F32)
            cbias = S.sb(es, "c_bias", [128, 1], F32)
            b_cw = Buf()
            S.dma("sp", cw[:], io["cw"], (), [b_cw])
            S.dma("sp", cbias[:], io["cb"], (), [b_cw])
            dg = S.sb(es, "c_diag", [128, 31, 128], BF16)
            b_dg = Buf()
            for k in range(31):
                S.ts("pool", dg[:, k, :], identf[:], cw[:, k:k + 1], ALU.mult, [b_id, b_cw], [b_dg], s2=1.0, op1=ALU.mult)
            S.scope = "fft"
            tsb = S.sb(es, "f_t", [128, 64, 128], BF16)
            b_t = Buf()
            for g in range(16):
                ps, pb = S.psum[g % 2], S.pbuf[g % 2]
                for q in range(4):
                    s2 = g * 4 + q
                    S.mm(ps[:, q * 128:(q + 1) * 128], m1[:, s2, :], pq[:, s2, :], True, True, [b_m1, b_pq], [pb])
                S.copy("act" if g % 2 else "dve", tsb[:, g * 4:(g + 1) * 4, :],
                       ps[:, :].rearrange("p (a c) -> p a c", a=4), [pb], [b_t])
            scr = io["fftscr"]
            b_scr = Buf()
            S.dma("sp", scr.rearrange("(p s c) -> p s c", p=128, c=128), tsb[:], [b_t], [b_scr])
            t2 = S.sb(es, "f_t2", [128, 64, 128], BF16)
            b_t2 = Buf()
            src = scr.rearrange("(c k s h) -> c s k h", c=2, k=64, s=64)
            for c_ in range(2):
                S.dma("sp", t2[c_ * 64:(c_ + 1) * 64, :, :], src[c_], [b_scr], [b_t2])
            S.scope = "conv"
            yc = S.sb(es, "c_y", [128, SEQ], BF16)
            b_yc = Buf()
            for t in range(8):
                ps, pb = S.psum[t % 2], S.pbuf[t % 2]
                for k in range(31):
                    S.mm(ps[:, :], dg[:, k, :], up[:, t * 512 + k:t * 512 + k + 512], k == 0, k == 30, [b_dg, b_up], [pb])
                S.ts("dve", yc[:, t * 512:(t + 1) * 512], ps[:, :], cbias[:, 0:1], ALU.add, [pb, b_cw], [b_yc])
            store_o(yc, b_yc, 3)
            S.scope = "fft"
            of = S.sb(es, "f_o", [128, SEQ], BF16)
            b_of = Buf()
            ofv = of[:].rearrange("p (k2 k1) -> p k1 k2", k1=64)
            gfv = gf[:].rearrange("p (k2 k1) -> p k1 k2", k1=64)
            for g in range(8):
                ps, pb = S.psum[2 + g % 2], S.pbuf[2 + g % 2]
                for q in range(8):
                    k1 = g * 8 + q
                    S.mm(ps[:, q * 64:(q + 1) * 64], t2[:, k1, :], m3[:], True, True, [b_t2, b_m3], [pb])
                S.tt("dve", ofv[:, g * 8:(g + 1) * 8, :], ps[:, :].rearrange("p (a k) -> p a k", a=8),
                     gfv[:, g * 8:(g + 1) * 8, :], ALU.mult, [pb, b_gf], [b_of])
            store_o(of, b_of, 0)
        S.barrier()

        S.scope = "na"
        with ExitStack() as es:
            qT = S.sb(es, "n_q", [64, 2, SEQ], BF16)
            kT = S.sb(es, "n_k", [64, 2, SEQ], BF16)
            b_q, b_k = Buf(), Buf()
            for i in range(4):
                for hh in range(2):
                    S.dma("act", qT[:, hh, i * 1024:(i + 1) * 1024],
                          recv1[i, B_NQ, hh * 65536:(hh + 1) * 65536].rearrange("(p t) -> p t", p=64), (), [b_q])
                    S.dma("sp", kT[:, hh, i * 1024:(i + 1) * 1024],
                          recv1[i, B_NK, hh * 65536:(hh + 1) * 65536].rearrange("(p t) -> p t", p=64), (), [b_k])
            ng, b_ng = load_fm(es, "n_g", B_NG)
            ve, b_ve = load_tm(es, "n_ve", B_NV)
            vo = S.sb(es, "n_vo", [128, 32, 128], BF16)
            b_vo = Buf()
            for i in range(4):
                flat = recv1[i, B_NV, :]
                S.dma("sp", vo[:, i * 8:i * 8 + 7, :],
                      flat[64 * 128:(64 + 7 * 128) * 128].rearrange("(t p c) -> p t c", p=128, c=128), (), [b_vo])
                if i < 3:
                    S.dma("sp", vo[0:64, i * 8 + 7, :], flat[960 * 128:1024 * 128].rearrange("(p c) -> p c", c=128), (), [b_vo])
                    S.dma("sp", vo[64:128, i * 8 + 7, :], recv1[i + 1, B_NV, 0:64 * 128].rearrange("(p c) -> p c", c=128), (), [b_vo])
            gn = S.sb(es, "n_gate", [128, SEQ], BF16)
            b_gn = Buf()
            S.act(gn[:], ng[:], AF.Silu, [b_ng], [b_gn])
            nab = S.sb(es, "n_bias", [128, 15, 64], F32)
            msk = S.sb(es, "n_mask", [128, 64], F32)
            b_nab = Buf()
            S.dma("sp", nab[:], io["nab"], (), [b_nab])
            S.dma("sp", msk[:], io["namask"], (), [b_nab])
            S.tt("dve", nab[:], nab[:], bc(msk[:], 1, 15), ALU.add, [b_nab], [b_nab])
            on = S.sb(es, "n_o", [128, SEQ], BF16)
            b_on = Buf()
            sc = [S.sb(es, "n_sc%d" % i, [128, 512], F32) for i in range(3)]
            b_sc = [Buf() for _ in range(3)]
            pe_ = [S.sb(es, "n_p%d" % i, [128, 512], BF16) for i in range(3)]
            b_pe = [Buf() for _ in range(3)]
            pn = [S.sb(es, "n_pn%d" % i, [128, 512], BF16) for i in range(3)]
            b_pn = [Buf() for _ in range(3)]
            pt = [S.sb(es, "n_pt%d" % i, [128, 4, 128], BF16) for i in range(3)]
            b_pt = [Buf() for _ in range(3)]
            st = S.sb(es, "n_st", [128, 64, 4], F32)
            b_st = [Buf() for _ in range(64)]
            def na_stage_qk(r):
                d2 = r % 3
                ks = _row_start(r) * 64
                sps, sb_ = S.psum[(0, 1, 6)[d2]], S.pbuf[(0, 1, 6)[d2]]
                for hh in range(2):
                    lo, hi = hh * 64, (hh + 1) * 64
                    S.mm(sps[lo:hi, :], qT[:, hh, r * 64:(r + 1) * 64], kT[:, hh, ks:ks + 512], True, True, [b_q, b_k], [sb_])

            def na_stage_a(r):
                d2 = r % 3
                rs_ = _row_start(r)
                ks = rs_ * 64
                j0 = rs_ - r + 7
                sps, sb_ = S.psum[(0, 1, 6)[d2]], S.pbuf[(0, 1, 6)[d2]]
                S.tt("dve", sc[d2][:], sps[:, :], nab[:, j0:j0 + 8, :].rearrange("p a k -> p (a k)"), ALU.add,
                     [sb_, b_nab], [b_sc[d2]])
                S.red(st[:, r, 0:1], sc[d2][:], ALU.max, [b_sc[d2]], [b_st[r]], negate=True)
                S.act(pe_[d2][:], sc[d2][:], AF.Exp, [b_sc[d2], b_st[r]], [b_pe[d2], b_st[r]], bias=st[:, r, 0:1], accum=st[:, r, 1:2])

            def na_stage_b(r):
                d2 = r % 3
                S.op("dve", lambda e, o=st[:, r, 2:3], i_=st[:, r, 1:2]: e.reciprocal(out=o, in_=i_), [b_st[r]], [b_st[r]])
                S.ts("pool", pn[d2][:], pe_[d2][:], st[:, r, 2:3], ALU.mult, [b_pe[d2], b_st[r]], [b_pn[d2]], s2=1.0, op1=ALU.mult)
                tps, tb = S.psum[(2, 3, 7)[d2]], S.pbuf[(2, 3, 7)[d2]]
                tpsb = tps.bitcast(BF16)
                for c_ in range(4):
                    S.tr(tpsb[:, c_ * 128:(c_ + 1) * 128], pn[d2][:, c_ * 128:(c_ + 1) * 128], identb[:], [b_pn[d2], b_id], [tb])
                S.copy("act", pt[d2][:], tpsb[:, 0:512].rearrange("p (a k) -> p a k", a=4), [tb], [b_pt[d2]])

            def na_stage_c(r):
                d2 = r % 3
                ks = _row_start(r) * 64
                ob = 4 + (r // 8) % 2
                ops_, opb = S.psum[ob], S.pbuf[ob]
                col = (r % 8) * 64
                for hh in range(2):
                    lo, hi = hh * 64, (hh + 1) * 64
                    for c_ in range(4):
                        tok0 = ks + 128 * c_
                        if tok0 % 128 == 0:
                            vt, bv, ti = ve, b_ve, tok0 // 128
                        else:
                            vt, bv, ti = vo, b_vo, (tok0 - 64) // 128
                        S.mm(ops_[lo:hi, col:col + 64], vt[:, ti, lo:hi], pt[d2][:, c_, lo:hi], c_ == 0, c_ == 3, [bv, b_pt[d2]], [opb])
                if r % 8 == 7:
                    r0 = (r // 8) * 8
                    S.tt("dve", on[:, r0 * 64:(r0 + 8) * 64], ops_[:, :], gn[:, r0 * 64:(r0 + 8) * 64], ALU.mult, [opb, b_gn], [b_on])

            na_stage_qk(0)
            na_stage_qk(1)
            for t in range(64 + 2):
                if t + 2 < 64:
                    na_stage_qk(t + 2)
                if t < 64:
                    na_stage_a(t)
                if 0 <= t - 1 < 64:
                    na_stage_b(t - 1)
                if 0 <= t - 2 < 64:
                    na_stage_c(t - 2)
            store_o(on, b_on, 1)
        S.barrier()

        S.scope = "ret"
        with ExitStack() as es:
            qp = S.sb(es, "r_qp", [128, SEQ], BF16)
            kp = S.sb(es, "r_kp", [128, SEQ], BF16)
            b_qp, b_kp = Buf(), Buf()
            with ExitStack() as es1:
                rin = S.sb(es1, "r_in", [128, SEQ], BF16)
                b_rin = Buf()
                rcos = S.sb(es1, "r_cos", [128, SEQ], F32)
                rsin = S.sb(es1, "r_sin", [128, SEQ], F32)
                b_tab = Buf()
                for h in range(4):
                    S.dma("sp", rcos[:, h * 1024:(h + 1) * 1024], io["rcos"][:, h * 1024:(h + 1) * 1024], (), [b_tab])
                    S.dma("act", rsin[:, h * 1024:(h + 1) * 1024], io["rsin"][:, h * 1024:(h + 1) * 1024], (), [b_tab])
                perm = S.sb(es1, "r_perm", [128, 128], BF16)
                b_perm = Buf()
                S.dma("act", perm[:], io["perm"], (), [b_perm])
                t1 = [S.sb(es1, "r_t1%d" % i, [128, 512], F32) for i in range(2)]
                t2_ = [S.sb(es1, "r_t2%d" % i, [128, 512], F32) for i in range(2)]
                b_t1 = [Buf() for _ in range(2)]
                b_t2 = [Buf() for _ in range(2)]
                for blk, dst, bdst in ((B_RQ, qp, b_qp), (B_RK, kp, b_kp)):
                    for i in range(4):
                        S.dma("sp", rin[:, i * 1024:(i + 1) * 1024], recv1[i, blk, :].rearrange("(p t) -> p t", p=128), (), [b_rin])
                    for c_ in range(8):
                        d2 = c_ % 2
                        sl = slice(c_ * 512, (c_ + 1) * 512)
                        ps, pb = S.psum[d2], S.pbuf[d2]
                        S.mm(ps[:, :], perm[:], rin[:, sl], True, True, [b_perm, b_rin], [pb])
                        S.tt("dve", t1[d2][:], ps[:, :], rsin[:, sl], ALU.mult, [pb, b_tab], [b_t1[d2]])
                        S.tt("pool", t2_[d2][:], rin[:, sl], rcos[:, sl], ALU.mult, [b_rin, b_tab], [b_t2[d2]])
                        S.tt("dve", dst[:, sl], t1[d2][:], t2_[d2][:], ALU.add, [b_t1[d2], b_t2[d2]], [bdst])
            S.barrier()
            lg = S.sb(es, "r_lg", [128, 2], F32)
            lg2 = S.sb(es, "r_lg2", [128, 4], F32)
            b_lg = Buf()
            S.dma("sp", lg[:], io["retlg"], (), [b_lg])
            S.dma("sp", lg2[:], io["retlg2"], (), [b_lg])
            for t_ in (lg, lg2):
                S.act(t_[:], t_[:], AF.Exp, [b_lg], [b_lg], scale=-1.0)
                S.act(t_[:], t_[:], AF.Ln, [b_lg], [b_lg], bias=1.0)
                S.ts("dve", t_[:], t_[:], -1.0, ALU.mult, [b_lg], [b_lg])
            rtab = S.sb(es, "r_tab", [128, 4, 128], F32)
            cexp = S.sb(es, "r_cexp", [128, 2], F32)
            b_rt = Buf()
            S.dma("sp", rtab[:], io["rtab"], (), [b_rt])
            S.dma("sp", cexp[:], io["cexp"], (), [b_rt])
            DT = S.sb(es, "r_DT", [128, 2, 128], F32)
            tmpD = S.sb(es, "r_tmpD", [128, 128], F32)
            b_DT, b_tmpD = Buf(), Buf()
            for h in range(2):
                S.act(DT[:, h, :], rtab[:, 0, :], AF.Exp, [b_rt, b_lg], [b_DT], scale=lg2[:, h:h + 1])
                S.act(tmpD[:], rtab[:, 1, :], AF.Exp, [b_rt, b_lg], [b_tmpD], scale=lg2[:, 2 + h:3 + h])
                S.tt("dve", DT[:, h, :], DT[:, h, :], tmpD[:], ALU.add, [b_DT, b_tmpD], [b_DT])
            qdec = S.sb(es, "r_qdec", [128, 2, 128], F32)
            kd = S.sb(es, "r_kd", [128, 4], F32)
            cdec = S.sb(es, "r_cdec", [128, 2], F32)
            b_dec = Buf()
            for dr in range(2):
                S.act(qdec[:, dr, :], rtab[:, 2 + dr, :], AF.Exp, [b_rt, b_lg], [b_dec], scale=lg[:, dr:dr + 1])
                for h in range(2):
                    c_ = dr * 2 + h
                    S.act(kd[:, c_:c_ + 1], cexp[:, dr:dr + 1], AF.Exp, [b_rt, b_lg], [b_dec], scale=lg2[:, c_:c_ + 1])
            S.act(cdec[:], lg[:], AF.Exp, [b_lg], [b_dec], scale=128.0)
            qf = S.sb(es, "r_qf", [128, SEQ], BF16)
            qb = S.sb(es, "r_qb", [128, SEQ], BF16)
            b_qf, b_qb = Buf(), Buf()
            qp3 = qp[:].rearrange("p (n i) -> p n i", i=128)
            S.tt("dve", qf[:].rearrange("p (n i) -> p n i", i=128), qp3, bc(qdec[:, 0, :], 1, 32), ALU.mult, [b_qp, b_dec], [b_qf])
            S.tt("pool", qb[:].rearrange("p (n i) -> p n i", i=128), qp3, bc(qdec[:, 1, :], 1, 32), ALU.mult, [b_qp, b_dec], [b_qb])
            ktok = S.sb(es, "r_ktok", [128, 32, 128], BF16)
            b_ktok = Buf()
            for g in range(8):
                ps, pb = S.psum[g % 2], S.pbuf[g % 2]
                psb_ = ps.bitcast(BF16)
                for q in range(4):
                    n = g * 4 + q
                    S.tr(psb_[:, q * 128:(q + 1) * 128], kp[:, n * 128:(n + 1) * 128], identb[:], [b_kp, b_id], [pb])
                S.copy("act" if g % 2 else "dve", ktok[:, g * 4:(g + 1) * 4, :], psb_[:, 0:512].rearrange("p (a k) -> p a k", a=4), [pb], [b_ktok])
            vr, b_vr = load_tm(es, "r_v", B_RV)
            rg, b_rg = load_tm(es, "r_g", B_RG)
            S.act(rg[:], rg[:], AF.Silu, [b_rg], [b_rg])
            vf = S.sb(es, "r_vf", [128, 32, 128], BF16)
            vb = S.sb(es, "r_vb", [128, 32, 128], BF16)
            b_vf, b_vb = Buf(), Buf()
            for h in range(2):
                sl = slice(h * 64, (h + 1) * 64)
                S.ts("dve", vf[:, :, sl], vr[:, :, sl], kd[:, h:h + 1], ALU.mult, [b_vr, b_dec], [b_vf])
                S.ts("pool", vb[:, :, sl], vr[:, :, sl], kd[:, 2 + h:3 + h], ALU.mult, [b_vr, b_dec], [b_vb], s2=1.0, op1=ALU.mult)
            sf = S.sb(es, "r_sf", [128, 32, 64], F32)
            sbk = S.sb(es, "r_sb", [128, 32, 64], F32)
            b_sf, b_sbk = Buf(), Buf()
            S.memset("pool", sf[:, 0, :], 0.0, [b_sf])
            S.memset("pool", sbk[:, 31, :], 0.0, [b_sbk])
            for dr in range(2):
                vt, bvt = (vf, b_vf) if dr == 0 else (vb, b_vb)
                st_, bst_ = (sf, b_sf) if dr == 0 else (sbk, b_sbk)
                order = list(range(0, 31)) if dr == 0 else list(range(31, 0, -1))
                for g0 in range(0, 31, 8):
                    grp = order[g0:g0 + 8]
                    bi = 2 + (g0 // 8) % 2
                    ps, pb = S.psum[bi], S.pbuf[bi]
                    for q, n in enumerate(grp):
                        for h in range(2):
                            sl = slice(h * 64, (h + 1) * 64)
                            S.mm(ps[sl, q * 64:(q + 1) * 64], ktok[:, n, sl], vt[:, n, sl], True, True, [b_ktok, bvt], [pb])
                    for q, n in enumerate(grp):
                        nxt = n + 1 if dr == 0 else n - 1
                        S.stt(st_[:, nxt, :], st_[:, n, :], cdec[:, dr:dr + 1], ps[:, q * 64:(q + 1) * 64], ALU.mult, ALU.add,
                              [bst_, pb, b_dec], [bst_])
            sfb = S.sb(es, "r_sfb", [128, 32, 64], BF16)
            sbb = S.sb(es, "r_sbb", [128, 32, 64], BF16)
            b_sfb, b_sbb = Buf(), Buf()
            S.copy("act", sfb[:], sf[:], [b_sf], [b_sfb])
            S.copy("act", sbb[:], sbk[:], [b_sbk], [b_sbb])
            orT = S.sb(es, "r_o", [128, SEQ], BF16)
            b_or = Buf()
            ad = [S.sb(es, "r_ad%d" % i, [128, 4, 128], BF16) for i in range(2)]
            b_ad = [Buf() for _ in range(2)]
            osb = [S.sb(es, "r_osb%d" % i, [128, 4, 2, 64], F32) for i in range(2)]
            osq = [S.sb(es, "r_osq%d" % i, [128, 4, 2, 64], F32) for i in range(2)]
            onb = [S.sb(es, "r_onb%d" % i, [128, 4, 128], BF16) for i in range(2)]
            b_osb = [Buf() for _ in range(2)]
            b_osq = [Buf() for _ in range(2)]
            b_onb = [Buf() for _ in range(2)]
            rst = S.sb(es, "r_rst", [128, 8, 8], F32)
            b_rst = [Buf() for _ in range(8)]
            def ret_stage1(g):
                d2 = g % 2
                for h in range(2):
                    sl = slice(h * 64, (h + 1) * 64)
                    aps, apb = S.psum[h], S.pbuf[h]
                    for cn in range(4):
                        n = g * 4 + cn
                        S.mm(aps[:, cn * 128:(cn + 1) * 128], kp[sl, n * 128:(n + 1) * 128],
                             qp[sl, n * 128:(n + 1) * 128], True, True, [b_kp, b_qp], [apb])
                    S.tt("dve", ad[h][:], aps[:, :].rearrange("p (c i) -> p c i", c=4), bc(DT[:, h, :], 1, 4), ALU.mult,
                         [apb, b_DT], [b_ad[h]])
                    ops_, opb = S.psum[2 + d2 * 2 + h], S.pbuf[2 + d2 * 2 + h]
                    for cn in range(4):
                        n = g * 4 + cn
                        oc = cn * 64
                        S.mm(ops_[:, oc:oc + 64], ad[h][:, cn, :], vr[:, n, sl], True, False, [b_ad[h], b_vr], [opb])
                        if n > 0:
                            S.mm(ops_[:, oc:oc + 64], qf[sl, n * 128:(n + 1) * 128], sfb[sl, n, :], False, n == 31,
                                 [b_qf, b_sfb], [opb])
                        if n < 31:
                            S.mm(ops_[:, oc:oc + 64], qb[sl, n * 128:(n + 1) * 128], sbb[sl, n, :], False, True,
                                 [b_qb, b_sbb], [opb])
                    S.copy("act", osb[d2][:, :, h, :], ops_[:, 0:256].rearrange("p (c e) -> p c e", e=64), [opb], [b_osb[d2]])

            def ret_stage2(g):
                d2 = g % 2
                S.tt("pool", osq[d2][:], osb[d2][:], osb[d2][:], ALU.mult, [b_osb[d2]], [b_osq[d2]])
                S.red(rst[:, g, :], osq[d2][:].rearrange("p c h e -> p (c h) e"), ALU.add, [b_osq[d2]], [b_rst[g]])
                S.ts("dve", rst[:, g, :], rst[:, g, :], 1.0 / 64.0, ALU.mult, [b_rst[g]], [b_rst[g]], s2=EPS, op1=ALU.add)
                S.act(rst[:, g, :], rst[:, g, :], AF.Sqrt, [b_rst[g]], [b_rst[g]])
                S.op("dve", lambda e, o=rst[:, g, :]: e.reciprocal(out=o, in_=o), [b_rst[g]], [b_rst[g]])
                S.tt("dve", osb[d2][:].rearrange("p c h e -> p (c h) e"), osb[d2][:].rearrange("p c h e -> p (c h) e"),
                     bc(rst[:, g, :], 2, 64), ALU.mult, [b_osb[d2], b_rst[g]], [b_osb[d2]])
                S.tt("pool", onb[d2][:], osb[d2][:].rearrange("p c h e -> p c (h e)"), rg[:, g * 4:(g + 1) * 4, :], ALU.mult,
                     [b_osb[d2], b_rg], [b_onb[d2]])
                tps, tb = S.psum[6 + d2], S.pbuf[6 + d2]
                tpsb = tps.bitcast(BF16)
                for cn in range(4):
                    S.tr(tpsb[:, cn * 128:(cn + 1) * 128], onb[d2][:, cn, :], identb[:], [b_onb[d2], b_id], [tb])
                S.copy("act", orT[:, g * 512:(g + 1) * 512], tpsb[:, 0:512], [tb], [b_or])

            ret_stage1(0)
            for g in range(8):
                if g + 1 < 8:
                    ret_stage1(g + 1)
                ret_stage2(g)
            store_o(orT, b_or, 2)
        S.barrier()

    S.barrier()


def emit_C(S, io, last):
    nc = S.nc
    recv2 = io["recv2"]
    with ExitStack() as es:
        oT = S.sb(es, "c_oT", [128, 16, NTOK], BF16)
        b_oT = [Buf() for _ in range(16)]
        wo = S.sb(es, "c_wo", [128, 16, D], BF16)
        b_wo = [Buf() for _ in range(4)]
        for n in range(4):
            S.dma("pool", wo[:, :, n * 512:(n + 1) * 512], io["w_out"][:, n * 512:(n + 1) * 512].rearrange("(k p) c -> p k c", p=128),
                  (), [b_wo[n]])
        gate = S.sb(es, "c_gate", [128, D], F32)
        b_gate = Buf()
        S.dma("sp", gate[:], bc(io["gate"], 0, 128), (), [b_gate])
        if last:
            fg = S.sb(es, "c_fg", [128, D], F32)
            b_fgn = Buf()
            S.dma("sp", fg[:], bc(io["final_g"], 0, 128), (), [b_fgn])
        es1 = es
        if True:
            yc = S.sb(es1, "c_yc", [128, 4, NTOK], BF16)
            b_yc = Buf()
            for j in range(4):
                S.dma("sp", yc[:, j, :], recv2[j, 3, :, :], (), [b_yc])
            cg = S.sb(es1, "c_cg", [128, 4, NTOK], BF16)
            b_cg = Buf()
            for j in range(4):
                S.dma("sp", cg[:, j, :], io["cvg"][j, :, :], (), [b_cg])
            for m in range(3):
                for j in range(4):
                    S.dma(("sp", "act")[j % 2], oT[:, m * 4 + j, :], recv2[j, m, :, :], (), [b_oT[m * 4 + j]])
            S.act(cg[:], cg[:], AF.Silu, [b_cg], [b_cg])
            wpw = S.sb(es1, "c_wpw", [128, 4, 512], BF16)
            b_wpw = Buf()
            S.dma("pool", wpw[:], io["w_pw"].rearrange("(k p) c -> p k c", p=128), (), [b_wpw])
            lngb = S.sb(es1, "c_lngb", [128, 2, 4], F32)
            b_ln = Buf()
            S.dma("sp", lngb[:, 0, :], io["lng"].rearrange("(c p) -> p c", p=128), (), [b_ln], slow=True)
            S.dma("sp", lngb[:, 1, :], io["lnb"].rearrange("(c p) -> p c", p=128), (), [b_ln], slow=True)
            ones = S.sb(es1, "c_ones", [128, 128], BF16)
            b_ones = Buf()
            S.memset("pool", ones[:], 1.0 / 512.0, [b_ones])
            ysq = S.sb(es1, "c_ysq", [128, 4, NTOK], BF16)
            b_ysq = Buf()
            S.act(ysq[:], yc[:], AF.Square, [b_yc], [b_ysq])
            sT = S.sb(es1, "c_sT", [128, 4, NTOK], BF16)
            b_sT = [Buf() for _ in range(2)]
            msq = S.sb(es1, "c_msq", [128, 512], F32)
            rstd = S.sb(es1, "c_rstd", [128, 512], F32)
            b_msq, b_rstd = Buf(), Buf()
            dtmp = [S.sb(es1, "c_d%d" % i, [128, 512], F32) for i in range(2)]
            b_dt = [Buf() for _ in range(2)]
            def ln_half(h):
                hs = slice(h * 512, (h + 1) * 512)
                pm, bpm = S.psum[0], S.pbuf[0]
                pq_, bpq = S.psum[1], S.pbuf[1]
                for j in range(4):
                    S.mm(pm[:, :], ones[:], yc[:, j, hs], j == 0, j == 3, [b_ones, b_yc], [bpm])
                for j in range(4):
                    S.mm(pq_[:, :], ones[:], ysq[:, j, hs], j == 0, j == 3, [b_ones, b_ysq], [bpq])
                S.act(msq[:], pm[:, :], AF.Square, [bpm], [b_msq])
                S.tt("dve", rstd[:], pq_[:, :], msq[:], ALU.subtract, [bpq, b_msq], [b_rstd])
                S.act(rstd[:], rstd[:], AF.Ln, [b_rstd], [b_rstd], bias=EPS)
                S.act(rstd[:], rstd[:], AF.Exp, [b_rstd], [b_rstd], scale=-0.5)
                for j in range(4):
                    d2 = j % 2
                    S.tt("dve", dtmp[d2][:], yc[:, j, hs], pm[:, :], ALU.subtract, [b_yc, bpm], [b_dt[d2]])
                    S.tt("pool", dtmp[d2][:], dtmp[d2][:], rstd[:], ALU.mult, [b_dt[d2], b_rstd], [b_dt[d2]])
                    S.ts("dve", dtmp[d2][:], dtmp[d2][:], lngb[:, 0, j:j + 1], ALU.mult, [b_dt[d2], b_ln], [b_dt[d2]],
                         s2=lngb[:, 1, j:j + 1], op1=ALU.add)
                    S.act(sT[:, j, hs], dtmp[d2][:], AF.Silu, [b_dt[d2]], [b_sT[h]])
            def pw_half(h):
                hs = slice(h * 512, (h + 1) * 512)
                for co in range(4):
                    ps, pb = S.psum[2 + co % 2], S.pbuf[2 + co % 2]
                    for ci in range(4):
                        S.mm(ps[:, :], wpw[:, ci, co * 128:(co + 1) * 128], sT[:, ci, hs], ci == 0, ci == 3, [b_wpw, b_sT[h]], [pb])
                    S.tt("dve", oT[:, 12 + co, hs], ps[:, :], cg[:, co, hs], ALU.mult, [pb, b_cg], [b_oT[12 + co]])
        xt = [S.sb(es, "c_x%d" % i, [128, D], F32) for i in range(2)]
        b_xt = [Buf() for _ in range(2)]
        tmp = [S.sb(es, "c_tmp%d" % i, [128, 512], F32) for i in range(2)]
        b_tmp = [Buf() for _ in range(2)]
        junk = S.sb(es, "c_junk", [128, D], F32)
        b_junk = Buf()
        ss = S.sb(es, "c_ss", [128, 8], F32)
        b_ss = [Buf() for _ in range(8)]
        outs = []
        def outproj_tile(t):
            d2 = t % 2
            S.dma("sp", xt[d2][:], io["x"][t * 128:(t + 1) * 128, :], (), [b_xt[d2]])
            for n in range(4):
                pi = 4 + (t * 4 + n) % 4
                ps, pb = S.psum[pi], S.pbuf[pi]
                for k in range(16):
                    S.mm(ps[:, :], oT[:, k, t * 128:(t + 1) * 128], wo[:, k, n * 512:(n + 1) * 512], k == 0, k == 15,
                         [b_oT[k], b_wo[n]], [pb])
                ti = (t * 4 + n) % 2
                S.tt("dve", tmp[ti][:], ps[:, :], gate[:, n * 512:(n + 1) * 512], ALU.mult, [pb, b_gate], [b_tmp[ti]])
                S.tt("pool", xt[d2][:, n * 512:(n + 1) * 512], xt[d2][:, n * 512:(n + 1) * 512], tmp[ti][:], ALU.add,
                     [b_xt[d2], b_tmp[ti]], [b_xt[d2]])
            if last:
                S.act(junk[:], xt[d2][:], AF.Square, [b_xt[d2]], [b_junk, b_ss[t]], accum=ss[:, t:t + 1])
                S.ts("dve", ss[:, t:t + 1], ss[:, t:t + 1], 1.0 / D, ALU.mult, [b_ss[t]], [b_ss[t]], s2=EPS, op1=ALU.add)
                S.act(ss[:, t:t + 1], ss[:, t:t + 1], AF.Sqrt, [b_ss[t]], [b_ss[t]])
                S.op("dve", lambda e, o=ss[:, t:t + 1]: e.reciprocal(out=o, in_=o), [b_ss[t]], [b_ss[t]])
                S.stt(xt[d2][:], xt[d2][:], ss[:, t:t + 1], fg[:], ALU.mult, ALU.mult, [b_xt[d2], b_ss[t], b_fgn], [b_xt[d2]])
            ob = Buf()
            outs.append(ob)
            S.dma("sp", io["xout"][t * 128:(t + 1) * 128, :], xt[d2][:], [b_xt[d2]], [ob])

        ln_half(0)
        pw_half(0)
        ln_half(1)
        for t in range(4):
            outproj_tile(t)
        pw_half(1)
        for t in range(4, 8):
            outproj_tile(t)
    S.barrier()


_PROGS = {}


CONST_BF16 = ("m1", "m3", "perm")


def _const_io(nc, io, names):
    for n in names:
        io[n] = dram_in(nc, n, CONST_SHAPES[n], BF16 if n in CONST_BF16 else F32)


def prog_mod():
    if "mod" in _PROGS:
        return _PROGS["mod"]
    nc = bass.Bass("TRN2", target_bir_lowering=False)
    io = {"cT": dram_in(nc, "cT", [128, 16, 2]), "wada": dram_in(nc, "wada", [D, 1536]),
          "bada": dram_in(nc, "bada", [2, 1536]), "mod": dram_out(nc, "mod", [2, 1536])}
    with ExitStack() as es:
        S = Sched(nc, es)
        emit_mod(S, io)
        S.run()
    _PROGS["mod"] = nc
    return nc


def prog_A():
    if "A" in _PROGS:
        return _PROGS["A"]
    nc = bass.Bass("TRN2", target_bir_lowering=False)
    io = {"x": dram_in(nc, "x", [NTOK, D]), "shift": dram_in(nc, "shift", [D]), "scale": dram_in(nc, "scale", [D]),
          "norm_g": dram_in(nc, "norm_g", [D]), "w_in": dram_in(nc, "w_in", [D, DIN]), "w_fft": dram_in(nc, "w_fft", [512, 512]),
          "send1": dram_out(nc, "send1", [4, NBLK1, BLK], BF16), "cvg": dram_out(nc, "cvg", [4, 128, NTOK], BF16)}
    _const_io(nc, io, ["ident", "ccsc"])
    with ExitStack() as es:
        S = Sched(nc, es)
        emit_A(S, io)
        S.run()
    _PROGS["A"] = nc
    return nc


B_CONSTS = ["ident", "m1", "m3", "namask", "rcos", "rsin", "perm", "rtab", "cexp"]


def prog_B():
    if "B" in _PROGS:
        return _PROGS["B"]
    nc = bass.Bass("TRN2", target_bir_lowering=False)
    io = {"recv1": dram_in(nc, "recv1", [4, NBLK1, BLK], BF16), "nab": dram_in(nc, "nab", [128, 15, 64]),
          "retlg": dram_in(nc, "retlg", [128, 2]), "retlg2": dram_in(nc, "retlg2", [128, 4]),
          "cw": dram_in(nc, "cw", [128, 31]), "cb": dram_in(nc, "cb", [128, 1]),
          "fftscr": dram_tmp(nc, "fftscr", [128 * 64 * 128], BF16),
          "send2": dram_out(nc, "send2", [4, 4, 128, NTOK], BF16)}
    _const_io(nc, io, B_CONSTS)
    with ExitStack() as es:
        S = Sched(nc, es)
        emit_B(S, io)
        S.run()
    _PROGS["B"] = nc
    return nc


def prog_C(last):
    key = "C%d" % int(last)
    if key in _PROGS:
        return _PROGS[key]
    nc = bass.Bass("TRN2", target_bir_lowering=False)
    io = {"recv2": dram_in(nc, "recv2", [4, 4, 128, NTOK], BF16), "cvg": dram_in(nc, "cvg", [4, 128, NTOK], BF16),
          "x": dram_in(nc, "x", [NTOK, D]), "gate": dram_in(nc, "gate", [D]), "w_out": dram_in(nc, "w_out", [D, D]),
          "w_pw": dram_in(nc, "w_pw", [512, 512]), "lng": dram_in(nc, "lng", [512]), "lnb": dram_in(nc, "lnb", [512]),
          "xout": dram_out(nc, "xout", [NTOK, D])}
    if last:
        io["final_g"] = dram_in(nc, "final_g", [D])
    with ExitStack() as es:
        S = Sched(nc, es)
        emit_C(S, io, last)
        S.run()
    _PROGS[key] = nc
    return nc


def prog_CA():
    if "CA" in _PROGS:
        return _PROGS["CA"]
    nc = bass.Bass("TRN2", target_bir_lowering=False)
    xmid = dram_tmp(nc, "xmid", [NTOK, D])
    ioc = {"recv2": dram_in(nc, "recv2", [4, 4, 128, NTOK], BF16), "cvg": dram_in(nc, "cvg", [4, 128, NTOK], BF16),
           "x": dram_in(nc, "x", [NTOK, D]), "gate": dram_in(nc, "gate", [D]), "w_out": dram_in(nc, "w_out", [D, D]),
           "w_pw": dram_in(nc, "w_pw", [512, 512]), "lng": dram_in(nc, "lng", [512]), "lnb": dram_in(nc, "lnb", [512]),
           "xout": xmid}
    ioa = {"x": xmid, "shift": dram_in(nc, "shift", [D]), "scale": dram_in(nc, "scale", [D]),
           "norm_g": dram_in(nc, "norm_g", [D]), "w_in": dram_in(nc, "w_in", [D, DIN]), "w_fft": dram_in(nc, "w_fft", [512, 512]),
           "send1": dram_out(nc, "send1", [4, NBLK1, BLK], BF16), "cvg": dram_out(nc, "cvg_next", [4, 128, NTOK], BF16)}
    _const_io(nc, ioa, ["ident", "ccsc"])
    xcopy = dram_out(nc, "xout", [NTOK, D])
    with ExitStack() as es:
        S = Sched(nc, es)
        emit_C(S, ioc, False)
        bx = Buf()
        S.dma("sp", xcopy, xmid, (), [bx])
        emit_A(S, ioa)
        S.run()
    _PROGS["CA"] = nc
    return nc


def _run(nc, in_maps):
    res = run_bass_kernel_spmd(nc, in_maps, core_ids=list(range(NCORE)))
    return res.results


def _percore_layer_inputs(l, na_rel_bias, ret_logit_fwd, ret_logit_bwd, conv_w, conv_b):
    col = np.arange(64)
    idx = np.clip(col[None, :] - col[:, None] + 15, 0, 30)
    outs = []
    for j in range(4):
        nab = np.empty((128, 15, 64), np.float32)
        for hh in range(2):
            rb = na_rel_bias[l, 2 * j + hh]
            nab[hh * 64:(hh + 1) * 64] = np.transpose(rb[:, idx], (1, 0, 2))
        lf = ret_logit_fwd[l, 2 * j:2 * j + 2]
        lb = ret_logit_bwd[l, 2 * j:2 * j + 2]
        retlg = np.empty((128, 2), np.float32)
        retlg[0:64, 0], retlg[64:128, 0] = lf[0], lf[1]
        retlg[0:64, 1], retlg[64:128, 1] = lb[0], lb[1]
        retlg2 = np.empty((128, 4), np.float32)
        retlg2[:, 0], retlg2[:, 1], retlg2[:, 2], retlg2[:, 3] = lf[0], lf[1], lb[0], lb[1]
        cw = np.ascontiguousarray(conv_w[l][:, j * 128:(j + 1) * 128].T)
        cb = np.ascontiguousarray(conv_b[l][j * 128:(j + 1) * 128, None])
        outs.append(dict(nab=nab, retlg=retlg, retlg2=retlg2, cw=cw, cb=cb))
    return outs


def kernel(x, c, norm_g, w_ada, b_ada, w_in, w_fft, na_rel_bias, ret_logit_fwd, ret_logit_bwd,
           conv_w, conv_b, conv_ln_g, conv_ln_b, conv_w_pw, w_out, final_g, _debug=None):
    f32 = np.float32
    x = np.asarray(x, f32)
    C = consts()
    cT = np.ascontiguousarray(np.asarray(c, f32).T.reshape(16, 128, NB).transpose(1, 0, 2))
    maps = []
    for k in range(NCORE):
        l, c0 = k // 4, (k % 4) * 1536
        maps.append(dict(cT=cT, wada=np.ascontiguousarray(w_ada[l][:, c0:c0 + 1536]),
                         bada=np.ascontiguousarray(np.broadcast_to(b_ada[l][None, c0:c0 + 1536], (2, 1536)))))
    r = _run(prog_mod(), maps)
    mod = np.stack([np.concatenate([np.asarray(r[l * 4 + q]["mod"]) for q in range(4)], axis=1) for l in range(DEPTH)])
    xs = [np.ascontiguousarray(x[k // 4, (k % 4) * NTOK:(k % 4 + 1) * NTOK]) for k in range(NCORE)]

    def a_inputs(l, k):
        b = k // 4
        return dict(shift=np.ascontiguousarray(mod[l, b, 0:D]), scale=np.ascontiguousarray(mod[l, b, D:2 * D]),
                    norm_g=norm_g[l], w_in=w_in[l], w_fft=w_fft[l], ident=C["ident"], ccsc=C["ccsc"])

    send1 = cvg = None
    for l in range(DEPTH):
        last = (l == DEPTH - 1)
        if l == 0:
            rA = _run(prog_A(), [dict(x=xs[k], **a_inputs(0, k)) for k in range(NCORE)])
            send1 = [np.asarray(rA[k]["send1"]) for k in range(NCORE)]
            cvg = [np.asarray(rA[k]["cvg"]) for k in range(NCORE)]
        pl = _percore_layer_inputs(l, na_rel_bias, ret_logit_fwd, ret_logit_bwd, conv_w, conv_b)
        maps = []
        for k in range(NCORE):
            b, j = k // 4, k % 4
            recv1 = np.stack([send1[b * 4 + i][j] for i in range(4)])
            m = dict(recv1=recv1, **pl[j])
            for n in B_CONSTS:
                m[n] = C[n]
            maps.append(m)
        rB = _run(prog_B(), maps)
        send2 = [np.asarray(rB[k]["send2"]) for k in range(NCORE)]
        maps = []
        for k in range(NCORE):
            b, i = k // 4, k % 4
            recv2 = np.stack([send2[b * 4 + j][i] for j in range(4)])
            m = dict(recv2=recv2, cvg=cvg[k], x=xs[k], gate=np.ascontiguousarray(mod[l, b, 2 * D:3 * D]), w_out=w_out[l],
                     w_pw=conv_w_pw[l], lng=conv_ln_g[l], lnb=conv_ln_b[l])
            if last:
                m["final_g"] = final_g
            else:
                m.update(a_inputs(l + 1, k))
            maps.append(m)
        if last:
            rC = _run(prog_C(True), maps)
        else:
            rC = _run(prog_CA(), maps)
            send1 = [np.asarray(rC[k]["send1"]) for k in range(NCORE)]
            cvg = [np.asarray(rC[k]["cvg_next"]) for k in range(NCORE)]
        xs = [np.asarray(rC[k]["xout"]) for k in range(NCORE)]
    out = np.empty((NB, SEQ, D), f32)
    for k in range(NCORE):
        out[k // 4, (k % 4) * NTOK:(k % 4 + 1) * NTOK] = xs[k]
    return out
```

```python
import numpy as np
import ml_dtypes
from contextlib import ExitStack
import concourse.bass as bass
import concourse.mybir as mybir
from concourse.bass_utils import run_bass_kernel_spmd

F32 = mybir.dt.float32
BF16 = mybir.dt.bfloat16
AF = mybir.ActivationFunctionType
ALU = mybir.AluOpType
AX = mybir.AxisListType

D = 2048
SEQ = 4096
NB = 2
DEPTH = 2
DIN = 6656
NTOK = 1024
NCORE = 8
EPS = 1e-6
NBLK1 = 13
BLK = 128 * 1024
B_P, B_Q, B_FG, B_NQ, B_NK, B_NV, B_NG, B_RQ, B_RK, B_RV, B_RG, B_CA, B_CB = range(13)


PROFILE_SCOPES = False


class Buf:
    __slots__ = ("name", "w", "r")

    def __init__(self, name=""):
        self.name = name
        self.w = None
        self.r = {}


class Sched:
    COMPUTE = ("pe", "act", "dve", "pool")
    NDMA = 12

    def __init__(self, nc, es):
        self.nc = nc
        self.es = es
        self.eng = {"pe": nc.tensor, "act": nc.scalar, "dve": nc.vector, "pool": nc.gpsimd, "sp": nc.sync}
        self.streams = {e: [] for e in self.eng}
        self.sems = {}
        self.cnt = {}
        self.waited = {e: {} for e in self.eng}
        for e in self.COMPUTE:
            self.sems[e] = es.enter_context(nc.semaphore("s_" + e))
            self.cnt[e] = 0
        self.dq = {}
        for q in ("sp", "pool", "act"):
            lst = []
            for i in range(self.NDMA):
                k = "d_%s_%d" % (q, i)
                self.sems[k] = es.enter_context(nc.semaphore(k))
                self.cnt[k] = 0
                lst.append(k)
            self.dq[q] = [lst, 0]
        self.scope = None
        self.psum = [es.enter_context(nc.psum_tensor("psb%d" % i, [128, 512], F32)) for i in range(8)]
        self.pbuf = [Buf("ps%d" % i) for i in range(8)]

    def sb(self, es, name, shape, dt):
        return es.enter_context(self.nc.sbuf_tensor(name, list(shape), dt))

    def _deps(self, reads, writes):
        deps = {}

        def add(k, v):
            if deps.get(k, 0) < v:
                deps[k] = v
        for b in reads:
            if b.w is not None:
                add(*b.w)
        for b in writes:
            if b.w is not None:
                add(*b.w)
            for k, v in b.r.items():
                add(k, v)
        return deps

    def _emit_waits(self, eng, deps):
        for k, v in deps.items():
            if eng == "pe" and k == "pe":
                continue
            if self.waited[eng].get(k, 0) < v:
                self.waited[eng][k] = v
                self.streams[eng].append(("wait", k, v))

    def _mark(self, tok, reads, writes):
        k, v = tok
        for b in reads:
            if b.r.get(k, 0) < v:
                b.r[k] = v
        for b in writes:
            b.w = tok
            b.r = {}

    def op(self, eng, fn, reads=(), writes=()):
        self._emit_waits(eng, self._deps(reads, writes))
        self.cnt[eng] += 1
        tok = (eng, self.cnt[eng])
        self.streams[eng].append(("op", fn, eng, 1, self.scope))
        self._mark(tok, reads, writes)
        return tok

    def dma(self, q, out, in_, reads=(), writes=(), slow=False):
        lst, idx = self.dq[q]
        k = lst[idx % len(lst)]
        self.dq[q][1] = idx + 1
        deps = self._deps(reads, writes)
        if self.cnt[k] > 0:
            deps[k] = max(deps.get(k, 0), self.cnt[k])
        self._emit_waits(q, deps)
        self.cnt[k] += 16
        tok = (k, self.cnt[k])
        if slow:
            self.streams[q].append(("op", lambda e: e.dma_start(out=out, in_=in_, allow_slow_non_contiguous=True), k, 16, self.scope))
        else:
            self.streams[q].append(("op", lambda e: e.dma_start(out=out, in_=in_), k, 16, self.scope))
        self._mark(tok, reads, writes)
        return tok

    def coll(self, kind, ins, outs, groups, reads=(), writes=()):
        q = "pool"
        lst, idx = self.dq[q]
        k = lst[idx % len(lst)]
        self.dq[q][1] = idx + 1
        deps = self._deps(reads, writes)
        if self.cnt[k] > 0:
            deps[k] = max(deps.get(k, 0), self.cnt[k])
        self._emit_waits(q, deps)
        self.cnt[k] += 16
        tok = (k, self.cnt[k])
        self.streams[q].append(("op", lambda e: e.collective_compute(kind, ALU.bypass, replica_groups=groups, ins=ins, outs=outs), k, 16, self.scope))
        self._mark(tok, reads, writes)
        return tok

    def barrier(self):
        allv = {k: v for k, v in self.cnt.items() if v > 0}
        for e in self.eng:
            self._emit_waits(e, dict(allv))

    def run(self):
        nc = self.nc
        self.barrier()
        with nc.Block() as block:
            def make(ename):
                def body(e):
                    cur, ctx = None, None
                    for it in self.streams[ename]:
                        if it[0] == "wait":
                            e.wait_ge(self.sems[it[1]], it[2])
                        else:
                            _, fn, k, inc, sc = it
                            if PROFILE_SCOPES and sc != cur:
                                if ctx is not None:
                                    ctx.__exit__(None, None, None)
                                    ctx = None
                                if sc is not None:
                                    ctx = nc.named_scope(sc)
                                    ctx.__enter__()
                                cur = sc
                            fn(e).then_inc(self.sems[k], inc)
                    if ctx is not None:
                        ctx.__exit__(None, None, None)
                return body
            block.sync(make("sp"))
            block.tensor(make("pe"))
            block.scalar(make("act"))
            block.vector(make("dve"))
            block.gpsimd(make("pool"))

    def mm(self, out, lhsT, rhs, start, stop, r, w):
        return self.op("pe", lambda e: e.matmul(out, lhsT=lhsT, rhs=rhs, start=start, stop=stop), r, w)

    def tr(self, out, in_, ident, r, w):
        return self.op("pe", lambda e: e.transpose(out=out, in_=in_, identity=ident), r, w)

    def act(self, out, in_, func, r, w, bias=None, scale=None, accum=None):
        kw = {}
        if bias is not None:
            kw["bias"] = bias
        if scale is not None:
            kw["scale"] = scale
        if accum is not None:
            kw["accum_out"] = accum
        return self.op("act", lambda e: e.activation(out=out, in_=in_, func=func, **kw), r, w)

    def tt(self, eng, out, in0, in1, op, r, w):
        return self.op(eng, lambda e: e.tensor_tensor(out=out, in0=in0, in1=in1, op=op), r, w)

    def ts(self, eng, out, in0, s1, op0, r, w, s2=None, op1=None):
        if op1 is None:
            return self.op(eng, lambda e: e.tensor_scalar(out=out, in0=in0, scalar1=s1, scalar2=None, op0=op0), r, w)
        return self.op(eng, lambda e: e.tensor_scalar(out=out, in0=in0, scalar1=s1, scalar2=s2, op0=op0, op1=op1), r, w)

    def stt(self, out, in0, scalar, in1, op0, op1, r, w):
        return self.op("dve", lambda e: e.scalar_tensor_tensor(out=out, in0=in0, scalar=scalar, in1=in1, op0=op0, op1=op1), r, w)

    def copy(self, eng, out, in_, r, w):
        if eng == "act":
            return self.op("act", lambda e: e.copy(out=out, in_=in_), r, w)
        return self.op(eng, lambda e: e.tensor_copy(out=out, in_=in_), r, w)

    def red(self, out, in_, op, r, w, negate=False):
        return self.op("dve", lambda e: e.tensor_reduce(out=out, in_=in_, axis=AX.X, op=op, negate=negate), r, w)

    def memset(self, eng, ap, val, w):
        return self.op(eng, lambda e: e.memset(ap, val), (), w)


def bc(ap, pos, count):
    lst = [list(x) for x in ap.ap]
    lst.insert(pos, [0, count])
    return bass.AP(ap.tensor, ap.offset, lst)


def dram_in(nc, name, shape, dt=F32):
    return nc.dram_tensor(name, list(shape), dt, kind="ExternalInput").ap()


def dram_out(nc, name, shape, dt=F32):
    return nc.dram_tensor(name, list(shape), dt, kind="ExternalOutput").ap()


def dram_tmp(nc, name, shape, dt=F32):
    return nc.dram_tensor(name, list(shape), dt, kind="Internal").ap()


_CONST = None


def consts():
    global _CONST
    if _CONST is not None:
        return _CONST
    c = {}
    c["ident"] = np.eye(128, dtype=np.float32)
    i128 = np.arange(128, dtype=np.float64)
    ang = 2 * np.pi * np.outer(i128, i128) / 128.0
    ccsc = np.stack([np.cos(ang), np.sin(ang)], axis=1) / np.sqrt(128.0)
    c["ccsc"] = ccsc.astype(np.float32)
    s1 = np.arange(64)[:, None, None]
    s2 = np.arange(64)[None, :, None]
    k1 = np.arange(64)[None, None, :]
    th = 2 * np.pi * (k1 * s1 / 64.0 + k1 * s2 / 4096.0)
    m1 = np.zeros((128, 64, 128), np.float64)
    m1[0:64, :, 0:64] = np.cos(th)
    m1[0:64, :, 64:128] = np.sin(th)
    m1[64:128, :, 0:64] = -np.sin(th)
    m1[64:128, :, 64:128] = np.cos(th)
    c["m1"] = m1.astype(ml_dtypes.bfloat16)
    ph = 2 * np.pi * np.outer(np.arange(64), np.arange(64)) / 64.0
    m3 = np.concatenate([np.cos(ph), -np.sin(ph)], axis=0) / 64.0
    c["m3"] = m3.astype(ml_dtypes.bfloat16)
    col = np.arange(64)
    cs = np.clip(col - 8, 0, 48)
    rel = col[None, :] - cs[:, None]
    m = np.where((rel >= 0) & (rel < 16), 0.0, -30000.0).astype(np.float32)
    c["namask"] = np.concatenate([m, m], axis=0)
    half = 32
    inv = 10000.0 ** (-np.arange(half, dtype=np.float32) / half)
    angr = np.arange(SEQ, dtype=np.float32)[:, None] * inv[None, :]
    cosr = np.cos(angr.astype(np.float64)).T
    sinr = np.sin(angr.astype(np.float64)).T
    cos64 = np.concatenate([cosr, cosr], axis=0)
    sin64 = np.concatenate([-sinr, sinr], axis=0)
    c["rcos"] = np.concatenate([cos64, cos64], axis=0).astype(np.float32)
    c["rsin"] = np.concatenate([sin64, sin64], axis=0).astype(np.float32)
    pm = np.zeros((128, 128), np.float32)
    for mm_ in range(128):
        pm[mm_ ^ 32, mm_] = 1.0
    c["perm"] = pm.astype(ml_dtypes.bfloat16)
    j = np.arange(128)[:, None]
    i = np.arange(128)[None, :]
    BIG = 1.0e7
    distf = np.where(i >= j, (i - j), BIG).astype(np.float32)
    distb = np.where(j > i, (j - i), BIG).astype(np.float32)
    io1 = np.broadcast_to(np.arange(1, 129, dtype=np.float32)[None, :], (128, 128))
    io2 = np.broadcast_to((128 - np.arange(128, dtype=np.float32))[None, :], (128, 128))
    c["rtab"] = np.ascontiguousarray(np.stack([distf, distb, io1, io2], axis=1)).astype(np.float32)
    ce = np.stack([127 - np.arange(128), np.arange(128)], axis=1).astype(np.float32)
    c["cexp"] = ce
    _CONST = c
    return c


CONST_SHAPES = {"ident": [128, 128], "ccsc": [128, 2, 128], "m1": [128, 64, 128], "m3": [128, 64],
                "namask": [128, 64], "rcos": [128, SEQ], "rsin": [128, SEQ], "perm": [128, 128],
                "rtab": [128, 4, 128], "cexp": [128, 2]}


def emit_mod(S, io):
    nc = S.nc
    with ExitStack() as es:
        cT = S.sb(es, "m_cT", [128, 16, 2], F32)
        bcT = Buf()
        ba = S.sb(es, "m_ba", [2, 1536], F32)
        bba = Buf()
        res = S.sb(es, "m_res", [2, 1536], F32)
        bres = Buf()
        wt = [S.sb(es, "m_w%d" % i, [128, 16, 512], F32) for i in range(3)]
        bw = [Buf() for _ in range(3)]
        S.dma("sp", cT[:], io["cT"], (), [bcT])
        S.dma("sp", ba[:], io["bada"], (), [bba])
        for n in range(3):
            S.dma("sp", wt[n][:], io["wada"][:, n * 512:(n + 1) * 512].rearrange("(k p) c -> p k c", p=128), (), [bw[n]])
        S.act(cT[:], cT[:], AF.Silu, [bcT], [bcT])
        cR = S.sb(es, "m_cR", [128, 16, 64, 2], F32)
        bcR = Buf()
        S.copy("dve", cR[:], bc(cT[:], 2, 64), [bcT], [bcR])
        for n in range(3):
            ps = S.psum[n]
            pb = S.pbuf[n]
            for k in range(16):
                S.mm(ps[:, :], cR[:, k, :, :].rearrange("p r b -> p (r b)"), wt[n][:, k, :], k == 0, k == 15, [bcR, bw[n]], [pb])
            S.tt("dve", res[:, n * 512:(n + 1) * 512], ps[0:2, :], ba[:, n * 512:(n + 1) * 512], ALU.add, [pb, bba], [bres])
        S.dma("sp", io["mod"], res[:], [bres], [Buf()])
    S.barrier()


def emit_A(S, io, layer_tag=""):
    nc = S.nc
    with ExitStack() as es:
        ident = S.sb(es, "a_ident", [128, 128], F32)
        b_ident = Buf()
        S.dma("sp", ident[:], io["ident"], (), [b_ident])
        gT = S.sb(es, "a_gT", [128, 16], F32)
        scT = S.sb(es, "a_scT", [128, 16], F32)
        shT = S.sb(es, "a_shT", [128, 16], F32)
        gsT = S.sb(es, "a_gsT", [128, 16], F32)
        b_mod = Buf()
        b_gs = Buf()
        S.dma("sp", gT[:], io["norm_g"].rearrange("(c p) -> p c", p=128), (), [b_mod], slow=True)
        S.dma("sp", scT[:], io["scale"].rearrange("(c p) -> p c", p=128), (), [b_mod], slow=True)
        S.dma("sp", shT[:], io["shift"].rearrange("(c p) -> p c", p=128), (), [b_mod], slow=True)
        S.stt(gsT[:], scT[:], 1.0, gT[:], ALU.add, ALU.mult, [b_mod], [b_gs])

        ccsc = S.sb(es, "a_ccsc", [128, 2, 128], F32)
        b_cc = Buf()
        S.dma("sp", ccsc[:], io["ccsc"], (), [b_cc])
        wf = S.sb(es, "a_wf", [128, 4, 512], F32)
        b_wf = Buf()
        S.dma("sp", wf[:], io["w_fft"].rearrange("(g p) c -> p g c", p=128), (), [b_wf])
        mcs = S.sb(es, "a_mcs", [128, 2, 4, 512], BF16)
        b_mcs = Buf()
        for cs in range(2):
            for g in range(4):
                pi = (cs * 4 + g) % 4
                ps, pb = S.psum[pi], S.pbuf[pi]
                S.mm(ps[:, :], ccsc[:, cs, :], wf[:, g, :], True, True, [b_cc, b_wf], [pb])
                S.copy("act" if g % 2 else "dve", mcs[:, cs, g, :], ps[:, :], [pb], [b_mcs])

        hT = S.sb(es, "a_hT", [128, 16, NTOK], BF16)
        b_hT = [Buf() for _ in range(2)]
        xt = [S.sb(es, "a_x%d" % i, [128, D], F32) for i in range(4)]
        b_xt = [Buf() for _ in range(4)]
        junk = S.sb(es, "a_junk", [128, D], F32)
        b_junk = Buf()
        ss = S.sb(es, "a_ss", [128, 8], F32)
        rs = S.sb(es, "a_rs", [128, 8], F32)
        b_ss = [Buf() for _ in range(8)]
        for grp in range(2):
            for tt_ in range(4):
                t = grp * 4 + tt_
                S.dma("sp", xt[tt_][:], io["x"][t * 128:(t + 1) * 128, :], (), [b_xt[tt_]])
                S.act(junk[:], xt[tt_][:], AF.Square, [b_xt[tt_]], [b_junk, b_ss[t]], accum=ss[:, t:t + 1])
                S.ts("dve", rs[:, t:t + 1], ss[:, t:t + 1], 1.0 / D, ALU.mult, [b_ss[t]], [b_ss[t]], s2=EPS, op1=ALU.add)
                S.act(rs[:, t:t + 1], rs[:, t:t + 1], AF.Sqrt, [b_ss[t]], [b_ss[t]])
                S.op("dve", lambda e, o=rs[:, t:t + 1]: e.reciprocal(out=o, in_=o), [b_ss[t]], [b_ss[t]])
                S.ts("dve", xt[tt_][:], xt[tt_][:], rs[:, t:t + 1], ALU.mult, [b_xt[tt_], b_ss[t]], [b_xt[tt_]])
            for k in range(16):
                pi = 4 + (k % 4)
                ps, pb = S.psum[pi], S.pbuf[pi]
                for tt_ in range(4):
                    S.tr(ps[:, tt_ * 128:(tt_ + 1) * 128], xt[tt_][:, k * 128:(k + 1) * 128], ident[:], [b_xt[tt_], b_ident], [pb])
                S.ts("dve", hT[:, k, grp * 512:(grp + 1) * 512], ps[:, :], gsT[:, k:k + 1], ALU.mult,
                     [pb, b_gs, b_mod], [b_hT[grp]], s2=shT[:, k:k + 1], op1=ALU.add)

        wb = [S.sb(es, "a_w%d" % i, [128, 16, 512], BF16) for i in range(3)]
        b_wb = [Buf() for _ in range(3)]
        st_fm = [S.sb(es, "a_sfm%d" % i, [128, NTOK], BF16) for i in range(2)]
        b_sfm = [Buf() for _ in range(2)]
        st_tm = [S.sb(es, "a_stm%d" % i, [128, 8, 512], BF16) for i in range(2)]
        b_stm = [Buf() for _ in range(2)]
        fxT = S.sb(es, "a_fxT", [128, 4, NTOK], BF16)
        b_fx = Buf()
        send1 = io["send1"]

        def wload(p):
            slot = p % 3
            S.dma("pool", wb[slot][:], io["w_in"][:, p * 512:(p + 1) * 512].rearrange("(k p) c -> p k c", p=128),
                  (), [b_wb[slot]])

        cnt = {"fm": 0, "tm": 0, "ev": 0, "ps": 0}

        def evac(out, in_, r, w, scale=None):
            cnt["ev"] += 1
            if cnt["ev"] % 2:
                if scale is None:
                    S.copy("act", out, in_, r, w)
                else:
                    S.act(out, in_, AF.Copy, r, w, scale=scale)
            else:
                if scale is None:
                    S.copy("dve", out, in_, r, w)
                else:
                    S.ts("dve", out, in_, scale, ALU.mult, r, w)

        def nextps():
            cnt["ps"] += 1
            i = cnt["ps"] % 4
            return S.psum[i], S.pbuf[i]

        def fm_piece(p, dst_fn, scale=None):
            slot = p % 3
            for j in range(4):
                dram_ap, sb_ap = dst_fn(j)
                if sb_ap is None:
                    si = cnt["fm"] % 2
                    cnt["fm"] += 1
                    stage, bst = st_fm[si], b_sfm[si]
                    tgt = stage
                else:
                    tgt, bst = sb_ap, b_fx
                for h in range(2):
                    ps, pb = nextps()
                    for k in range(16):
                        S.mm(ps[:, :], wb[slot][:, k, j * 128:(j + 1) * 128], hT[:, k, h * 512:(h + 1) * 512],
                             k == 0, k == 15, [b_wb[slot], b_hT[h]], [pb])
                    if sb_ap is None:
                        evac(tgt[:, h * 512:(h + 1) * 512], ps[:, :], [pb], [bst], scale)
                    else:
                        evac(tgt[:, j, h * 512:(h + 1) * 512], ps[:, :], [pb], [bst], scale)
                if dram_ap is not None:
                    S.dma("sp", dram_ap, tgt[:], [bst], [Buf()])

        def tm_from(lhs_fn, nk, rhs_fn, rbufs, blk):
            si = cnt["tm"] % 2
            cnt["tm"] += 1
            stage, bst = st_tm[si], b_stm[si]
            for t in range(8):
                ps, pb = nextps()
                for k in range(nk):
                    S.mm(ps[:, :], lhs_fn(k, t), rhs_fn(k), k == 0, k == nk - 1, rbufs(t), [pb])
                evac(stage[:, t, :], ps[:, :], [pb], [bst])
            for j in range(4):
                dst = send1[j, blk, :].rearrange("(t p c) -> p t c", p=128, c=128)
                S.dma("sp", dst, stage[:, :, j * 128:(j + 1) * 128], [bst], [Buf()])

        def send_fm(blk):
            return lambda j: (send1[j, blk, :].rearrange("(p t) -> p t", p=128), None)

        wload(0)
        wload(1)
        for p in range(13):
            if p + 2 < 13:
                wload(p + 2)
            slot = p % 3
            if p == 0:
                fm_piece(0, lambda j: (None, fxT))
                for cs, blk in ((0, B_P), (1, B_Q)):
                    tm_from(lambda k, t: fxT[:, k, t * 128:(t + 1) * 128], 4,
                            lambda k, cs=cs: mcs[:, cs, k, :], lambda t: [b_fx, b_mcs], blk)
            elif p == 1:
                fm_piece(1, send_fm(B_FG))
            elif p == 2:
                fm_piece(2, send_fm(B_NQ), scale=0.125)
            elif p == 3:
                fm_piece(3, send_fm(B_NK))
            elif p in (4, 8, 9):
                blk = {4: B_NV, 8: B_RV, 9: B_RG}[p]
                tm_from(lambda k, t: hT[:, k, t * 128:(t + 1) * 128], 16,
                        lambda k, slot=slot: wb[slot][:, k, :], lambda t, slot=slot: [b_hT[t // 4], b_wb[slot]], blk)
            elif p == 5:
                fm_piece(5, send_fm(B_NG))
            elif p == 6:
                fm_piece(6, send_fm(B_RQ), scale=0.125)
            elif p == 7:
                fm_piece(7, send_fm(B_RK))
            elif p == 10:
                fm_piece(10, send_fm(B_CA))
            elif p == 11:
                fm_piece(11, send_fm(B_CB))
            elif p == 12:
                fm_piece(12, lambda j: (io["cvg"][j, :, :], None))
    S.barrier()


def _row_start(r):
    return min(max(r - 4, 0), 56)


def emit_B(S, io):
    nc = S.nc
    recv1 = io["recv1"]
    send2 = io["send2"]

    def load_fm(es, name, blk):
        t = S.sb(es, name, [128, SEQ], BF16)
        b = Buf()
        for i in range(4):
            S.dma(("sp", "act")[i % 2], t[:, i * 1024:(i + 1) * 1024], recv1[i, blk, :].rearrange("(p t) -> p t", p=128), (), [b])
        return t, b

    def load_tm(es, name, blk):
        t = S.sb(es, name, [128, 32, 128], BF16)
        b = Buf()
        for i in range(4):
            S.dma(("sp", "act")[i % 2], t[:, i * 8:(i + 1) * 8, :], recv1[i, blk, :].rearrange("(t p c) -> p t c", p=128, c=128), (), [b])
        return t, b

    def store_o(t, b, blk):
        for i in range(4):
            S.dma("sp", send2[i, blk, :, :], t[:, i * 1024:(i + 1) * 1024], [b], [Buf()])

    with ExitStack() as es0:
        identf = S.sb(es0, "b_identf", [128, 128], F32)
        identb = S.sb(es0, "b_identb", [128, 128], BF16)
        b_id = Buf()
        S.dma("sp", identf[:], io["ident"], (), [b_id])
        S.copy("dve", identb[:], identf[:], [b_id], [b_id])

        qT = S.sb(es0, "n_q", [64, 2, SEQ], BF16)
        kT = S.sb(es0, "n_k", [64, 2, SEQ], BF16)
        ng = S.sb(es0, "n_g", [128, SEQ], BF16)
        ve = S.sb(es0, "n_ve", [128, 32, 128], BF16)
        vo = S.sb(es0, "n_vo", [128, 32, 128], BF16)
        nab = S.sb(es0, "n_bias", [128, 15, 64], F32)
        msk = S.sb(es0, "n_mask", [128, 64], F32)
        b_q, b_k, b_ng, b_ve, b_vo, b_nab = Buf(), Buf(), Buf(), Buf(), Buf(), Buf()

        S.scope = "fft"
        with ExitStack() as es:
            pq = S.sb(es, "f_pq", [128, 64, 128], BF16)
            b_pq = Buf()
            for i in range(4):
                for c_, blk in ((0, B_P), (1, B_Q)):
                    S.dma("sp", pq[c_ * 64 + i * 16:c_ * 64 + (i + 1) * 16, :, :],
                          recv1[i, blk, :].rearrange("(a s c) -> a s c", a=16, c=128), (), [b_pq])
            m1 = S.sb(es, "f_m1", [128, 64, 128], BF16)
            b_m1 = Buf()
            for h in range(4):
                S.dma("act", m1[:, h * 16:(h + 1) * 16, :], io["m1"][:, h * 16:(h + 1) * 16, :], (), [b_m1])
            m3 = S.sb(es, "f_m3", [128, 64], BF16)
            b_m3 = Buf()
            S.dma("act", m3[:], io["m3"], (), [b_m3])
            fg, b_fg = load_fm(es, "f_fg", B_FG)
            gf = S.sb(es, "f_gate", [128, SEQ], BF16)
            b_gf = Buf()
            S.act(gf[:], fg[:], AF.Silu, [b_fg], [b_gf])
            S.scope = "conv"
            ca, b_ca = load_fm(es, "c_a", B_CA)
            cbt, b_cb = load_fm(es, "c_b", B_CB)
            S.act(cbt[:], cbt[:], AF.Sigmoid, [b_cb], [b_cb])
            up = S.sb(es, "c_up", [128, SEQ + 32], BF16)
            b_up = Buf()
            S.memset("pool", up[:, 0:15], 0.0, [b_up])
            S.memset("pool", up[:, 15 + SEQ:SEQ + 32], 0.0, [b_up])
            S.tt("dve", up[:, 15:15 + SEQ], ca[:], cbt[:], ALU.mult, [b_ca, b_cb], [b_up])
            cw = S.sb(es, "c_w", [128, 31], F32)
            cbias = S.sb(es, "c_bias", [128, 1], F32)
            b_cw = Buf()
            S.dma("sp", cw[:], io["cw"], (), [b_cw])
            S.dma("sp", cbias[:], io["cb"], (), [b_cw])
            dg = S.sb(es, "c_diag", [128, 31, 128], BF16)
            b_dg = Buf()
            for k in range(31):
                S.ts("pool", dg[:, k, :], identf[:], cw[:, k:k + 1], ALU.mult, [b_id, b_cw], [b_dg], s2=1.0, op1=ALU.mult)
            S.scope = "na"
            for i in range(4):
                for hh in range(2):
                    S.dma("act", qT[:, hh, i * 1024:(i + 1) * 1024],
                          recv1[i, B_NQ, hh * 65536:(hh + 1) * 65536].rearrange("(p t) -> p t", p=64), (), [b_q])
                    S.dma("act", kT[:, hh, i * 1024:(i + 1) * 1024],
                          recv1[i, B_NK, hh * 65536:(hh + 1) * 65536].rearrange("(p t) -> p t", p=64), (), [b_k])
            for i in range(4):
                S.dma("act", ng[:, i * 1024:(i + 1) * 1024], recv1[i, B_NG, :].rearrange("(p t) -> p t", p=128), (), [b_ng])
                S.dma("act", ve[:, i * 8:(i + 1) * 8, :], recv1[i, B_NV, :].rearrange("(t p c) -> p t c", p=128, c=128), (), [b_ve])
            for i in range(4):
                flat = recv1[i, B_NV, :]
                S.dma("act", vo[:, i * 8:i * 8 + 7, :],
                      flat[64 * 128:(64 + 7 * 128) * 128].rearrange("(t p c) -> p t c", p=128, c=128), (), [b_vo])
                if i < 3:
                    S.dma("act", vo[0:64, i * 8 + 7, :], flat[960 * 128:1024 * 128].rearrange("(p c) -> p c", c=128), (), [b_vo])
                    S.dma("act", vo[64:128, i * 8 + 7, :], recv1[i + 1, B_NV, 0:64 * 128].rearrange("(p c) -> p c", c=128), (), [b_vo])
            S.dma("act", nab[:], io["nab"], (), [b_nab])
            S.dma("act", msk[:], io["namask"], (), [b_nab])
            S.scope = "fft"
            tsb = S.sb(es, "f_t", [128, 64, 128], BF16)
            b_t = Buf()
            for g in range(16):
                ps, pb = S.psum[g % 2], S.pbuf[g % 2]
                for q in range(4):
                    s2 = g * 4 + q
                    S.mm(ps[:, q * 128:(q + 1) * 128], m1[:, s2, :], pq[:, s2, :], True, True, [b_m1, b_pq], [pb])
                S.copy("act" if g % 2 else "dve", tsb[:, g * 4:(g + 1) * 4, :],
                       ps[:, :].rearrange("p (a c) -> p a c", a=4), [pb], [b_t])
            scr = io["fftscr"]
            b_scr = Buf()
            S.dma("sp", scr.rearrange("(p s c) -> p s c", p=128, c=128), tsb[:], [b_t], [b_scr])
            t2 = S.sb(es, "f_t2", [128, 64, 128], BF16)
            b_t2 = Buf()
            src = scr.rearrange("(c k s h) -> c s k h", c=2, k=64, s=64)
            for c_ in range(2):
                S.dma("sp", t2[c_ * 64:(c_ + 1) * 64, :, :], src[c_], [b_scr], [b_t2])
            S.scope = "conv"
            yc = S.sb(es, "c_y", [128, SEQ], BF16)
            b_yc = Buf()
            for t in range(8):
                ps, pb = S.psum[t % 2], S.pbuf[t % 2]
                for k in range(31):
                    S.mm(ps[:, :], dg[:, k, :], up[:, t * 512 + k:t * 512 + k + 512], k == 0, k == 30, [b_dg, b_up], [pb])
                S.ts("dve", yc[:, t * 512:(t + 1) * 512], ps[:, :], cbias[:, 0:1], ALU.add, [pb, b_cw], [b_yc])
            store_o(yc, b_yc, 3)
            S.scope = "fft"
            of = S.sb(es, "f_o", [128, SEQ], BF16)
            b_of = Buf()
            ofv = of[:].rearrange("p (k2 k1) -> p k1 k2", k1=64)
            gfv = gf[:].rearrange("p (k2 k1) -> p k1 k2", k1=64)
            for g in range(8):
                ps, pb = S.psum[2 + g % 2], S.pbuf[2 + g % 2]
                for q in range(8):
                    k1 = g * 8 + q
                    S.mm(ps[:, q * 64:(q + 1) * 64], t2[:, k1, :], m3[:], True, True, [b_t2, b_m3], [pb])
                S.tt("dve", ofv[:, g * 8:(g + 1) * 8, :], ps[:, :].rearrange("p (a k) -> p a k", a=8),
                     gfv[:, g * 8:(g + 1) * 8, :], ALU.mult, [pb, b_gf], [b_of])
            store_o(of, b_of, 0)
        S.barrier()

        S.scope = "na"
        with ExitStack() as es:
            gn = S.sb(es, "n_gate", [128, SEQ], BF16)
            b_gn = Buf()
            S.act(gn[:], ng[:], AF.Silu, [b_ng], [b_gn])
            S.tt("dve", nab[:], nab[:], bc(msk[:], 1, 15), ALU.add, [b_nab], [b_nab])
            on = S.sb(es, "n_o", [128, SEQ], BF16)
            b_on = Buf()
            sc = [S.sb(es, "n_sc%d" % i, [128, 512], F32) for i in range(3)]
            b_sc = [Buf() for _ in range(3)]
            pe_ = [S.sb(es, "n_p%d" % i, [128, 512], BF16) for i in range(3)]
            b_pe = [Buf() for _ in range(3)]
            pn = [S.sb(es, "n_pn%d" % i, [128, 512], BF16) for i in range(3)]
            b_pn = [Buf() for _ in range(3)]
            pt = [S.sb(es, "n_pt%d" % i, [128, 4, 128], BF16) for i in range(3)]
            b_pt = [Buf() for _ in range(3)]
            st = S.sb(es, "n_st", [128, 64, 4], F32)
            b_st = [Buf() for _ in range(64)]
            def na_stage_qk(r):
                d2 = r % 3
                ks = _row_start(r) * 64
                sps, sb_ = S.psum[(0, 1, 6)[d2]], S.pbuf[(0, 1, 6)[d2]]
                for hh in range(2):
                    lo, hi = hh * 64, (hh + 1) * 64
                    S.mm(sps[lo:hi, :], qT[:, hh, r * 64:(r + 1) * 64], kT[:, hh, ks:ks + 512], True, True, [b_q, b_k], [sb_])

            def na_stage_a(r):
                d2 = r % 3
                rs_ = _row_start(r)
                ks = rs_ * 64
                j0 = rs_ - r + 7
                sps, sb_ = S.psum[(0, 1, 6)[d2]], S.pbuf[(0, 1, 6)[d2]]
                S.tt("dve", sc[d2][:], sps[:, :], nab[:, j0:j0 + 8, :].rearrange("p a k -> p (a k)"), ALU.add,
                     [sb_, b_nab], [b_sc[d2]])
                S.red(st[:, r, 0:1], sc[d2][:], ALU.max, [b_sc[d2]], [b_st[r]], negate=True)
                S.act(pe_[d2][:], sc[d2][:], AF.Exp, [b_sc[d2], b_st[r]], [b_pe[d2], b_st[r]], bias=st[:, r, 0:1], accum=st[:, r, 1:2])

            def na_stage_b(r):
                d2 = r % 3
                S.op("dve", lambda e, o=st[:, r, 2:3], i_=st[:, r, 1:2]: e.reciprocal(out=o, in_=i_), [b_st[r]], [b_st[r]])
                S.ts("pool", pn[d2][:], pe_[d2][:], st[:, r, 2:3], ALU.mult, [b_pe[d2], b_st[r]], [b_pn[d2]], s2=1.0, op1=ALU.mult)
                tps, tb = S.psum[(2, 3, 7)[d2]], S.pbuf[(2, 3, 7)[d2]]
                tpsb = tps.bitcast(BF16)
                for c_ in range(4):
                    S.tr(tpsb[:, c_ * 128:(c_ + 1) * 128], pn[d2][:, c_ * 128:(c_ + 1) * 128], identb[:], [b_pn[d2], b_id], [tb])
                S.copy("act", pt[d2][:], tpsb[:, 0:512].rearrange("p (a k) -> p a k", a=4), [tb], [b_pt[d2]])

            def na_stage_c(r):
                d2 = r % 3
                ks = _row_start(r) * 64
                ob = 4 + (r // 8) % 2
                ops_, opb = S.psum[ob], S.pbuf[ob]
                col = (r % 8) * 64
                for hh in range(2):
                    lo, hi = hh * 64, (hh + 1) * 64
                    for c_ in range(4):
                        tok0 = ks + 128 * c_
                        if tok0 % 128 == 0:
                            vt, bv, ti = ve, b_ve, tok0 // 128
                        else:
                            vt, bv, ti = vo, b_vo, (tok0 - 64) // 128
                        S.mm(ops_[lo:hi, col:col + 64], vt[:, ti, lo:hi], pt[d2][:, c_, lo:hi], c_ == 0, c_ == 3, [bv, b_pt[d2]], [opb])
                if r % 8 == 7:
                    r0 = (r // 8) * 8
                    S.tt("dve", on[:, r0 * 64:(r0 + 8) * 64], ops_[:, :], gn[:, r0 * 64:(r0 + 8) * 64], ALU.mult, [opb, b_gn], [b_on])

            na_stage_qk(0)
            na_stage_qk(1)
            for t in range(64 + 2):
                if t + 2 < 64:
                    na_stage_qk(t + 2)
                if t < 64:
                    na_stage_a(t)
                if 0 <= t - 1 < 64:
                    na_stage_b(t - 1)
                if 0 <= t - 2 < 64:
                    na_stage_c(t - 2)
            store_o(on, b_on, 1)
        S.barrier()

        S.scope = "ret"
        with ExitStack() as es:
            qp = S.sb(es, "r_qp", [128, SEQ], BF16)
            kp = S.sb(es, "r_kp", [128, SEQ], BF16)
            b_qp, b_kp = Buf(), Buf()
            with ExitStack() as es1:
                rin = S.sb(es1, "r_in", [128, SEQ], BF16)
                b_rin = Buf()
                rcos = S.sb(es1, "r_cos", [128, SEQ], F32)
                rsin = S.sb(es1, "r_sin", [128, SEQ], F32)
                b_tab = Buf()
                for h in range(4):
                    S.dma("sp", rcos[:, h * 1024:(h + 1) * 1024], io["rcos"][:, h * 1024:(h + 1) * 1024], (), [b_tab])
                    S.dma("act", rsin[:, h * 1024:(h + 1) * 1024], io["rsin"][:, h * 1024:(h + 1) * 1024], (), [b_tab])
                perm = S.sb(es1, "r_perm", [128, 128], BF16)
                b_perm = Buf()
                S.dma("act", perm[:], io["perm"], (), [b_perm])
                t1 = [S.sb(es1, "r_t1%d" % i, [128, 512], F32) for i in range(2)]
                t2_ = [S.sb(es1, "r_t2%d" % i, [128, 512], F32) for i in range(2)]
                b_t1 = [Buf() for _ in range(2)]
                b_t2 = [Buf() for _ in range(2)]
                for blk, dst, bdst in ((B_RQ, qp, b_qp), (B_RK, kp, b_kp)):
                    for i in range(4):
                        S.dma("sp", rin[:, i * 1024:(i + 1) * 1024], recv1[i, blk, :].rearrange("(p t) -> p t", p=128), (), [b_rin])
                    for c_ in range(8):
                        d2 = c_ % 2
                        sl = slice(c_ * 512, (c_ + 1) * 512)
                        ps, pb = S.psum[d2], S.pbuf[d2]
                        S.mm(ps[:, :], perm[:], rin[:, sl], True, True, [b_perm, b_rin], [pb])
                        S.tt("dve", t1[d2][:], ps[:, :], rsin[:, sl], ALU.mult, [pb, b_tab], [b_t1[d2]])
                        S.tt("pool", t2_[d2][:], rin[:, sl], rcos[:, sl], ALU.mult, [b_rin, b_tab], [b_t2[d2]])
                        S.tt("dve", dst[:, sl], t1[d2][:], t2_[d2][:], ALU.add, [b_t1[d2], b_t2[d2]], [bdst])
            S.barrier()
            lg = S.sb(es, "r_lg", [128, 2], F32)
            lg2 = S.sb(es, "r_lg2", [128, 4], F32)
            b_lg = Buf()
            S.dma("sp", lg[:], io["retlg"], (), [b_lg])
            S.dma("sp", lg2[:], io["retlg2"], (), [b_lg])
            for t_ in (lg, lg2):
                S.act(t_[:], t_[:], AF.Exp, [b_lg], [b_lg], scale=-1.0)
                S.act(t_[:], t_[:], AF.Ln, [b_lg], [b_lg], bias=1.0)
                S.ts("dve", t_[:], t_[:], -1.0, ALU.mult, [b_lg], [b_lg])
            rtab = S.sb(es, "r_tab", [128, 4, 128], F32)
            cexp = S.sb(es, "r_cexp", [128, 2], F32)
            b_rt = Buf()
            S.dma("sp", rtab[:], io["rtab"], (), [b_rt])
            S.dma("sp", cexp[:], io["cexp"], (), [b_rt])
            DT = S.sb(es, "r_DT", [128, 2, 128], F32)
            tmpD = S.sb(es, "r_tmpD", [128, 128], F32)
            b_DT, b_tmpD = Buf(), Buf()
            for h in range(2):
                S.act(DT[:, h, :], rtab[:, 0, :], AF.Exp, [b_rt, b_lg], [b_DT], scale=lg2[:, h:h + 1])
                S.act(tmpD[:], rtab[:, 1, :], AF.Exp, [b_rt, b_lg], [b_tmpD], scale=lg2[:, 2 + h:3 + h])
                S.tt("dve", DT[:, h, :], DT[:, h, :], tmpD[:], ALU.add, [b_DT, b_tmpD], [b_DT])
            qdec = S.sb(es, "r_qdec", [128, 2, 128], F32)
            kd = S.sb(es, "r_kd", [128, 4], F32)
            cdec = S.sb(es, "r_cdec", [128, 2], F32)
            b_dec = Buf()
            for dr in range(2):
                S.act(qdec[:, dr, :], rtab[:, 2 + dr, :], AF.Exp, [b_rt, b_lg], [b_dec], scale=lg[:, dr:dr + 1])
                for h in range(2):
                    c_ = dr * 2 + h
                    S.act(kd[:, c_:c_ + 1], cexp[:, dr:dr + 1], AF.Exp, [b_rt, b_lg], [b_dec], scale=lg2[:, c_:c_ + 1])
            S.act(cdec[:], lg[:], AF.Exp, [b_lg], [b_dec], scale=128.0)
            qf = S.sb(es, "r_qf", [128, SEQ], BF16)
            qb = S.sb(es, "r_qb", [128, SEQ], BF16)
            b_qf, b_qb = Buf(), Buf()
            qp3 = qp[:].rearrange("p (n i) -> p n i", i=128)
            S.tt("dve", qf[:].rearrange("p (n i) -> p n i", i=128), qp3, bc(qdec[:, 0, :], 1, 32), ALU.mult, [b_qp, b_dec], [b_qf])
            S.tt("pool", qb[:].rearrange("p (n i) -> p n i", i=128), qp3, bc(qdec[:, 1, :], 1, 32), ALU.mult, [b_qp, b_dec], [b_qb])
            ktok = S.sb(es, "r_ktok", [128, 32, 128], BF16)
            b_ktok = Buf()
            for g in range(8):
                ps, pb = S.psum[g % 2], S.pbuf[g % 2]
                psb_ = ps.bitcast(BF16)
                for q in range(4):
                    n = g * 4 + q
                    S.tr(psb_[:, q * 128:(q + 1) * 128], kp[:, n * 128:(n + 1) * 128], identb[:], [b_kp, b_id], [pb])
                S.copy("act" if g % 2 else "dve", ktok[:, g * 4:(g + 1) * 4, :], psb_[:, 0:512].rearrange("p (a k) -> p a k", a=4), [pb], [b_ktok])
            vr, b_vr = load_tm(es, "r_v", B_RV)
            rg, b_rg = load_tm(es, "r_g", B_RG)
            S.act(rg[:], rg[:], AF.Silu, [b_rg], [b_rg])
            vf = S.sb(es, "r_vf", [128, 32, 128], BF16)
            vb = S.sb(es, "r_vb", [128, 32, 128], BF16)
            b_vf, b_vb = Buf(), Buf()
            for h in range(2):
                sl = slice(h * 64, (h + 1) * 64)
                S.ts("dve", vf[:, :, sl], vr[:, :, sl], kd[:, h:h + 1], ALU.mult, [b_vr, b_dec], [b_vf])
                S.ts("pool", vb[:, :, sl], vr[:, :, sl], kd[:, 2 + h:3 + h], ALU.mult, [b_vr, b_dec], [b_vb], s2=1.0, op1=ALU.mult)
            sf = S.sb(es, "r_sf", [128, 32, 64], F32)
            sbk = S.sb(es, "r_sb", [128, 32, 64], F32)
            b_sf, b_sbk = Buf(), Buf()
            S.memset("pool", sf[:, 0, :], 0.0, [b_sf])
            S.memset("pool", sbk[:, 31, :], 0.0, [b_sbk])
            for dr in range(2):
                vt, bvt = (vf, b_vf) if dr == 0 else (vb, b_vb)
                st_, bst_ = (sf, b_sf) if dr == 0 else (sbk, b_sbk)
                order = list(range(0, 31)) if dr == 0 else list(range(31, 0, -1))
                for g0 in range(0, 31, 8):
                    grp = order[g0:g0 + 8]
                    bi = 2 + (g0 // 8) % 2
                    ps, pb = S.psum[bi], S.pbuf[bi]
                    for q, n in enumerate(grp):
                        for h in range(2):
                            sl = slice(h * 64, (h + 1) * 64)
                            S.mm(ps[sl, q * 64:(q + 1) * 64], ktok[:, n, sl], vt[:, n, sl], True, True, [b_ktok, bvt], [pb])
                    for q, n in enumerate(grp):
                        nxt = n + 1 if dr == 0 else n - 1
                        S.stt(st_[:, nxt, :], st_[:, n, :], cdec[:, dr:dr + 1], ps[:, q * 64:(q + 1) * 64], ALU.mult, ALU.add,
                              [bst_, pb, b_dec], [bst_])
            sfb = S.sb(es, "r_sfb", [128, 32, 64], BF16)
            sbb = S.sb(es, "r_sbb", [128, 32, 64], BF16)
            b_sfb, b_sbb = Buf(), Buf()
            S.copy("act", sfb[:], sf[:], [b_sf], [b_sfb])
            S.copy("act", sbb[:], sbk[:], [b_sbk], [b_sbb])
            orT = S.sb(es, "r_o", [128, SEQ], BF16)
            b_or = Buf()
            ad = [S.sb(es, "r_ad%d" % i, [128, 4, 128], BF16) for i in range(2)]
            b_ad = [Buf() for _ in range(2)]
            osb = [S.sb(es, "r_osb%d" % i, [128, 4, 2, 64], F32) for i in range(2)]
            osq = [S.sb(es, "r_osq%d" % i, [128, 4, 2, 64], F32) for i in range(2)]
            onb = [S.sb(es, "r_onb%d" % i, [128, 4, 128], BF16) for i in range(2)]
            b_osb = [Buf() for _ in range(2)]
            b_osq = [Buf() for _ in range(2)]
            b_onb = [Buf() for _ in range(2)]
            rst = S.sb(es, "r_rst", [128, 8, 8], F32)
            b_rst = [Buf() for _ in range(8)]
            def ret_stage1(g):
                d2 = g % 2
                for h in range(2):
                    sl = slice(h * 64, (h + 1) * 64)
                    aps, apb = S.psum[h], S.pbuf[h]
                    for cn in range(4):
                        n = g * 4 + cn
                        S.mm(aps[:, cn * 128:(cn + 1) * 128], kp[sl, n * 128:(n + 1) * 128],
                             qp[sl, n * 128:(n + 1) * 128], True, True, [b_kp, b_qp], [apb])
                    S.tt("dve", ad[h][:], aps[:, :].rearrange("p (c i) -> p c i", c=4), bc(DT[:, h, :], 1, 4), ALU.mult,
                         [apb, b_DT], [b_ad[h]])
                    ops_, opb = S.psum[2 + d2 * 2 + h], S.pbuf[2 + d2 * 2 + h]
                    for cn in range(4):
                        n = g * 4 + cn
                        oc = cn * 64
                        S.mm(ops_[:, oc:oc + 64], ad[h][:, cn, :], vr[:, n, sl], True, False, [b_ad[h], b_vr], [opb])
                        if n > 0:
                            S.mm(ops_[:, oc:oc + 64], qf[sl, n * 128:(n + 1) * 128], sfb[sl, n, :], False, n == 31,
                                 [b_qf, b_sfb], [opb])
                        if n < 31:
                            S.mm(ops_[:, oc:oc + 64], qb[sl, n * 128:(n + 1) * 128], sbb[sl, n, :], False, True,
                                 [b_qb, b_sbb], [opb])
                    S.copy("act", osb[d2][:, :, h, :], ops_[:, 0:256].rearrange("p (c e) -> p c e", e=64), [opb], [b_osb[d2]])

            def ret_stage2(g):
                d2 = g % 2
                S.tt("pool", osq[d2][:], osb[d2][:], osb[d2][:], ALU.mult, [b_osb[d2]], [b_osq[d2]])
                S.red(rst[:, g, :], osq[d2][:].rearrange("p c h e -> p (c h) e"), ALU.add, [b_osq[d2]], [b_rst[g]])
                S.ts("dve", rst[:, g, :], rst[:, g, :], 1.0 / 64.0, ALU.mult, [b_rst[g]], [b_rst[g]], s2=EPS, op1=ALU.add)
                S.act(rst[:, g, :], rst[:, g, :], AF.Sqrt, [b_rst[g]], [b_rst[g]])
                S.op("dve", lambda e, o=rst[:, g, :]: e.reciprocal(out=o, in_=o), [b_rst[g]], [b_rst[g]])
                S.tt("dve", osb[d2][:].rearrange("p c h e -> p (c h) e"), osb[d2][:].rearrange("p c h e -> p (c h) e"),
                     bc(rst[:, g, :], 2, 64), ALU.mult, [b_osb[d2], b_rst[g]], [b_osb[d2]])
                S.tt("pool", onb[d2][:], osb[d2][:].rearrange("p c h e -> p c (h e)"), rg[:, g * 4:(g + 1) * 4, :], ALU.mult,
                     [b_osb[d2], b_rg], [b_onb[d2]])
                tps, tb = S.psum[6 + d2], S.pbuf[6 + d2]
                tpsb = tps.bitcast(BF16)
                for cn in range(4):
                    S.tr(tpsb[:, cn * 128:(cn + 1) * 128], onb[d2][:, cn, :], identb[:], [b_onb[d2], b_id], [tb])
                S.copy("act", orT[:, g * 512:(g + 1) * 512], tpsb[:, 0:512], [tb], [b_or])

            ret_stage1(0)
            for g in range(8):
                if g + 1 < 8:
                    ret_stage1(g + 1)
                ret_stage2(g)
            store_o(orT, b_or, 2)
        S.barrier()

    S.barrier()


def emit_C(S, io, last):
    nc = S.nc
    recv2 = io["recv2"]
    with ExitStack() as es:
        oT = S.sb(es, "c_oT", [128, 16, NTOK], BF16)
        b_oT = [Buf() for _ in range(16)]
        wo = S.sb(es, "c_wo", [128, 16, D], BF16)
        b_wo = [Buf() for _ in range(4)]
        for n in range(4):
            S.dma("pool", wo[:, :, n * 512:(n + 1) * 512], io["w_out"][:, n * 512:(n + 1) * 512].rearrange("(k p) c -> p k c", p=128),
                  (), [b_wo[n]])
        gate = S.sb(es, "c_gate", [128, D], F32)
        b_gate = Buf()
        S.dma("sp", gate[:], bc(io["gate"], 0, 128), (), [b_gate])
        if last:
            fg = S.sb(es, "c_fg", [128, D], F32)
            b_fgn = Buf()
            S.dma("sp", fg[:], bc(io["final_g"], 0, 128), (), [b_fgn])
        es1 = es
        if True:
            yc = S.sb(es1, "c_yc", [128, 4, NTOK], BF16)
            b_yc = Buf()
            for j in range(4):
                S.dma("sp", yc[:, j, :], recv2[j, 3, :, :], (), [b_yc])
            cg = S.sb(es1, "c_cg", [128, 4, NTOK], BF16)
            b_cg = Buf()
            for j in range(4):
                S.dma("sp", cg[:, j, :], io["cvg"][j, :, :], (), [b_cg])
            for m in range(3):
                for j in range(4):
                    S.dma(("sp", "act")[j % 2], oT[:, m * 4 + j, :], recv2[j, m, :, :], (), [b_oT[m * 4 + j]])
            S.act(cg[:], cg[:], AF.Silu, [b_cg], [b_cg])
            wpw = S.sb(es1, "c_wpw", [128, 4, 512], BF16)
            b_wpw = Buf()
            S.dma("pool", wpw[:], io["w_pw"].rearrange("(k p) c -> p k c", p=128), (), [b_wpw])
            lngb = S.sb(es1, "c_lngb", [128, 2, 4], F32)
            b_ln = Buf()
            S.dma("sp", lngb[:, 0, :], io["lng"].rearrange("(c p) -> p c", p=128), (), [b_ln], slow=True)
            S.dma("sp", lngb[:, 1, :], io["lnb"].rearrange("(c p) -> p c", p=128), (), [b_ln], slow=True)
            ones = S.sb(es1, "c_ones", [128, 128], BF16)
            b_ones = Buf()
            S.memset("pool", ones[:], 1.0 / 512.0, [b_ones])
            ysq = S.sb(es1, "c_ysq", [128, 4, NTOK], BF16)
            b_ysq = Buf()
            S.act(ysq[:], yc[:], AF.Square, [b_yc], [b_ysq])
            sT = S.sb(es1, "c_sT", [128, 4, NTOK], BF16)
            b_sT = [Buf() for _ in range(2)]
            msq = S.sb(es1, "c_msq", [128, 512], F32)
            rstd = S.sb(es1, "c_rstd", [128, 512], F32)
            b_msq, b_rstd = Buf(), Buf()
            dtmp = [S.sb(es1, "c_d%d" % i, [128, 512], F32) for i in range(2)]
            b_dt = [Buf() for _ in range(2)]
            def ln_half(h):
                hs = slice(h * 512, (h + 1) * 512)
                pm, bpm = S.psum[0], S.pbuf[0]
                pq_, bpq = S.psum[1], S.pbuf[1]
                for j in range(4):
                    S.mm(pm[:, :], ones[:], yc[:, j, hs], j == 0, j == 3, [b_ones, b_yc], [bpm])
                for j in range(4):
                    S.mm(pq_[:, :], ones[:], ysq[:, j, hs], j == 0, j == 3, [b_ones, b_ysq], [bpq])
                S.act(msq[:], pm[:, :], AF.Square, [bpm], [b_msq])
                S.tt("dve", rstd[:], pq_[:, :], msq[:], ALU.subtract, [bpq, b_msq], [b_rstd])
                S.act(rstd[:], rstd[:], AF.Ln, [b_rstd], [b_rstd], bias=EPS)
                S.act(rstd[:], rstd[:], AF.Exp, [b_rstd], [b_rstd], scale=-0.5)
                for j in range(4):
                    d2 = j % 2
                    S.tt("dve", dtmp[d2][:], yc[:, j, hs], pm[:, :], ALU.subtract, [b_yc, bpm], [b_dt[d2]])
                    S.tt("pool", dtmp[d2][:], dtmp[d2][:], rstd[:], ALU.mult, [b_dt[d2], b_rstd], [b_dt[d2]])
                    S.ts("dve", dtmp[d2][:], dtmp[d2][:], lngb[:, 0, j:j + 1], ALU.mult, [b_dt[d2], b_ln], [b_dt[d2]],
                         s2=lngb[:, 1, j:j + 1], op1=ALU.add)
                    S.act(sT[:, j, hs], dtmp[d2][:], AF.Silu, [b_dt[d2]], [b_sT[h]])
            def pw_half(h):
                hs = slice(h * 512, (h + 1) * 512)
                for co in range(4):
                    ps, pb = S.psum[2 + co % 2], S.pbuf[2 + co % 2]
                    for ci in range(4):
                        S.mm(ps[:, :], wpw[:, ci, co * 128:(co + 1) * 128], sT[:, ci, hs], ci == 0, ci == 3, [b_wpw, b_sT[h]], [pb])
                    S.tt("dve", oT[:, 12 + co, hs], ps[:, :], cg[:, co, hs], ALU.mult, [pb, b_cg], [b_oT[12 + co]])
        xt = [S.sb(es, "c_x%d" % i, [128, D], F32) for i in range(2)]
        b_xt = [Buf() for _ in range(2)]
        tmp = [S.sb(es, "c_tmp%d" % i, [128, 512], F32) for i in range(2)]
        b_tmp = [Buf() for _ in range(2)]
        junk = S.sb(es, "c_junk", [128, D], F32)
        b_junk = Buf()
        ss = S.sb(es, "c_ss", [128, 8], F32)
        b_ss = [Buf() for _ in range(8)]
        outs = []
        def outproj_tile(t):
            d2 = t % 2
            S.dma("sp", xt[d2][:], io["x"][t * 128:(t + 1) * 128, :], (), [b_xt[d2]])
            for n in range(4):
                pi = 4 + (t * 4 + n) % 4
                ps, pb = S.psum[pi], S.pbuf[pi]
                for k in range(16):
                    S.mm(ps[:, :], oT[:, k, t * 128:(t + 1) * 128], wo[:, k, n * 512:(n + 1) * 512], k == 0, k == 15,
                         [b_oT[k], b_wo[n]], [pb])
                ti = (t * 4 + n) % 2
                S.tt("dve", tmp[ti][:], ps[:, :], gate[:, n * 512:(n + 1) * 512], ALU.mult, [pb, b_gate], [b_tmp[ti]])
                S.tt("pool", xt[d2][:, n * 512:(n + 1) * 512], xt[d2][:, n * 512:(n + 1) * 512], tmp[ti][:], ALU.add,
                     [b_xt[d2], b_tmp[ti]], [b_xt[d2]])
            if last:
                S.act(junk[:], xt[d2][:], AF.Square, [b_xt[d2]], [b_junk, b_ss[t]], accum=ss[:, t:t + 1])
                S.ts("dve", ss[:, t:t + 1], ss[:, t:t + 1], 1.0 / D, ALU.mult, [b_ss[t]], [b_ss[t]], s2=EPS, op1=ALU.add)
                S.act(ss[:, t:t + 1], ss[:, t:t + 1], AF.Sqrt, [b_ss[t]], [b_ss[t]])
                S.op("dve", lambda e, o=ss[:, t:t + 1]: e.reciprocal(out=o, in_=o), [b_ss[t]], [b_ss[t]])
                S.stt(xt[d2][:], xt[d2][:], ss[:, t:t + 1], fg[:], ALU.mult, ALU.mult, [b_xt[d2], b_ss[t], b_fgn], [b_xt[d2]])
            ob = Buf()
            outs.append(ob)
            S.dma("sp", io["xout"][t * 128:(t + 1) * 128, :], xt[d2][:], [b_xt[d2]], [ob])

        ln_half(0)
        pw_half(0)
        ln_half(1)
        for t in range(4):
            outproj_tile(t)
        pw_half(1)
        for t in range(4, 8):
            outproj_tile(t)
    S.barrier()


_PROGS = {}


CONST_BF16 = ("m1", "m3", "perm")


def _const_io(nc, io, names):
    for n in names:
        io[n] = dram_in(nc, n, CONST_SHAPES[n], BF16 if n in CONST_BF16 else F32)


def prog_mod():
    if "mod" in _PROGS:
        return _PROGS["mod"]
    nc = bass.Bass("TRN2", target_bir_lowering=False)
    io = {"cT": dram_in(nc, "cT", [128, 16, 2]), "wada": dram_in(nc, "wada", [D, 1536]),
          "bada": dram_in(nc, "bada", [2, 1536]), "mod": dram_out(nc, "mod", [2, 1536])}
    with ExitStack() as es:
        S = Sched(nc, es)
        emit_mod(S, io)
        S.run()
    _PROGS["mod"] = nc
    return nc


def prog_A():
    if "A" in _PROGS:
        return _PROGS["A"]
    nc = bass.Bass("TRN2", target_bir_lowering=False)
    io = {"x": dram_in(nc, "x", [NTOK, D]), "shift": dram_in(nc, "shift", [D]), "scale": dram_in(nc, "scale", [D]),
          "norm_g": dram_in(nc, "norm_g", [D]), "w_in": dram_in(nc, "w_in", [D, DIN]), "w_fft": dram_in(nc, "w_fft", [512, 512]),
          "send1": dram_out(nc, "send1", [4, NBLK1, BLK], BF16), "cvg": dram_out(nc, "cvg", [4, 128, NTOK], BF16)}
    _const_io(nc, io, ["ident", "ccsc"])
    with ExitStack() as es:
        S = Sched(nc, es)
        emit_A(S, io)
        S.run()
    _PROGS["A"] = nc
    return nc


B_CONSTS = ["ident", "m1", "m3", "namask", "rcos", "rsin", "perm", "rtab", "cexp"]


def prog_B():
    if "B" in _PROGS:
        return _PROGS["B"]
    nc = bass.Bass("TRN2", target_bir_lowering=False)
    io = {"recv1": dram_in(nc, "recv1", [4, NBLK1, BLK], BF16), "nab": dram_in(nc, "nab", [128, 15, 64]),
          "retlg": dram_in(nc, "retlg", [128, 2]), "retlg2": dram_in(nc, "retlg2", [128, 4]),
          "cw": dram_in(nc, "cw", [128, 31]), "cb": dram_in(nc, "cb", [128, 1]),
          "fftscr": dram_tmp(nc, "fftscr", [128 * 64 * 128], BF16),
          "send2": dram_out(nc, "send2", [4, 4, 128, NTOK], BF16)}
    _const_io(nc, io, B_CONSTS)
    with ExitStack() as es:
        S = Sched(nc, es)
        emit_B(S, io)
        S.run()
    _PROGS["B"] = nc
    return nc


def prog_C(last):
    key = "C%d" % int(last)
    if key in _PROGS:
        return _PROGS[key]
    nc = bass.Bass("TRN2", target_bir_lowering=False)
    io = {"recv2": dram_in(nc, "recv2", [4, 4, 128, NTOK], BF16), "cvg": dram_in(nc, "cvg", [4, 128, NTOK], BF16),
          "x": dram_in(nc, "x", [NTOK, D]), "gate": dram_in(nc, "gate", [D]), "w_out": dram_in(nc, "w_out", [D, D]),
          "w_pw": dram_in(nc, "w_pw", [512, 512]), "lng": dram_in(nc, "lng", [512]), "lnb": dram_in(nc, "lnb", [512]),
          "xout": dram_out(nc, "xout", [NTOK, D])}
    if last:
        io["final_g"] = dram_in(nc, "final_g", [D])
    with ExitStack() as es:
        S = Sched(nc, es)
        emit_C(S, io, last)
        S.run()
    _PROGS[key] = nc
    return nc


def prog_CA():
    if "CA" in _PROGS:
        return _PROGS["CA"]
    nc = bass.Bass("TRN2", target_bir_lowering=False)
    xmid = dram_tmp(nc, "xmid", [NTOK, D])
    ioc = {"recv2": dram_in(nc, "recv2", [4, 4, 128, NTOK], BF16), "cvg": dram_in(nc, "cvg", [4, 128, NTOK], BF16),
           "x": dram_in(nc, "x", [NTOK, D]), "gate": dram_in(nc, "gate", [D]), "w_out": dram_in(nc, "w_out", [D, D]),
           "w_pw": dram_in(nc, "w_pw", [512, 512]), "lng": dram_in(nc, "lng", [512]), "lnb": dram_in(nc, "lnb", [512]),
           "xout": xmid}
    ioa = {"x": xmid, "shift": dram_in(nc, "shift", [D]), "scale": dram_in(nc, "scale", [D]),
           "norm_g": dram_in(nc, "norm_g", [D]), "w_in": dram_in(nc, "w_in", [D, DIN]), "w_fft": dram_in(nc, "w_fft", [512, 512]),
           "send1": dram_out(nc, "send1", [4, NBLK1, BLK], BF16), "cvg": dram_out(nc, "cvg_next", [4, 128, NTOK], BF16)}
    _const_io(nc, ioa, ["ident", "ccsc"])
    xcopy = dram_out(nc, "xout", [NTOK, D])
    with ExitStack() as es:
        S = Sched(nc, es)
        emit_C(S, ioc, False)
        bx = Buf()
        S.dma("sp", xcopy, xmid, (), [bx])
        emit_A(S, ioa)
        S.run()
    _PROGS["CA"] = nc
    return nc


def _run(nc, in_maps):
    res = run_bass_kernel_spmd(nc, in_maps, core_ids=list(range(NCORE)))
    return res.results


def _percore_layer_inputs(l, na_rel_bias, ret_logit_fwd, ret_logit_bwd, conv_w, conv_b):
    col = np.arange(64)
    idx = np.clip(col[None, :] - col[:, None] + 15, 0, 30)
    outs = []
    for j in range(4):
        nab = np.empty((128, 15, 64), np.float32)
        for hh in range(2):
            rb = na_rel_bias[l, 2 * j + hh]
            nab[hh * 64:(hh + 1) * 64] = np.transpose(rb[:, idx], (1, 0, 2))
        lf = ret_logit_fwd[l, 2 * j:2 * j + 2]
        lb = ret_logit_bwd[l, 2 * j:2 * j + 2]
        retlg = np.empty((128, 2), np.float32)
        retlg[0:64, 0], retlg[64:128, 0] = lf[0], lf[1]
        retlg[0:64, 1], retlg[64:128, 1] = lb[0], lb[1]
        retlg2 = np.empty((128, 4), np.float32)
        retlg2[:, 0], retlg2[:, 1], retlg2[:, 2], retlg2[:, 3] = lf[0], lf[1], lb[0], lb[1]
        cw = np.ascontiguousarray(conv_w[l][:, j * 128:(j + 1) * 128].T)
        cb = np.ascontiguousarray(conv_b[l][j * 128:(j + 1) * 128, None])
        outs.append(dict(nab=nab, retlg=retlg, retlg2=retlg2, cw=cw, cb=cb))
    return outs


def kernel(x, c, norm_g, w_ada, b_ada, w_in, w_fft, na_rel_bias, ret_logit_fwd, ret_logit_bwd,
           conv_w, conv_b, conv_ln_g, conv_ln_b, conv_w_pw, w_out, final_g, _debug=None):
    f32 = np.float32
    x = np.asarray(x, f32)
    C = consts()
    cT = np.ascontiguousarray(np.asarray(c, f32).T.reshape(16, 128, NB).transpose(1, 0, 2))
    maps = []
    for k in range(NCORE):
        l, c0 = k // 4, (k % 4) * 1536
        maps.append(dict(cT=cT, wada=np.ascontiguousarray(w_ada[l][:, c0:c0 + 1536]),
                         bada=np.ascontiguousarray(np.broadcast_to(b_ada[l][None, c0:c0 + 1536], (2, 1536)))))
    r = _run(prog_mod(), maps)
    mod = np.stack([np.concatenate([np.asarray(r[l * 4 + q]["mod"]) for q in range(4)], axis=1) for l in range(DEPTH)])
    xs = [np.ascontiguousarray(x[k // 4, (k % 4) * NTOK:(k % 4 + 1) * NTOK]) for k in range(NCORE)]

    def a_inputs(l, k):
        b = k // 4
        return dict(shift=np.ascontiguousarray(mod[l, b, 0:D]), scale=np.ascontiguousarray(mod[l, b, D:2 * D]),
                    norm_g=norm_g[l], w_in=w_in[l], w_fft=w_fft[l], ident=C["ident"], ccsc=C["ccsc"])

    send1 = cvg = None
    for l in range(DEPTH):
        last = (l == DEPTH - 1)
        if l == 0:
            rA = _run(prog_A(), [dict(x=xs[k], **a_inputs(0, k)) for k in range(NCORE)])
            send1 = [np.asarray(rA[k]["send1"]) for k in range(NCORE)]
            cvg = [np.asarray(rA[k]["cvg"]) for k in range(NCORE)]
        pl = _percore_layer_inputs(l, na_rel_bias, ret_logit_fwd, ret_logit_bwd, conv_w, conv_b)
        maps = []
        for k in range(NCORE):
            b, j = k // 4, k % 4
            recv1 = np.stack([send1[b * 4 + i][j] for i in range(4)])
            m = dict(recv1=recv1, **pl[j])
            for n in B_CONSTS:
                m[n] = C[n]
            maps.append(m)
        rB = _run(prog_B(), maps)
        send2 = [np.asarray(rB[k]["send2"]) for k in range(NCORE)]
        maps = []
        for k in range(NCORE):
            b, i = k // 4, k % 4
            recv2 = np.stack([send2[b * 4 + j][i] for j in range(4)])
            m = dict(recv2=recv2, cvg=cvg[k], x=xs[k], gate=np.ascontiguousarray(mod[l, b, 2 * D:3 * D]), w_out=w_out[l],
                     w_pw=conv_w_pw[l], lng=conv_ln_g[l], lnb=conv_ln_b[l])
            if last:
                m["final_g"] = final_g
            else:
                m.update(a_inputs(l + 1, k))
            maps.append(m)
        if last:
            rC = _run(prog_C(True), maps)
        else:
            rC = _run(prog_CA(), maps)
            send1 = [np.asarray(rC[k]["send1"]) for k in range(NCORE)]
            cvg = [np.asarray(rC[k]["cvg_next"]) for k in range(NCORE)]
        xs = [np.asarray(rC[k]["xout"]) for k in range(NCORE)]
    out = np.empty((NB, SEQ, D), f32)
    for k in range(NCORE):
        out[k // 4, (k % 4) * NTOK:(k % 4 + 1) * NTOK] = xs[k]
    return out
```

```python
import numpy as np
import ml_dtypes
from contextlib import ExitStack
import concourse.bass as bass
import concourse.mybir as mybir
from concourse.bass_utils import run_bass_kernel_spmd

F32 = mybir.dt.float32
BF16 = mybir.dt.bfloat16
AF = mybir.ActivationFunctionType
ALU = mybir.AluOpType
AX = mybir.AxisListType

D = 2048
SEQ = 4096
NB = 2
DEPTH = 2
DIN = 6656
NTOK = 1024
NCORE = 8
EPS = 1e-6
NBLK1 = 13
BLK = 128 * 1024
B_P, B_Q, B_FG, B_NQ, B_NK, B_NV, B_NG, B_RQ, B_RK, B_RV, B_RG, B_CA, B_CB = range(13)


PROFILE_SCOPES = False


class Buf:
    __slots__ = ("name", "w", "r")

    def __init__(self, name=""):
        self.name = name
        self.w = {}
        self.r = {}


class Sched:
    COMPUTE = ("pe", "act", "dve", "pool")
    NDMA = 12

    def __init__(self, nc, es):
        self.nc = nc
        self.es = es
        self.eng = {"pe": nc.tensor, "act": nc.scalar, "dve": nc.vector, "pool": nc.gpsimd, "sp": nc.sync}
        self.streams = {e: [] for e in self.eng}
        self.sems = {}
        self.cnt = {}
        self.waited = {e: {} for e in self.eng}
        for e in self.COMPUTE:
            self.sems[e] = es.enter_context(nc.semaphore("s_" + e))
            self.cnt[e] = 0
        self.dq = {}
        for q in ("sp", "pool", "act"):
            lst = []
            for i in range(self.NDMA):
                k = "d_%s_%d" % (q, i)
                self.sems[k] = es.enter_context(nc.semaphore(k))
                self.cnt[k] = 0
                lst.append(k)
            self.dq[q] = [lst, 0]
        self.scope = None
        self.psum = [es.enter_context(nc.psum_tensor("psb%d" % i, [128, 512], F32)) for i in range(8)]
        self.pbuf = [Buf("ps%d" % i) for i in range(8)]

    def sb(self, es, name, shape, dt):
        return es.enter_context(self.nc.sbuf_tensor(name, list(shape), dt))

    def _deps(self, reads, writes, par=False):
        deps = {}

        def add(k, v):
            if deps.get(k, 0) < v:
                deps[k] = v
        for b in reads:
            for k, v in b.w.items():
                add(k, v)
        for b in writes:
            if not (par and not b.r):
                for k, v in b.w.items():
                    add(k, v)
            for k, v in b.r.items():
                add(k, v)
        return deps

    def _emit_waits(self, eng, deps):
        for k, v in deps.items():
            if eng == "pe" and k == "pe":
                continue
            if self.waited[eng].get(k, 0) < v:
                self.waited[eng][k] = v
                self.streams[eng].append(("wait", k, v))

    def _mark(self, tok, reads, writes, par=False):
        k, v = tok
        for b in reads:
            if b.r.get(k, 0) < v:
                b.r[k] = v
        for b in writes:
            if par and not b.r:
                if b.w.get(k, 0) < v:
                    b.w[k] = v
            else:
                b.w = {k: v}
                b.r = {}

    def op(self, eng, fn, reads=(), writes=()):
        self._emit_waits(eng, self._deps(reads, writes))
        self.cnt[eng] += 1
        tok = (eng, self.cnt[eng])
        self.streams[eng].append(("op", fn, eng, 1, self.scope))
        self._mark(tok, reads, writes)
        return tok

    def dma(self, q, out, in_, reads=(), writes=(), slow=False):
        lst, idx = self.dq[q]
        k = lst[idx % len(lst)]
        self.dq[q][1] = idx + 1
        deps = self._deps(reads, writes, par=True)
        if self.cnt[k] > 0:
            deps[k] = max(deps.get(k, 0), self.cnt[k])
        self._emit_waits(q, deps)
        self.cnt[k] += 16
        tok = (k, self.cnt[k])
        if slow:
            self.streams[q].append(("op", lambda e: e.dma_start(out=out, in_=in_, allow_slow_non_contiguous=True), k, 16, self.scope))
        else:
            self.streams[q].append(("op", lambda e: e.dma_start(out=out, in_=in_), k, 16, self.scope))
        self._mark(tok, reads, writes, par=True)
        return tok

    def coll(self, kind, ins, outs, groups, reads=(), writes=()):
        q = "pool"
        lst, idx = self.dq[q]
        k = lst[idx % len(lst)]
        self.dq[q][1] = idx + 1
        deps = self._deps(reads, writes)
        if self.cnt[k] > 0:
            deps[k] = max(deps.get(k, 0), self.cnt[k])
        self._emit_waits(q, deps)
        self.cnt[k] += 16
        tok = (k, self.cnt[k])
        self.streams[q].append(("op", lambda e: e.collective_compute(kind, ALU.bypass, replica_groups=groups, ins=ins, outs=outs), k, 16, self.scope))
        self._mark(tok, reads, writes)
        return tok

    def barrier(self):
        allv = {k: v for k, v in self.cnt.items() if v > 0}
        for e in self.eng:
            self._emit_waits(e, dict(allv))

    def run(self):
        nc = self.nc
        self.barrier()
        with nc.Block() as block:
            def make(ename):
                def body(e):
                    cur, ctx = None, None
                    for it in self.streams[ename]:
                        if it[0] == "wait":
                            e.wait_ge(self.sems[it[1]], it[2])
                        else:
                            _, fn, k, inc, sc = it
                            if PROFILE_SCOPES and sc != cur:
                                if ctx is not None:
                                    ctx.__exit__(None, None, None)
                                    ctx = None
                                if sc is not None:
                                    ctx = nc.named_scope(sc)
                                    ctx.__enter__()
                                cur = sc
                            fn(e).then_inc(self.sems[k], inc)
                    if ctx is not None:
                        ctx.__exit__(None, None, None)
                return body
            block.sync(make("sp"))
            block.tensor(make("pe"))
            block.scalar(make("act"))
            block.vector(make("dve"))
            block.gpsimd(make("pool"))

    def mm(self, out, lhsT, rhs, start, stop, r, w):
        return self.op("pe", lambda e: e.matmul(out, lhsT=lhsT, rhs=rhs, start=start, stop=stop), r, w)

    def tr(self, out, in_, ident, r, w):
        return self.op("pe", lambda e: e.transpose(out=out, in_=in_, identity=ident), r, w)

    def act(self, out, in_, func, r, w, bias=None, scale=None, accum=None):
        kw = {}
        if bias is not None:
            kw["bias"] = bias
        if scale is not None:
            kw["scale"] = scale
        if accum is not None:
            kw["accum_out"] = accum
        return self.op("act", lambda e: e.activation(out=out, in_=in_, func=func, **kw), r, w)

    def tt(self, eng, out, in0, in1, op, r, w):
        return self.op(eng, lambda e: e.tensor_tensor(out=out, in0=in0, in1=in1, op=op), r, w)

    def ts(self, eng, out, in0, s1, op0, r, w, s2=None, op1=None):
        if op1 is None:
            return self.op(eng, lambda e: e.tensor_scalar(out=out, in0=in0, scalar1=s1, scalar2=None, op0=op0), r, w)
        return self.op(eng, lambda e: e.tensor_scalar(out=out, in0=in0, scalar1=s1, scalar2=s2, op0=op0, op1=op1), r, w)

    def stt(self, out, in0, scalar, in1, op0, op1, r, w):
        return self.op("dve", lambda e: e.scalar_tensor_tensor(out=out, in0=in0, scalar=scalar, in1=in1, op0=op0, op1=op1), r, w)

    def copy(self, eng, out, in_, r, w):
        if eng == "act":
            return self.op("act", lambda e: e.copy(out=out, in_=in_), r, w)
        return self.op(eng, lambda e: e.tensor_copy(out=out, in_=in_), r, w)

    def red(self, out, in_, op, r, w, negate=False):
        return self.op("dve", lambda e: e.tensor_reduce(out=out, in_=in_, axis=AX.X, op=op, negate=negate), r, w)

    def memset(self, eng, ap, val, w):
        return self.op(eng, lambda e: e.memset(ap, val), (), w)


def bc(ap, pos, count):
    lst = [list(x) for x in ap.ap]
    lst.insert(pos, [0, count])
    return bass.AP(ap.tensor, ap.offset, lst)


def dram_in(nc, name, shape, dt=F32):
    return nc.dram_tensor(name, list(shape), dt, kind="ExternalInput").ap()


def dram_out(nc, name, shape, dt=F32):
    return nc.dram_tensor(name, list(shape), dt, kind="ExternalOutput").ap()


def dram_tmp(nc, name, shape, dt=F32):
    return nc.dram_tensor(name, list(shape), dt, kind="Internal").ap()


_CONST = None


def consts():
    global _CONST
    if _CONST is not None:
        return _CONST
    c = {}
    c["ident"] = np.eye(128, dtype=np.float32)
    i128 = np.arange(128, dtype=np.float64)
    ang = 2 * np.pi * np.outer(i128, i128) / 128.0
    ccsc = np.stack([np.cos(ang), np.sin(ang)], axis=1) / np.sqrt(128.0)
    c["ccsc"] = ccsc.astype(np.float32)
    s1 = np.arange(64)[:, None, None]
    s2 = np.arange(64)[None, :, None]
    k1 = np.arange(64)[None, None, :]
    th = 2 * np.pi * (k1 * s1 / 64.0 + k1 * s2 / 4096.0)
    m1 = np.zeros((128, 64, 128), np.float64)
    m1[0:64, :, 0:64] = np.cos(th)
    m1[0:64, :, 64:128] = np.sin(th)
    m1[64:128, :, 0:64] = -np.sin(th)
    m1[64:128, :, 64:128] = np.cos(th)
    c["m1"] = m1.astype(ml_dtypes.bfloat16)
    ph = 2 * np.pi * np.outer(np.arange(64), np.arange(64)) / 64.0
    m3 = np.concatenate([np.cos(ph), -np.sin(ph)], axis=0) / 64.0
    c["m3"] = m3.astype(ml_dtypes.bfloat16)
    col = np.arange(64)
    cs = np.clip(col - 8, 0, 48)
    rel = col[None, :] - cs[:, None]
    m = np.where((rel >= 0) & (rel < 16), 0.0, -30000.0).astype(np.float32)
    c["namask"] = np.concatenate([m, m], axis=0)
    half = 32
    inv = 10000.0 ** (-np.arange(half, dtype=np.float32) / half)
    angr = np.arange(SEQ, dtype=np.float32)[:, None] * inv[None, :]
    cosr = np.cos(angr.astype(np.float64)).T
    sinr = np.sin(angr.astype(np.float64)).T
    cos64 = np.concatenate([cosr, cosr], axis=0)
    sin64 = np.concatenate([-sinr, sinr], axis=0)
    c["rcos"] = np.concatenate([cos64, cos64], axis=0).astype(np.float32)
    c["rsin"] = np.concatenate([sin64, sin64], axis=0).astype(np.float32)
    pm = np.zeros((128, 128), np.float32)
    for mm_ in range(128):
        pm[mm_ ^ 32, mm_] = 1.0
    c["perm"] = pm.astype(ml_dtypes.bfloat16)
    j = np.arange(128)[:, None]
    i = np.arange(128)[None, :]
    BIG = 1.0e7
    distf = np.where(i >= j, (i - j), BIG).astype(np.float32)
    distb = np.where(j > i, (j - i), BIG).astype(np.float32)
    io1 = np.broadcast_to(np.arange(1, 129, dtype=np.float32)[None, :], (128, 128))
    io2 = np.broadcast_to((128 - np.arange(128, dtype=np.float32))[None, :], (128, 128))
    c["rtab"] = np.ascontiguousarray(np.stack([distf, distb, io1, io2], axis=1)).astype(np.float32)
    ce = np.stack([127 - np.arange(128), np.arange(128)], axis=1).astype(np.float32)
    c["cexp"] = ce
    _CONST = c
    return c


CONST_SHAPES = {"ident": [128, 128], "ccsc": [128, 2, 128], "m1": [128, 64, 128], "m3": [128, 64],
                "namask": [128, 64], "rcos": [128, SEQ], "rsin": [128, SEQ], "perm": [128, 128],
                "rtab": [128, 4, 128], "cexp": [128, 2]}


def emit_mod(S, io):
    nc = S.nc
    with ExitStack() as es:
        cT = S.sb(es, "m_cT", [128, 16, 2], F32)
        bcT = Buf()
        ba = S.sb(es, "m_ba", [2, 1536], F32)
        bba = Buf()
        res = S.sb(es, "m_res", [2, 1536], F32)
        bres = Buf()
        wt = [S.sb(es, "m_w%d" % i, [128, 16, 512], F32) for i in range(3)]
        bw = [Buf() for _ in range(3)]
        S.dma("sp", cT[:], io["cT"], (), [bcT])
        S.dma("sp", ba[:], io["bada"], (), [bba])
        for n in range(3):
            S.dma("sp", wt[n][:], io["wada"][:, n * 512:(n + 1) * 512].rearrange("(k p) c -> p k c", p=128), (), [bw[n]])
        S.act(cT[:], cT[:], AF.Silu, [bcT], [bcT])
        cR = S.sb(es, "m_cR", [128, 16, 64, 2], F32)
        bcR = Buf()
        S.copy("dve", cR[:], bc(cT[:], 2, 64), [bcT], [bcR])
        for n in range(3):
            ps = S.psum[n]
            pb = S.pbuf[n]
            for k in range(16):
                S.mm(ps[:, :], cR[:, k, :, :].rearrange("p r b -> p (r b)"), wt[n][:, k, :], k == 0, k == 15, [bcR, bw[n]], [pb])
            S.tt("dve", res[:, n * 512:(n + 1) * 512], ps[0:2, :], ba[:, n * 512:(n + 1) * 512], ALU.add, [pb, bba], [bres])
        S.dma("sp", io["mod"], res[:], [bres], [Buf()])
    S.barrier()


def emit_A(S, io, layer_tag=""):
    nc = S.nc
    with ExitStack() as es:
        ident = S.sb(es, "a_ident", [128, 128], F32)
        b_ident = Buf()
        S.dma("sp", ident[:], io["ident"], (), [b_ident])
        gT = S.sb(es, "a_gT", [128, 16], F32)
        scT = S.sb(es, "a_scT", [128, 16], F32)
        shT = S.sb(es, "a_shT", [128, 16], F32)
        gsT = S.sb(es, "a_gsT", [128, 16], F32)
        b_mod = Buf()
        b_gs = Buf()
        S.dma("sp", gT[:], io["norm_g"].rearrange("(c p) -> p c", p=128), (), [b_mod], slow=True)
        S.dma("sp", scT[:], io["scale"].rearrange("(c p) -> p c", p=128), (), [b_mod], slow=True)
        S.dma("sp", shT[:], io["shift"].rearrange("(c p) -> p c", p=128), (), [b_mod], slow=True)
        S.stt(gsT[:], scT[:], 1.0, gT[:], ALU.add, ALU.mult, [b_mod], [b_gs])

        ccsc = S.sb(es, "a_ccsc", [128, 2, 128], F32)
        b_cc = Buf()
        S.dma("sp", ccsc[:], io["ccsc"], (), [b_cc])
        wf = S.sb(es, "a_wf", [128, 4, 512], F32)
        b_wf = Buf()
        S.dma("sp", wf[:], io["w_fft"].rearrange("(g p) c -> p g c", p=128), (), [b_wf])
        mcs = S.sb(es, "a_mcs", [128, 2, 4, 512], BF16)
        b_mcs = Buf()
        for cs in range(2):
            for g in range(4):
                pi = (cs * 4 + g) % 4
                ps, pb = S.psum[pi], S.pbuf[pi]
                S.mm(ps[:, :], ccsc[:, cs, :], wf[:, g, :], True, True, [b_cc, b_wf], [pb])
                S.copy("act" if g % 2 else "dve", mcs[:, cs, g, :], ps[:, :], [pb], [b_mcs])

        hT = S.sb(es, "a_hT", [128, 16, NTOK], BF16)
        b_hT = [Buf() for _ in range(2)]
        xt = [S.sb(es, "a_x%d" % i, [128, D], F32) for i in range(4)]
        b_xt = [Buf() for _ in range(4)]
        junk = S.sb(es, "a_junk", [128, D], F32)
        b_junk = Buf()
        ss = S.sb(es, "a_ss", [128, 8], F32)
        rs = S.sb(es, "a_rs", [128, 8], F32)
        b_ss = [Buf() for _ in range(8)]
        for grp in range(2):
            for tt_ in range(4):
                t = grp * 4 + tt_
                S.dma("sp", xt[tt_][:], io["x"][t * 128:(t + 1) * 128, :], (), [b_xt[tt_]])
                S.act(junk[:], xt[tt_][:], AF.Square, [b_xt[tt_]], [b_junk, b_ss[t]], accum=ss[:, t:t + 1])
                S.ts("dve", rs[:, t:t + 1], ss[:, t:t + 1], 1.0 / D, ALU.mult, [b_ss[t]], [b_ss[t]], s2=EPS, op1=ALU.add)
                S.act(rs[:, t:t + 1], rs[:, t:t + 1], AF.Sqrt, [b_ss[t]], [b_ss[t]])
                S.op("dve", lambda e, o=rs[:, t:t + 1]: e.reciprocal(out=o, in_=o), [b_ss[t]], [b_ss[t]])
                S.ts("dve", xt[tt_][:], xt[tt_][:], rs[:, t:t + 1], ALU.mult, [b_xt[tt_], b_ss[t]], [b_xt[tt_]])
            for k in range(16):
                pi = 4 + (k % 4)
                ps, pb = S.psum[pi], S.pbuf[pi]
                for tt_ in range(4):
                    S.tr(ps[:, tt_ * 128:(tt_ + 1) * 128], xt[tt_][:, k * 128:(k + 1) * 128], ident[:], [b_xt[tt_], b_ident], [pb])
                S.ts("dve", hT[:, k, grp * 512:(grp + 1) * 512], ps[:, :], gsT[:, k:k + 1], ALU.mult,
                     [pb, b_gs, b_mod], [b_hT[grp]], s2=shT[:, k:k + 1], op1=ALU.add)

        wb = [S.sb(es, "a_w%d" % i, [128, 16, 512], BF16) for i in range(3)]
        b_wb = [Buf() for _ in range(3)]
        st_fm = [S.sb(es, "a_sfm%d" % i, [128, NTOK], BF16) for i in range(2)]
        b_sfm = [Buf() for _ in range(2)]
        st_tm = [S.sb(es, "a_stm%d" % i, [128, 8, 512], BF16) for i in range(2)]
        b_stm = [Buf() for _ in range(2)]
        fxT = S.sb(es, "a_fxT", [128, 4, NTOK], BF16)
        b_fx = Buf()
        send1 = io["send1"]

        def wload(p):
            slot = p % 3
            S.dma("pool", wb[slot][:], io["w_in"][:, p * 512:(p + 1) * 512].rearrange("(k p) c -> p k c", p=128),
                  (), [b_wb[slot]])

        cnt = {"fm": 0, "tm": 0, "ev": 0, "ps": 0}

        def evac(out, in_, r, w, scale=None):
            cnt["ev"] += 1
            if cnt["ev"] % 2:
                if scale is None:
                    S.copy("act", out, in_, r, w)
                else:
                    S.act(out, in_, AF.Copy, r, w, scale=scale)
            else:
                if scale is None:
                    S.copy("dve", out, in_, r, w)
                else:
                    S.ts("dve", out, in_, scale, ALU.mult, r, w)

        def nextps():
            cnt["ps"] += 1
            i = cnt["ps"] % 4
            return S.psum[i], S.pbuf[i]

        def fm_piece(p, dst_fn, scale=None):
            slot = p % 3
            for j in range(4):
                dram_ap, sb_ap = dst_fn(j)
                if sb_ap is None:
                    si = cnt["fm"] % 2
                    cnt["fm"] += 1
                    stage, bst = st_fm[si], b_sfm[si]
                    tgt = stage
                else:
                    tgt, bst = sb_ap, b_fx
                for h in range(2):
                    ps, pb = nextps()
                    for k in range(16):
                        S.mm(ps[:, :], wb[slot][:, k, j * 128:(j + 1) * 128], hT[:, k, h * 512:(h + 1) * 512],
                             k == 0, k == 15, [b_wb[slot], b_hT[h]], [pb])
                    if sb_ap is None:
                        evac(tgt[:, h * 512:(h + 1) * 512], ps[:, :], [pb], [bst], scale)
                    else:
                        evac(tgt[:, j, h * 512:(h + 1) * 512], ps[:, :], [pb], [bst], scale)
                if dram_ap is not None:
                    S.dma("sp", dram_ap, tgt[:], [bst], [Buf()])

        def tm_from(lhs_fn, nk, rhs_fn, rbufs, blk):
            si = cnt["tm"] % 2
            cnt["tm"] += 1
            stage, bst = st_tm[si], b_stm[si]
            for t in range(8):
                ps, pb = nextps()
                for k in range(nk):
                    S.mm(ps[:, :], lhs_fn(k, t), rhs_fn(k), k == 0, k == nk - 1, rbufs(t), [pb])
                evac(stage[:, t, :], ps[:, :], [pb], [bst])
            for j in range(4):
                dst = send1[j, blk, :].rearrange("(t p c) -> p t c", p=128, c=128)
                S.dma("sp", dst, stage[:, :, j * 128:(j + 1) * 128], [bst], [Buf()])

        def send_fm(blk):
            return lambda j: (send1[j, blk, :].rearrange("(p t) -> p t", p=128), None)

        wload(0)
        wload(1)
        for p in range(13):
            if p + 2 < 13:
                wload(p + 2)
            slot = p % 3
            if p == 0:
                fm_piece(0, lambda j: (None, fxT))
                for cs, blk in ((0, B_P), (1, B_Q)):
                    tm_from(lambda k, t: fxT[:, k, t * 128:(t + 1) * 128], 4,
                            lambda k, cs=cs: mcs[:, cs, k, :], lambda t: [b_fx, b_mcs], blk)
            elif p == 1:
                fm_piece(1, send_fm(B_FG))
            elif p == 2:
                fm_piece(2, send_fm(B_NQ), scale=0.125)
            elif p == 3:
                fm_piece(3, send_fm(B_NK))
            elif p in (4, 8, 9):
                blk = {4: B_NV, 8: B_RV, 9: B_RG}[p]
                tm_from(lambda k, t: hT[:, k, t * 128:(t + 1) * 128], 16,
                        lambda k, slot=slot: wb[slot][:, k, :], lambda t, slot=slot: [b_hT[t // 4], b_wb[slot]], blk)
            elif p == 5:
                fm_piece(5, send_fm(B_NG))
            elif p == 6:
                fm_piece(6, send_fm(B_RQ), scale=0.125)
            elif p == 7:
                fm_piece(7, send_fm(B_RK))
            elif p == 10:
                fm_piece(10, send_fm(B_CA))
            elif p == 11:
                fm_piece(11, send_fm(B_CB))
            elif p == 12:
                fm_piece(12, lambda j: (io["cvg"][j, :, :], None))
    S.barrier()


def _row_start(r):
    return min(max(r - 4, 0), 56)


def emit_B(S, io):
    nc = S.nc
    recv1 = io["recv1"]
    send2 = io["send2"]

    def load_fm(es, name, blk):
        t = S.sb(es, name, [128, SEQ], BF16)
        b = Buf()
        for i in range(4):
            S.dma(("sp", "act")[i % 2], t[:, i * 1024:(i + 1) * 1024], recv1[i, blk, :].rearrange("(p t) -> p t", p=128), (), [b])
        return t, b

    def load_tm(es, name, blk):
        t = S.sb(es, name, [128, 32, 128], BF16)
        b = Buf()
        for i in range(4):
            S.dma(("sp", "act")[i % 2], t[:, i * 8:(i + 1) * 8, :], recv1[i, blk, :].rearrange("(t p c) -> p t c", p=128, c=128), (), [b])
        return t, b

    def store_o(t, b, blk):
        for i in range(4):
            S.dma("sp", send2[i, blk, :, :], t[:, i * 1024:(i + 1) * 1024], [b], [Buf()])

    with ExitStack() as es0:
        identf = S.sb(es0, "b_identf", [128, 128], F32)
        identb = S.sb(es0, "b_identb", [128, 128], BF16)
        b_id = Buf()
        S.dma("sp", identf[:], io["ident"], (), [b_id])
        S.copy("dve", identb[:], identf[:], [b_id], [b_id])

        qT = S.sb(es0, "n_q", [64, 2, SEQ], BF16)
        kT = S.sb(es0, "n_k", [64, 2, SEQ], BF16)
        ng = S.sb(es0, "n_g", [128, SEQ], BF16)
        ve = S.sb(es0, "n_ve", [128, 32, 128], BF16)
        vo = S.sb(es0, "n_vo", [128, 32, 128], BF16)
        nab = S.sb(es0, "n_bias", [128, 15, 64], F32)
        msk = S.sb(es0, "n_mask", [128, 64], F32)
        b_q, b_k, b_ng, b_ve, b_vo, b_nab = Buf(), Buf(), Buf(), Buf(), Buf(), Buf()

        S.scope = "fft"
        with ExitStack() as es:
            pq = S.sb(es, "f_pq", [128, 64, 128], BF16)
            b_pq = Buf()
            for i in range(4):
                for c_, blk in ((0, B_P), (1, B_Q)):
                    S.dma("sp", pq[c_ * 64 + i * 16:c_ * 64 + (i + 1) * 16, :, :],
                          recv1[i, blk, :].rearrange("(a s c) -> a s c", a=16, c=128), (), [b_pq])
            m1 = S.sb(es, "f_m1", [128, 64, 128], BF16)
            b_m1 = Buf()
            for h in range(4):
                S.dma("act", m1[:, h * 16:(h + 1) * 16, :], io["m1"][:, h * 16:(h + 1) * 16, :], (), [b_m1])
            m3 = S.sb(es, "f_m3", [128, 64], BF16)
            b_m3 = Buf()
            S.dma("act", m3[:], io["m3"], (), [b_m3])
            fg, b_fg = load_fm(es, "f_fg", B_FG)
            gf = S.sb(es, "f_gate", [128, SEQ], BF16)
            b_gf = Buf()
            S.act(gf[:], fg[:], AF.Silu, [b_fg], [b_gf])
            S.scope = "conv"
            ca, b_ca = load_fm(es, "c_a", B_CA)
            cbt, b_cb = load_fm(es, "c_b", B_CB)
            S.act(cbt[:], cbt[:], AF.Sigmoid, [b_cb], [b_cb])
            up = S.sb(es, "c_up", [128, SEQ + 32], BF16)
            b_up = Buf()
            S.memset("pool", up[:, 0:15], 0.0, [b_up])
            S.memset("pool", up[:, 15 + SEQ:SEQ + 32], 0.0, [b_up])
            S.tt("dve", up[:, 15:15 + SEQ], ca[:], cbt[:], ALU.mult, [b_ca, b_cb], [b_up])
            cw = S.sb(es, "c_w", [128, 31], F32)
            cbias = S.sb(es, "c_bias", [128, 1], F32)
            b_cw = Buf()
            S.dma("sp", cw[:], io["cw"], (), [b_cw])
            S.dma("sp", cbias[:], io["cb"], (), [b_cw])
            dg = S.sb(es, "c_diag", [128, 31, 128], BF16)
            b_dg = Buf()
            for k in range(31):
                S.ts("pool", dg[:, k, :], identf[:], cw[:, k:k + 1], ALU.mult, [b_id, b_cw], [b_dg], s2=1.0, op1=ALU.mult)
            S.scope = "na"
            for i in range(4):
                for hh in range(2):
                    S.dma("act", qT[:, hh, i * 1024:(i + 1) * 1024],
                          recv1[i, B_NQ, hh * 65536:(hh + 1) * 65536].rearrange("(p t) -> p t", p=64), (), [b_q])
                    S.dma("act", kT[:, hh, i * 1024:(i + 1) * 1024],
                          recv1[i, B_NK, hh * 65536:(hh + 1) * 65536].rearrange("(p t) -> p t", p=64), (), [b_k])
            for i in range(4):
                S.dma("act", ng[:, i * 1024:(i + 1) * 1024], recv1[i, B_NG, :].rearrange("(p t) -> p t", p=128), (), [b_ng])
                S.dma("act", ve[:, i * 8:(i + 1) * 8, :], recv1[i, B_NV, :].rearrange("(t p c) -> p t c", p=128, c=128), (), [b_ve])
            for i in range(4):
                flat = recv1[i, B_NV, :]
                S.dma("act", vo[:, i * 8:i * 8 + 7, :],
                      flat[64 * 128:(64 + 7 * 128) * 128].rearrange("(t p c) -> p t c", p=128, c=128), (), [b_vo])
                if i < 3:
                    S.dma("act", vo[0:64, i * 8 + 7, :], flat[960 * 128:1024 * 128].rearrange("(p c) -> p c", c=128), (), [b_vo])
                    S.dma("act", vo[64:128, i * 8 + 7, :], recv1[i + 1, B_NV, 0:64 * 128].rearrange("(p c) -> p c", c=128), (), [b_vo])
            S.dma("act", nab[:], io["nab"], (), [b_nab])
            S.dma("act", msk[:], io["namask"], (), [b_nab])
            S.scope = "fft"
            tsb = S.sb(es, "f_t", [128, 64, 128], BF16)
            b_t = Buf()
            for g in range(16):
                ps, pb = S.psum[g % 2], S.pbuf[g % 2]
                for q in range(4):
                    s2 = g * 4 + q
                    S.mm(ps[:, q * 128:(q + 1) * 128], m1[:, s2, :], pq[:, s2, :], True, True, [b_m1, b_pq], [pb])
                S.copy("act" if g % 2 else "dve", tsb[:, g * 4:(g + 1) * 4, :],
                       ps[:, :].rearrange("p (a c) -> p a c", a=4), [pb], [b_t])
            scr = io["fftscr"]
            b_scr = Buf()
            S.dma("sp", scr.rearrange("(p s c) -> p s c", p=128, c=128), tsb[:], [b_t], [b_scr])
            t2 = S.sb(es, "f_t2", [128, 64, 128], BF16)
            b_t2 = Buf()
            src = scr.rearrange("(c k s h) -> c s k h", c=2, k=64, s=64)
            for c_ in range(2):
                S.dma("sp", t2[c_ * 64:(c_ + 1) * 64, :, :], src[c_], [b_scr], [b_t2])
            S.scope = "conv"
            yc = S.sb(es, "c_y", [128, SEQ], BF16)
            b_yc = Buf()
            for t in range(8):
                ps, pb = S.psum[t % 2], S.pbuf[t % 2]
                for k in range(31):
                    S.mm(ps[:, :], dg[:, k, :], up[:, t * 512 + k:t * 512 + k + 512], k == 0, k == 30, [b_dg, b_up], [pb])
                S.ts("dve", yc[:, t * 512:(t + 1) * 512], ps[:, :], cbias[:, 0:1], ALU.add, [pb, b_cw], [b_yc])
            store_o(yc, b_yc, 3)
            S.scope = "fft"
            of = S.sb(es, "f_o", [128, SEQ], BF16)
            b_of = Buf()
            ofv = of[:].rearrange("p (k2 k1) -> p k1 k2", k1=64)
            gfv = gf[:].rearrange("p (k2 k1) -> p k1 k2", k1=64)
            for g in range(8):
                ps, pb = S.psum[2 + g % 2], S.pbuf[2 + g % 2]
                for q in range(8):
                    k1 = g * 8 + q
                    S.mm(ps[:, q * 64:(q + 1) * 64], t2[:, k1, :], m3[:], True, True, [b_t2, b_m3], [pb])
                S.tt("dve", ofv[:, g * 8:(g + 1) * 8, :], ps[:, :].rearrange("p (a k) -> p a k", a=8),
                     gfv[:, g * 8:(g + 1) * 8, :], ALU.mult, [pb, b_gf], [b_of])
            store_o(of, b_of, 0)
        S.barrier()

        S.scope = "na"
        with ExitStack() as es:
            gn = S.sb(es, "n_gate", [128, SEQ], BF16)
            b_gn = Buf()
            S.act(gn[:], ng[:], AF.Silu, [b_ng], [b_gn])
            S.tt("dve", nab[:], nab[:], bc(msk[:], 1, 15), ALU.add, [b_nab], [b_nab])
            on = S.sb(es, "n_o", [128, SEQ], BF16)
            b_on = Buf()
            sc = [S.sb(es, "n_sc%d" % i, [128, 512], F32) for i in range(3)]
            b_sc = [Buf() for _ in range(3)]
            pe_ = [S.sb(es, "n_p%d" % i, [128, 512], BF16) for i in range(3)]
            b_pe = [Buf() for _ in range(3)]
            pn = [S.sb(es, "n_pn%d" % i, [128, 512], BF16) for i in range(3)]
            b_pn = [Buf() for _ in range(3)]
            pt = [S.sb(es, "n_pt%d" % i, [128, 4, 128], BF16) for i in range(3)]
            b_pt = [Buf() for _ in range(3)]
            st = S.sb(es, "n_st", [128, 64, 4], F32)
            b_st = [Buf() for _ in range(64)]
            def na_stage_qk(r):
                d2 = r % 3
                ks = _row_start(r) * 64
                sps, sb_ = S.psum[(0, 1, 6)[d2]], S.pbuf[(0, 1, 6)[d2]]
                for hh in range(2):
                    lo, hi = hh * 64, (hh + 1) * 64
                    S.mm(sps[lo:hi, :], qT[:, hh, r * 64:(r + 1) * 64], kT[:, hh, ks:ks + 512], True, True, [b_q, b_k], [sb_])

            def na_stage_a(r):
                d2 = r % 3
                rs_ = _row_start(r)
                ks = rs_ * 64
                j0 = rs_ - r + 7
                sps, sb_ = S.psum[(0, 1, 6)[d2]], S.pbuf[(0, 1, 6)[d2]]
                S.tt("dve", sc[d2][:], sps[:, :], nab[:, j0:j0 + 8, :].rearrange("p a k -> p (a k)"), ALU.add,
                     [sb_, b_nab], [b_sc[d2]])
                S.red(st[:, r, 0:1], sc[d2][:], ALU.max, [b_sc[d2]], [b_st[r]], negate=True)
                S.act(pe_[d2][:], sc[d2][:], AF.Exp, [b_sc[d2], b_st[r]], [b_pe[d2], b_st[r]], bias=st[:, r, 0:1], accum=st[:, r, 1:2])

            def na_stage_b(r):
                d2 = r % 3
                S.op("dve", lambda e, o=st[:, r, 2:3], i_=st[:, r, 1:2]: e.reciprocal(out=o, in_=i_), [b_st[r]], [b_st[r]])
                S.ts("pool", pn[d2][:], pe_[d2][:], st[:, r, 2:3], ALU.mult, [b_pe[d2], b_st[r]], [b_pn[d2]], s2=1.0, op1=ALU.mult)
                tps, tb = S.psum[(2, 3, 7)[d2]], S.pbuf[(2, 3, 7)[d2]]
                tpsb = tps.bitcast(BF16)
                for c_ in range(4):
                    S.tr(tpsb[:, c_ * 128:(c_ + 1) * 128], pn[d2][:, c_ * 128:(c_ + 1) * 128], identb[:], [b_pn[d2], b_id], [tb])
                S.copy("act", pt[d2][:], tpsb[:, 0:512].rearrange("p (a k) -> p a k", a=4), [tb], [b_pt[d2]])

            def na_stage_c(r):
                d2 = r % 3
                ks = _row_start(r) * 64
                ob = 4 + (r // 8) % 2
                ops_, opb = S.psum[ob], S.pbuf[ob]
                col = (r % 8) * 64
                for hh in range(2):
                    lo, hi = hh * 64, (hh + 1) * 64
                    for c_ in range(4):
                        tok0 = ks + 128 * c_
                        if tok0 % 128 == 0:
                            vt, bv, ti = ve, b_ve, tok0 // 128
                        else:
                            vt, bv, ti = vo, b_vo, (tok0 - 64) // 128
                        S.mm(ops_[lo:hi, col:col + 64], vt[:, ti, lo:hi], pt[d2][:, c_, lo:hi], c_ == 0, c_ == 3, [bv, b_pt[d2]], [opb])
                if r % 8 == 7:
                    r0 = (r // 8) * 8
                    S.tt("dve", on[:, r0 * 64:(r0 + 8) * 64], ops_[:, :], gn[:, r0 * 64:(r0 + 8) * 64], ALU.mult, [opb, b_gn], [b_on])

            na_stage_qk(0)
            na_stage_qk(1)
            for t in range(64 + 2):
                if t + 2 < 64:
                    na_stage_qk(t + 2)
                if t < 64:
                    na_stage_a(t)
                if 0 <= t - 1 < 64:
                    na_stage_b(t - 1)
                if 0 <= t - 2 < 64:
                    na_stage_c(t - 2)
            store_o(on, b_on, 1)
        S.barrier()

        S.scope = "ret"
        with ExitStack() as es:
            qp = S.sb(es, "r_qp", [128, SEQ], BF16)
            kp = S.sb(es, "r_kp", [128, SEQ], BF16)
            b_qp, b_kp = Buf(), Buf()
            with ExitStack() as es1:
                rin = S.sb(es1, "r_in", [128, SEQ], BF16)
                b_rin = Buf()
                rcos = S.sb(es1, "r_cos", [128, SEQ], F32)
                rsin = S.sb(es1, "r_sin", [128, SEQ], F32)
                b_tab = Buf()
                for h in range(4):
                    S.dma("sp", rcos[:, h * 1024:(h + 1) * 1024], io["rcos"][:, h * 1024:(h + 1) * 1024], (), [b_tab])
                    S.dma("act", rsin[:, h * 1024:(h + 1) * 1024], io["rsin"][:, h * 1024:(h + 1) * 1024], (), [b_tab])
                perm = S.sb(es1, "r_perm", [128, 128], BF16)
                b_perm = Buf()
                S.dma("act", perm[:], io["perm"], (), [b_perm])
                t1 = [S.sb(es1, "r_t1%d" % i, [128, 512], F32) for i in range(2)]
                t2_ = [S.sb(es1, "r_t2%d" % i, [128, 512], F32) for i in range(2)]
                b_t1 = [Buf() for _ in range(2)]
                b_t2 = [Buf() for _ in range(2)]
                for blk, dst, bdst in ((B_RQ, qp, b_qp), (B_RK, kp, b_kp)):
                    for i in range(4):
                        S.dma("sp", rin[:, i * 1024:(i + 1) * 1024], recv1[i, blk, :].rearrange("(p t) -> p t", p=128), (), [b_rin])
                    for c_ in range(8):
                        d2 = c_ % 2
                        sl = slice(c_ * 512, (c_ + 1) * 512)
                        ps, pb = S.psum[d2], S.pbuf[d2]
                        S.mm(ps[:, :], perm[:], rin[:, sl], True, True, [b_perm, b_rin], [pb])
                        S.tt("dve", t1[d2][:], ps[:, :], rsin[:, sl], ALU.mult, [pb, b_tab], [b_t1[d2]])
                        S.tt("pool", t2_[d2][:], rin[:, sl], rcos[:, sl], ALU.mult, [b_rin, b_tab], [b_t2[d2]])
                        S.tt("dve", dst[:, sl], t1[d2][:], t2_[d2][:], ALU.add, [b_t1[d2], b_t2[d2]], [bdst])
            S.barrier()
            lg = S.sb(es, "r_lg", [128, 2], F32)
            lg2 = S.sb(es, "r_lg2", [128, 4], F32)
            b_lg = Buf()
            S.dma("sp", lg[:], io["retlg"], (), [b_lg])
            S.dma("sp", lg2[:], io["retlg2"], (), [b_lg])
            for t_ in (lg, lg2):
                S.act(t_[:], t_[:], AF.Exp, [b_lg], [b_lg], scale=-1.0)
                S.act(t_[:], t_[:], AF.Ln, [b_lg], [b_lg], bias=1.0)
                S.ts("dve", t_[:], t_[:], -1.0, ALU.mult, [b_lg], [b_lg])
            rtab = S.sb(es, "r_tab", [128, 4, 128], F32)
            cexp = S.sb(es, "r_cexp", [128, 2], F32)
            b_rt = Buf()
            S.dma("sp", rtab[:], io["rtab"], (), [b_rt])
            S.dma("sp", cexp[:], io["cexp"], (), [b_rt])
            DT = S.sb(es, "r_DT", [128, 2, 128], F32)
            tmpD = S.sb(es, "r_tmpD", [128, 128], F32)
            b_DT, b_tmpD = Buf(), Buf()
            for h in range(2):
                S.act(DT[:, h, :], rtab[:, 0, :], AF.Exp, [b_rt, b_lg], [b_DT], scale=lg2[:, h:h + 1])
                S.act(tmpD[:], rtab[:, 1, :], AF.Exp, [b_rt, b_lg], [b_tmpD], scale=lg2[:, 2 + h:3 + h])
                S.tt("dve", DT[:, h, :], DT[:, h, :], tmpD[:], ALU.add, [b_DT, b_tmpD], [b_DT])
            qdec = S.sb(es, "r_qdec", [128, 2, 128], F32)
            kd = S.sb(es, "r_kd", [128, 4], F32)
            cdec = S.sb(es, "r_cdec", [128, 2], F32)
            b_dec = Buf()
            for dr in range(2):
                S.act(qdec[:, dr, :], rtab[:, 2 + dr, :], AF.Exp, [b_rt, b_lg], [b_dec], scale=lg[:, dr:dr + 1])
                for h in range(2):
                    c_ = dr * 2 + h
                    S.act(kd[:, c_:c_ + 1], cexp[:, dr:dr + 1], AF.Exp, [b_rt, b_lg], [b_dec], scale=lg2[:, c_:c_ + 1])
            S.act(cdec[:], lg[:], AF.Exp, [b_lg], [b_dec], scale=128.0)
            qf = S.sb(es, "r_qf", [128, SEQ], BF16)
            qb = S.sb(es, "r_qb", [128, SEQ], BF16)
            b_qf, b_qb = Buf(), Buf()
            qp3 = qp[:].rearrange("p (n i) -> p n i", i=128)
            S.tt("dve", qf[:].rearrange("p (n i) -> p n i", i=128), qp3, bc(qdec[:, 0, :], 1, 32), ALU.mult, [b_qp, b_dec], [b_qf])
            S.tt("pool", qb[:].rearrange("p (n i) -> p n i", i=128), qp3, bc(qdec[:, 1, :], 1, 32), ALU.mult, [b_qp, b_dec], [b_qb])
            ktok = S.sb(es, "r_ktok", [128, 32, 128], BF16)
            b_ktok = Buf()
            for g in range(8):
                ps, pb = S.psum[g % 2], S.pbuf[g % 2]
                psb_ = ps.bitcast(BF16)
                for q in range(4):
                    n = g * 4 + q
                    S.tr(psb_[:, q * 128:(q + 1) * 128], kp[:, n * 128:(n + 1) * 128], identb[:], [b_kp, b_id], [pb])
                S.copy("act" if g % 2 else "dve", ktok[:, g * 4:(g + 1) * 4, :], psb_[:, 0:512].rearrange("p (a k) -> p a k", a=4), [pb], [b_ktok])
            vr, b_vr = load_tm(es, "r_v", B_RV)
            rg, b_rg = load_tm(es, "r_g", B_RG)
            S.act(rg[:], rg[:], AF.Silu, [b_rg], [b_rg])
            vf = S.sb(es, "r_vf", [128, 32, 128], BF16)
            vb = S.sb(es, "r_vb", [128, 32, 128], BF16)
            b_vf, b_vb = Buf(), Buf()
            for h in range(2):
                sl = slice(h * 64, (h + 1) * 64)
                S.ts("dve", vf[:, :, sl], vr[:, :, sl], kd[:, h:h + 1], ALU.mult, [b_vr, b_dec], [b_vf])
                S.ts("pool", vb[:, :, sl], vr[:, :, sl], kd[:, 2 + h:3 + h], ALU.mult, [b_vr, b_dec], [b_vb], s2=1.0, op1=ALU.mult)
            sf = S.sb(es, "r_sf", [128, 32, 64], F32)
            sbk = S.sb(es, "r_sb", [128, 32, 64], F32)
            b_sf, b_sbk = Buf(), Buf()
            S.memset("pool", sf[:, 0, :], 0.0, [b_sf])
            S.memset("pool", sbk[:, 31, :], 0.0, [b_sbk])
            for dr in range(2):
                vt, bvt = (vf, b_vf) if dr == 0 else (vb, b_vb)
                st_, bst_ = (sf, b_sf) if dr == 0 else (sbk, b_sbk)
                order = list(range(0, 31)) if dr == 0 else list(range(31, 0, -1))
                for g0 in range(0, 31, 8):
                    grp = order[g0:g0 + 8]
                    bi = 2 + (g0 // 8) % 2
                    ps, pb = S.psum[bi], S.pbuf[bi]
                    for q, n in enumerate(grp):
                        for h in range(2):
                            sl = slice(h * 64, (h + 1) * 64)
                            S.mm(ps[sl, q * 64:(q + 1) * 64], ktok[:, n, sl], vt[:, n, sl], True, True, [b_ktok, bvt], [pb])
                    for q, n in enumerate(grp):
                        nxt = n + 1 if dr == 0 else n - 1
                        S.stt(st_[:, nxt, :], st_[:, n, :], cdec[:, dr:dr + 1], ps[:, q * 64:(q + 1) * 64], ALU.mult, ALU.add,
                              [bst_, pb, b_dec], [bst_])
            sfb = S.sb(es, "r_sfb", [128, 32, 64], BF16)
            sbb = S.sb(es, "r_sbb", [128, 32, 64], BF16)
            b_sfb, b_sbb = Buf(), Buf()
            S.copy("act", sfb[:], sf[:], [b_sf], [b_sfb])
            S.copy("act", sbb[:], sbk[:], [b_sbk], [b_sbb])
            orT = S.sb(es, "r_o", [128, SEQ], BF16)
            b_or = Buf()
            ad = [S.sb(es, "r_ad%d" % i, [128, 4, 128], BF16) for i in range(2)]
            b_ad = [Buf() for _ in range(2)]
            osb = [S.sb(es, "r_osb%d" % i, [128, 4, 2, 64], F32) for i in range(2)]
            osq = [S.sb(es, "r_osq%d" % i, [128, 4, 2, 64], F32) for i in range(2)]
            onb = [S.sb(es, "r_onb%d" % i, [128, 4, 128], BF16) for i in range(2)]
            b_osb = [Buf() for _ in range(2)]
            b_osq = [Buf() for _ in range(2)]
            b_onb = [Buf() for _ in range(2)]
            rst = S.sb(es, "r_rst", [128, 8, 8], F32)
            b_rst = [Buf() for _ in range(8)]
            def ret_stage1(g):
                d2 = g % 2
                for h in range(2):
                    sl = slice(h * 64, (h + 1) * 64)
                    aps, apb = S.psum[h], S.pbuf[h]
                    for cn in range(4):
                        n = g * 4 + cn
                        S.mm(aps[:, cn * 128:(cn + 1) * 128], kp[sl, n * 128:(n + 1) * 128],
                             qp[sl, n * 128:(n + 1) * 128], True, True, [b_kp, b_qp], [apb])
                    S.tt("dve", ad[h][:], aps[:, :].rearrange("p (c i) -> p c i", c=4), bc(DT[:, h, :], 1, 4), ALU.mult,
                         [apb, b_DT], [b_ad[h]])
                    ops_, opb = S.psum[2 + d2 * 2 + h], S.pbuf[2 + d2 * 2 + h]
                    for cn in range(4):
                        n = g * 4 + cn
                        oc = cn * 64
                        S.mm(ops_[:, oc:oc + 64], ad[h][:, cn, :], vr[:, n, sl], True, False, [b_ad[h], b_vr], [opb])
                        if n > 0:
                            S.mm(ops_[:, oc:oc + 64], qf[sl, n * 128:(n + 1) * 128], sfb[sl, n, :], False, n == 31,
                                 [b_qf, b_sfb], [opb])
                        if n < 31:
                            S.mm(ops_[:, oc:oc + 64], qb[sl, n * 128:(n + 1) * 128], sbb[sl, n, :], False, True,
                                 [b_qb, b_sbb], [opb])
                    S.copy("act", osb[d2][:, :, h, :], ops_[:, 0:256].rearrange("p (c e) -> p c e", e=64), [opb], [b_osb[d2]])

            def ret_stage2(g):
                d2 = g % 2
                S.tt("pool", osq[d2][:], osb[d2][:], osb[d2][:], ALU.mult, [b_osb[d2]], [b_osq[d2]])
                S.red(rst[:, g, :], osq[d2][:].rearrange("p c h e -> p (c h) e"), ALU.add, [b_osq[d2]], [b_rst[g]])
                S.ts("dve", rst[:, g, :], rst[:, g, :], 1.0 / 64.0, ALU.mult, [b_rst[g]], [b_rst[g]], s2=EPS, op1=ALU.add)
                S.act(rst[:, g, :], rst[:, g, :], AF.Sqrt, [b_rst[g]], [b_rst[g]])
                S.op("dve", lambda e, o=rst[:, g, :]: e.reciprocal(out=o, in_=o), [b_rst[g]], [b_rst[g]])
                S.tt("dve", osb[d2][:].rearrange("p c h e -> p (c h) e"), osb[d2][:].rearrange("p c h e -> p (c h) e"),
                     bc(rst[:, g, :], 2, 64), ALU.mult, [b_osb[d2], b_rst[g]], [b_osb[d2]])
                S.tt("pool", onb[d2][:], osb[d2][:].rearrange("p c h e -> p c (h e)"), rg[:, g * 4:(g + 1) * 4, :], ALU.mult,
                     [b_osb[d2], b_rg], [b_onb[d2]])
                tps, tb = S.psum[6 + d2], S.pbuf[6 + d2]
                tpsb = tps.bitcast(BF16)
                for cn in range(4):
                    S.tr(tpsb[:, cn * 128:(cn + 1) * 128], onb[d2][:, cn, :], identb[:], [b_onb[d2], b_id], [tb])
                S.copy("act", orT[:, g * 512:(g + 1) * 512], tpsb[:, 0:512], [tb], [b_or])

            ret_stage1(0)
            for g in range(8):
                if g + 1 < 8:
                    ret_stage1(g + 1)
                ret_stage2(g)
            store_o(orT, b_or, 2)
        S.barrier()

    S.barrier()


def emit_C(S, io, last):
    nc = S.nc
    recv2 = io["recv2"]
    with ExitStack() as es:
        oT = S.sb(es, "c_oT", [128, 16, NTOK], BF16)
        b_oT = [Buf() for _ in range(16)]
        wo = S.sb(es, "c_wo", [128, 16, D], BF16)
        b_wo = [Buf() for _ in range(4)]
        for n in range(4):
            S.dma("pool", wo[:, :, n * 512:(n + 1) * 512], io["w_out"][:, n * 512:(n + 1) * 512].rearrange("(k p) c -> p k c", p=128),
                  (), [b_wo[n]])
        gate = S.sb(es, "c_gate", [128, D], F32)
        b_gate = Buf()
        S.dma("sp", gate[:], bc(io["gate"], 0, 128), (), [b_gate])
        if last:
            fg = S.sb(es, "c_fg", [128, D], F32)
            b_fgn = Buf()
            S.dma("sp", fg[:], bc(io["final_g"], 0, 128), (), [b_fgn])
        es1 = es
        if True:
            yc = S.sb(es1, "c_yc", [128, 4, NTOK], BF16)
            b_yc = Buf()
            for j in range(4):
                S.dma("sp", yc[:, j, :], recv2[j, 3, :, :], (), [b_yc])
            cg = S.sb(es1, "c_cg", [128, 4, NTOK], BF16)
            b_cg = Buf()
            for j in range(4):
                S.dma("sp", cg[:, j, :], io["cvg"][j, :, :], (), [b_cg])
            for m in range(3):
                for j in range(4):
                    S.dma(("sp", "act")[j % 2], oT[:, m * 4 + j, :], recv2[j, m, :, :], (), [b_oT[m * 4 + j]])
            S.act(cg[:], cg[:], AF.Silu, [b_cg], [b_cg])
            wpw = S.sb(es1, "c_wpw", [128, 4, 512], BF16)
            b_wpw = Buf()
            S.dma("pool", wpw[:], io["w_pw"].rearrange("(k p) c -> p k c", p=128), (), [b_wpw])
            lngb = S.sb(es1, "c_lngb", [128, 2, 4], F32)
            b_ln = Buf()
            S.dma("sp", lngb[:, 0, :], io["lng"].rearrange("(c p) -> p c", p=128), (), [b_ln], slow=True)
            S.dma("sp", lngb[:, 1, :], io["lnb"].rearrange("(c p) -> p c", p=128), (), [b_ln], slow=True)
            ones = S.sb(es1, "c_ones", [128, 128], BF16)
            b_ones = Buf()
            S.memset("pool", ones[:], 1.0 / 512.0, [b_ones])
            ysq = S.sb(es1, "c_ysq", [128, 4, NTOK], BF16)
            b_ysq = Buf()
            S.act(ysq[:], yc[:], AF.Square, [b_yc], [b_ysq])
            sT = S.sb(es1, "c_sT", [128, 4, NTOK], BF16)
            b_sT = [Buf() for _ in range(2)]
            msq = S.sb(es1, "c_msq", [128, 512], F32)
            rstd = S.sb(es1, "c_rstd", [128, 512], F32)
            b_msq, b_rstd = Buf(), Buf()
            dtmp = [S.sb(es1, "c_d%d" % i, [128, 512], F32) for i in range(2)]
            b_dt = [Buf() for _ in range(2)]
            def ln_half(h):
                hs = slice(h * 512, (h + 1) * 512)
                pm, bpm = S.psum[0], S.pbuf[0]
                pq_, bpq = S.psum[1], S.pbuf[1]
                for j in range(4):
                    S.mm(pm[:, :], ones[:], yc[:, j, hs], j == 0, j == 3, [b_ones, b_yc], [bpm])
                for j in range(4):
                    S.mm(pq_[:, :], ones[:], ysq[:, j, hs], j == 0, j == 3, [b_ones, b_ysq], [bpq])
                S.act(msq[:], pm[:, :], AF.Square, [bpm], [b_msq])
                S.tt("dve", rstd[:], pq_[:, :], msq[:], ALU.subtract, [bpq, b_msq], [b_rstd])
                S.act(rstd[:], rstd[:], AF.Ln, [b_rstd], [b_rstd], bias=EPS)
                S.act(rstd[:], rstd[:], AF.Exp, [b_rstd], [b_rstd], scale=-0.5)
                for j in range(4):
                    d2 = j % 2
                    S.tt("dve", dtmp[d2][:], yc[:, j, hs], pm[:, :], ALU.subtract, [b_yc, bpm], [b_dt[d2]])
                    S.tt("pool", dtmp[d2][:], dtmp[d2][:], rstd[:], ALU.mult, [b_dt[d2], b_rstd], [b_dt[d2]])
                    S.ts("dve", dtmp[d2][:], dtmp[d2][:], lngb[:, 0, j:j + 1], ALU.mult, [b_dt[d2], b_ln], [b_dt[d2]],
                         s2=lngb[:, 1, j:j + 1], op1=ALU.add)
                    S.act(sT[:, j, hs], dtmp[d2][:], AF.Silu, [b_dt[d2]], [b_sT[h]])
            def pw_half(h):
                hs = slice(h * 512, (h + 1) * 512)
                for co in range(4):
                    ps, pb = S.psum[2 + co % 2], S.pbuf[2 + co % 2]
                    for ci in range(4):
                        S.mm(ps[:, :], wpw[:, ci, co * 128:(co + 1) * 128], sT[:, ci, hs], ci == 0, ci == 3, [b_wpw, b_sT[h]], [pb])
                    S.tt("dve", oT[:, 12 + co, hs], ps[:, :], cg[:, co, hs], ALU.mult, [pb, b_cg], [b_oT[12 + co]])
        xt = [S.sb(es, "c_x%d" % i, [128, D], F32) for i in range(2)]
        b_xt = [Buf() for _ in range(2)]
        tmp = [S.sb(es, "c_tmp%d" % i, [128, 512], F32) for i in range(2)]
        b_tmp = [Buf() for _ in range(2)]
        junk = S.sb(es, "c_junk", [128, D], F32)
        b_junk = Buf()
        ss = S.sb(es, "c_ss", [128, 8], F32)
        b_ss = [Buf() for _ in range(8)]
        outs = []
        def outproj_tile(t):
            d2 = t % 2
            S.dma("sp", xt[d2][:], io["x"][t * 128:(t + 1) * 128, :], (), [b_xt[d2]])
            for n in range(4):
                pi = 4 + (t * 4 + n) % 4
                ps, pb = S.psum[pi], S.pbuf[pi]
                for k in range(16):
                    S.mm(ps[:, :], oT[:, k, t * 128:(t + 1) * 128], wo[:, k, n * 512:(n + 1) * 512], k == 0, k == 15,
                         [b_oT[k], b_wo[n]], [pb])
                ti = (t * 4 + n) % 2
                S.tt("dve", tmp[ti][:], ps[:, :], gate[:, n * 512:(n + 1) * 512], ALU.mult, [pb, b_gate], [b_tmp[ti]])
                S.tt("pool", xt[d2][:, n * 512:(n + 1) * 512], xt[d2][:, n * 512:(n + 1) * 512], tmp[ti][:], ALU.add,
                     [b_xt[d2], b_tmp[ti]], [b_xt[d2]])
            if last:
                S.act(junk[:], xt[d2][:], AF.Square, [b_xt[d2]], [b_junk, b_ss[t]], accum=ss[:, t:t + 1])
                S.ts("dve", ss[:, t:t + 1], ss[:, t:t + 1], 1.0 / D, ALU.mult, [b_ss[t]], [b_ss[t]], s2=EPS, op1=ALU.add)
                S.act(ss[:, t:t + 1], ss[:, t:t + 1], AF.Sqrt, [b_ss[t]], [b_ss[t]])
                S.op("dve", lambda e, o=ss[:, t:t + 1]: e.reciprocal(out=o, in_=o), [b_ss[t]], [b_ss[t]])
                S.stt(xt[d2][:], xt[d2][:], ss[:, t:t + 1], fg[:], ALU.mult, ALU.mult, [b_xt[d2], b_ss[t], b_fgn], [b_xt[d2]])
            ob = Buf()
            outs.append(ob)
            S.dma("sp", io["xout"][t * 128:(t + 1) * 128, :], xt[d2][:], [b_xt[d2]], [ob])

        ln_half(0)
        pw_half(0)
        ln_half(1)
        for t in range(4):
            outproj_tile(t)
        pw_half(1)
        for t in range(4, 8):
            outproj_tile(t)
    S.barrier()


_PROGS = {}


CONST_BF16 = ("m1", "m3", "perm")


def _const_io(nc, io, names):
    for n in names:
        io[n] = dram_in(nc, n, CONST_SHAPES[n], BF16 if n in CONST_BF16 else F32)


def prog_mod():
    if "mod" in _PROGS:
        return _PROGS["mod"]
    nc = bass.Bass("TRN2", target_bir_lowering=False)
    io = {"cT": dram_in(nc, "cT", [128, 16, 2]), "wada": dram_in(nc, "wada", [D, 1536]),
          "bada": dram_in(nc, "bada", [2, 1536]), "mod": dram_out(nc, "mod", [2, 1536])}
    with ExitStack() as es:
        S = Sched(nc, es)
        emit_mod(S, io)
        S.run()
    _PROGS["mod"] = nc
    return nc


def prog_A():
    if "A" in _PROGS:
        return _PROGS["A"]
    nc = bass.Bass("TRN2", target_bir_lowering=False)
    io = {"x": dram_in(nc, "x", [NTOK, D]), "shift": dram_in(nc, "shift", [D]), "scale": dram_in(nc, "scale", [D]),
          "norm_g": dram_in(nc, "norm_g", [D]), "w_in": dram_in(nc, "w_in", [D, DIN]), "w_fft": dram_in(nc, "w_fft", [512, 512]),
          "send1": dram_out(nc, "send1", [4, NBLK1, BLK], BF16), "cvg": dram_out(nc, "cvg", [4, 128, NTOK], BF16)}
    _const_io(nc, io, ["ident", "ccsc"])
    with ExitStack() as es:
        S = Sched(nc, es)
        emit_A(S, io)
        S.run()
    _PROGS["A"] = nc
    return nc


B_CONSTS = ["ident", "m1", "m3", "namask", "rcos", "rsin", "perm", "rtab", "cexp"]


def prog_B():
    if "B" in _PROGS:
        return _PROGS["B"]
    nc = bass.Bass("TRN2", target_bir_lowering=False)
    io = {"recv1": dram_in(nc, "recv1", [4, NBLK1, BLK], BF16), "nab": dram_in(nc, "nab", [128, 15, 64]),
          "retlg": dram_in(nc, "retlg", [128, 2]), "retlg2": dram_in(nc, "retlg2", [128, 4]),
          "cw": dram_in(nc, "cw", [128, 31]), "cb": dram_in(nc, "cb", [128, 1]),
          "fftscr": dram_tmp(nc, "fftscr", [128 * 64 * 128], BF16),
          "send2": dram_out(nc, "send2", [4, 4, 128, NTOK], BF16)}
    _const_io(nc, io, B_CONSTS)
    with ExitStack() as es:
        S = Sched(nc, es)
        emit_B(S, io)
        S.run()
    _PROGS["B"] = nc
    return nc


def prog_C(last):
    key = "C%d" % int(last)
    if key in _PROGS:
        return _PROGS[key]
    nc = bass.Bass("TRN2", target_bir_lowering=False)
    io = {"recv2": dram_in(nc, "recv2", [4, 4, 128, NTOK], BF16), "cvg": dram_in(nc, "cvg", [4, 128, NTOK], BF16),
          "x": dram_in(nc, "x", [NTOK, D]), "gate": dram_in(nc, "gate", [D]), "w_out": dram_in(nc, "w_out", [D, D]),
          "w_pw": dram_in(nc, "w_pw", [512, 512]), "lng": dram_in(nc, "lng", [512]), "lnb": dram_in(nc, "lnb", [512]),
          "xout": dram_out(nc, "xout", [NTOK, D])}
    if last:
        io["final_g"] = dram_in(nc, "final_g", [D])
    with ExitStack() as es:
        S = Sched(nc, es)
        emit_C(S, io, last)
        S.run()
    _PROGS[key] = nc
    return nc


def prog_CA():
    if "CA" in _PROGS:
        return _PROGS["CA"]
    nc = bass.Bass("TRN2", target_bir_lowering=False)
    xmid = dram_tmp(nc, "xmid", [NTOK, D])
    ioc = {"recv2": dram_in(nc, "recv2", [4, 4, 128, NTOK], BF16), "cvg": dram_in(nc, "cvg", [4, 128, NTOK], BF16),
           "x": dram_in(nc, "x", [NTOK, D]), "gate": dram_in(nc, "gate", [D]), "w_out": dram_in(nc, "w_out", [D, D]),
           "w_pw": dram_in(nc, "w_pw", [512, 512]), "lng": dram_in(nc, "lng", [512]), "lnb": dram_in(nc, "lnb", [512]),
           "xout": xmid}
    ioa = {"x": xmid, "shift": dram_in(nc, "shift", [D]), "scale": dram_in(nc, "scale", [D]),
           "norm_g": dram_in(nc, "norm_g", [D]), "w_in": dram_in(nc, "w_in", [D, DIN]), "w_fft": dram_in(nc, "w_fft", [512, 512]),
           "send1": dram_out(nc, "send1", [4, NBLK1, BLK], BF16), "cvg": dram_out(nc, "cvg_next", [4, 128, NTOK], BF16)}
    _const_io(nc, ioa, ["ident", "ccsc"])
    xcopy = dram_out(nc, "xout", [NTOK, D])
    with ExitStack() as es:
        S = Sched(nc, es)
        emit_C(S, ioc, False)
        bx = Buf()
        S.dma("sp", xcopy, xmid, (), [bx])
        emit_A(S, ioa)
        S.run()
    _PROGS["CA"] = nc
    return nc


def _run(nc, in_maps):
    res = run_bass_kernel_spmd(nc, in_maps, core_ids=list(range(NCORE)))
    return res.results


def _percore_layer_inputs(l, na_rel_bias, ret_logit_fwd, ret_logit_bwd, conv_w, conv_b):
    col = np.arange(64)
    idx = np.clip(col[None, :] - col[:, None] + 15, 0, 30)
    outs = []
    for j in range(4):
        nab = np.empty((128, 15, 64), np.float32)
        for hh in range(2):
            rb = na_rel_bias[l, 2 * j + hh]
            nab[hh * 64:(hh + 1) * 64] = np.transpose(rb[:, idx], (1, 0, 2))
        lf = ret_logit_fwd[l, 2 * j:2 * j + 2]
        lb = ret_logit_bwd[l, 2 * j:2 * j + 2]
        retlg = np.empty((128, 2), np.float32)
        retlg[0:64, 0], retlg[64:128, 0] = lf[0], lf[1]
        retlg[0:64, 1], retlg[64:128, 1] = lb[0], lb[1]
        retlg2 = np.empty((128, 4), np.float32)
        retlg2[:, 0], retlg2[:, 1], retlg2[:, 2], retlg2[:, 3] = lf[0], lf[1], lb[0], lb[1]
        cw = np.ascontiguousarray(conv_w[l][:, j * 128:(j + 1) * 128].T)
        cb = np.ascontiguousarray(conv_b[l][j * 128:(j + 1) * 128, None])
        outs.append(dict(nab=nab, retlg=retlg, retlg2=retlg2, cw=cw, cb=cb))
    return outs


def kernel(x, c, norm_g, w_ada, b_ada, w_in, w_fft, na_rel_bias, ret_logit_fwd, ret_logit_bwd,
           conv_w, conv_b, conv_ln_g, conv_ln_b, conv_w_pw, w_out, final_g, _debug=None):
    f32 = np.float32
    x = np.asarray(x, f32)
    C = consts()
    cT = np.ascontiguousarray(np.asarray(c, f32).T.reshape(16, 128, NB).transpose(1, 0, 2))
    maps = []
    for k in range(NCORE):
        l, c0 = k // 4, (k % 4) * 1536
        maps.append(dict(cT=cT, wada=np.ascontiguousarray(w_ada[l][:, c0:c0 + 1536]),
                         bada=np.ascontiguousarray(np.broadcast_to(b_ada[l][None, c0:c0 + 1536], (2, 1536)))))
    r = _run(prog_mod(), maps)
    mod = np.stack([np.concatenate([np.asarray(r[l * 4 + q]["mod"]) for q in range(4)], axis=1) for l in range(DEPTH)])
    xs = [np.ascontiguousarray(x[k // 4, (k % 4) * NTOK:(k % 4 + 1) * NTOK]) for k in range(NCORE)]

    def a_inputs(l, k):
        b = k // 4
        return dict(shift=np.ascontiguousarray(mod[l, b, 0:D]), scale=np.ascontiguousarray(mod[l, b, D:2 * D]),
                    norm_g=norm_g[l], w_in=w_in[l], w_fft=w_fft[l], ident=C["ident"], ccsc=C["ccsc"])

    send1 = cvg = None
    for l in range(DEPTH):
        last = (l == DEPTH - 1)
        if l == 0:
            rA = _run(prog_A(), [dict(x=xs[k], **a_inputs(0, k)) for k in range(NCORE)])
            send1 = [np.asarray(rA[k]["send1"]) for k in range(NCORE)]
            cvg = [np.asarray(rA[k]["cvg"]) for k in range(NCORE)]
        pl = _percore_layer_inputs(l, na_rel_bias, ret_logit_fwd, ret_logit_bwd, conv_w, conv_b)
        maps = []
        for k in range(NCORE):
            b, j = k // 4, k % 4
            recv1 = np.stack([send1[b * 4 + i][j] for i in range(4)])
            m = dict(recv1=recv1, **pl[j])
            for n in B_CONSTS:
                m[n] = C[n]
            maps.append(m)
        rB = _run(prog_B(), maps)
        send2 = [np.asarray(rB[k]["send2"]) for k in range(NCORE)]
        maps = []
        for k in range(NCORE):
            b, i = k // 4, k % 4
            recv2 = np.stack([send2[b * 4 + j][i] for j in range(4)])
            m = dict(recv2=recv2, cvg=cvg[k], x=xs[k], gate=np.ascontiguousarray(mod[l, b, 2 * D:3 * D]), w_out=w_out[l],
                     w_pw=conv_w_pw[l], lng=conv_ln_g[l], lnb=conv_ln_b[l])
            if last:
                m["final_g"] = final_g
            else:
                m.update(a_inputs(l + 1, k))
            maps.append(m)
        if last:
            rC = _run(prog_C(True), maps)
        else:
            rC = _run(prog_CA(), maps)
            send1 = [np.asarray(rC[k]["send1"]) for k in range(NCORE)]
            cvg = [np.asarray(rC[k]["cvg_next"]) for k in range(NCORE)]
        xs = [np.asarray(rC[k]["xout"]) for k in range(NCORE)]
    out = np.empty((NB, SEQ, D), f32)
    for k in range(NCORE):
        out[k // 4, (k % 4) * NTOK:(k % 4 + 1) * NTOK] = xs[k]
    return out
```

```python
import numpy as np
import ml_dtypes
from contextlib import ExitStack
import concourse.bass as bass
import concourse.mybir as mybir
from concourse.bass_utils import run_bass_kernel_spmd

F32 = mybir.dt.float32
BF16 = mybir.dt.bfloat16
AF = mybir.ActivationFunctionType
ALU = mybir.AluOpType
AX = mybir.AxisListType

D = 2048
SEQ = 4096
NB = 2
DEPTH = 2
DIN = 6656
NTOK = 1024
NCORE = 8
EPS = 1e-6
NBLK1 = 13
BLK = 128 * 1024
B_P, B_Q, B_FG, B_NQ, B_NK, B_NV, B_NG, B_RQ, B_RK, B_RV, B_RG, B_CA, B_CB = range(13)


PROFILE_SCOPES = False


class Buf:
    __slots__ = ("name", "w", "r")

    def __init__(self, name=""):
        self.name = name
        self.w = {}
        self.r = {}


class Sched:
    COMPUTE = ("pe", "act", "dve", "pool")
    NDMA = 12

    def __init__(self, nc, es):
        self.nc = nc
        self.es = es
        self.eng = {"pe": nc.tensor, "act": nc.scalar, "dve": nc.vector, "pool": nc.gpsimd, "sp": nc.sync}
        self.streams = {e: [] for e in self.eng}
        self.sems = {}
        self.cnt = {}
        self.waited = {e: {} for e in self.eng}
        for e in self.COMPUTE:
            self.sems[e] = es.enter_context(nc.semaphore("s_" + e))
            self.cnt[e] = 0
        self.dq = {}
        for q in ("sp", "pool", "act"):
            lst = []
            for i in range(self.NDMA):
                k = "d_%s_%d" % (q, i)
                self.sems[k] = es.enter_context(nc.semaphore(k))
                self.cnt[k] = 0
                lst.append(k)
            self.dq[q] = [lst, 0]
        self.scope = None
        self.psum = [es.enter_context(nc.psum_tensor("psb%d" % i, [128, 512], F32)) for i in range(8)]
        self.pbuf = [Buf("ps%d" % i) for i in range(8)]

    def sb(self, es, name, shape, dt):
        return es.enter_context(self.nc.sbuf_tensor(name, list(shape), dt))

    def _deps(self, reads, writes, par=False):
        deps = {}

        def add(k, v):
            if deps.get(k, 0) < v:
                deps[k] = v
        for b in reads:
            for k, v in b.w.items():
                add(k, v)
        for b in writes:
            if not (par and not b.r):
                for k, v in b.w.items():
                    add(k, v)
            for k, v in b.r.items():
                add(k, v)
        return deps

    def _emit_waits(self, eng, deps):
        for k, v in deps.items():
            if eng == "pe" and k == "pe":
                continue
            if self.waited[eng].get(k, 0) < v:
                self.waited[eng][k] = v
                self.streams[eng].append(("wait", k, v))

    def _mark(self, tok, reads, writes, par=False):
        k, v = tok
        for b in reads:
            if b.r.get(k, 0) < v:
                b.r[k] = v
        for b in writes:
            if par and not b.r:
                if b.w.get(k, 0) < v:
                    b.w[k] = v
            else:
                b.w = {k: v}
                b.r = {}

    def op(self, eng, fn, reads=(), writes=()):
        self._emit_waits(eng, self._deps(reads, writes))
        self.cnt[eng] += 1
        tok = (eng, self.cnt[eng])
        self.streams[eng].append(("op", fn, eng, 1, self.scope))
        self._mark(tok, reads, writes)
        return tok

    def dma(self, q, out, in_, reads=(), writes=(), slow=False):
        lst, idx = self.dq[q]
        k = lst[idx % len(lst)]
        self.dq[q][1] = idx + 1
        deps = self._deps(reads, writes, par=True)
        if self.cnt[k] > 0:
            deps[k] = max(deps.get(k, 0), self.cnt[k])
        self._emit_waits(q, deps)
        self.cnt[k] += 16
        tok = (k, self.cnt[k])
        if slow:
            self.streams[q].append(("op", lambda e: e.dma_start(out=out, in_=in_, allow_slow_non_contiguous=True), k, 16, self.scope))
        else:
            self.streams[q].append(("op", lambda e: e.dma_start(out=out, in_=in_), k, 16, self.scope))
        self._mark(tok, reads, writes, par=True)
        return tok

    def coll(self, kind, ins, outs, groups, reads=(), writes=()):
        q = "pool"
        lst, idx = self.dq[q]
        k = lst[idx % len(lst)]
        self.dq[q][1] = idx + 1
        deps = self._deps(reads, writes)
        if self.cnt[k] > 0:
            deps[k] = max(deps.get(k, 0), self.cnt[k])
        self._emit_waits(q, deps)
        self.cnt[k] += 16
        tok = (k, self.cnt[k])
        self.streams[q].append(("op", lambda e: e.collective_compute(kind, ALU.bypass, replica_groups=groups, ins=ins, outs=outs), k, 16, self.scope))
        self._mark(tok, reads, writes)
        return tok

    def barrier(self):
        allv = {k: v for k, v in self.cnt.items() if v > 0}
        for e in self.eng:
            self._emit_waits(e, dict(allv))

    def run(self):
        nc = self.nc
        self.barrier()
        with nc.Block() as block:
            def make(ename):
                def body(e):
                    cur, ctx = None, None
                    for it in self.streams[ename]:
                        if it[0] == "wait":
                            e.wait_ge(self.sems[it[1]], it[2])
                        else:
                            _, fn, k, inc, sc = it
                            if PROFILE_SCOPES and sc != cur:
                                if ctx is not None:
                                    ctx.__exit__(None, None, None)
                                    ctx = None
                                if sc is not None:
                                    ctx = nc.named_scope(sc)
                                    ctx.__enter__()
                                cur = sc
                            fn(e).then_inc(self.sems[k], inc)
                    if ctx is not None:
                        ctx.__exit__(None, None, None)
                return body
            block.sync(make("sp"))
            block.tensor(make("pe"))
            block.scalar(make("act"))
            block.vector(make("dve"))
            block.gpsimd(make("pool"))

    def mm(self, out, lhsT, rhs, start, stop, r, w):
        return self.op("pe", lambda e: e.matmul(out, lhsT=lhsT, rhs=rhs, start=start, stop=stop), r, w)

    def tr(self, out, in_, ident, r, w):
        return self.op("pe", lambda e: e.transpose(out=out, in_=in_, identity=ident), r, w)

    def act(self, out, in_, func, r, w, bias=None, scale=None, accum=None):
        kw = {}
        if bias is not None:
            kw["bias"] = bias
        if scale is not None:
            kw["scale"] = scale
        if accum is not None:
            kw["accum_out"] = accum
        return self.op("act", lambda e: e.activation(out=out, in_=in_, func=func, **kw), r, w)

    def tt(self, eng, out, in0, in1, op, r, w):
        return self.op(eng, lambda e: e.tensor_tensor(out=out, in0=in0, in1=in1, op=op), r, w)

    def ts(self, eng, out, in0, s1, op0, r, w, s2=None, op1=None):
        if op1 is None:
            return self.op(eng, lambda e: e.tensor_scalar(out=out, in0=in0, scalar1=s1, scalar2=None, op0=op0), r, w)
        return self.op(eng, lambda e: e.tensor_scalar(out=out, in0=in0, scalar1=s1, scalar2=s2, op0=op0, op1=op1), r, w)

    def stt(self, out, in0, scalar, in1, op0, op1, r, w):
        return self.op("dve", lambda e: e.scalar_tensor_tensor(out=out, in0=in0, scalar=scalar, in1=in1, op0=op0, op1=op1), r, w)

    def copy(self, eng, out, in_, r, w):
        if eng == "act":
            return self.op("act", lambda e: e.copy(out=out, in_=in_), r, w)
        return self.op(eng, lambda e: e.tensor_copy(out=out, in_=in_), r, w)

    def red(self, out, in_, op, r, w, negate=False):
        return self.op("dve", lambda e: e.tensor_reduce(out=out, in_=in_, axis=AX.X, op=op, negate=negate), r, w)

    def memset(self, eng, ap, val, w):
        return self.op(eng, lambda e: e.memset(ap, val), (), w)


def bc(ap, pos, count):
    lst = [list(x) for x in ap.ap]
    lst.insert(pos, [0, count])
    return bass.AP(ap.tensor, ap.offset, lst)


def dram_in(nc, name, shape, dt=F32):
    return nc.dram_tensor(name, list(shape), dt, kind="ExternalInput").ap()


def dram_out(nc, name, shape, dt=F32):
    return nc.dram_tensor(name, list(shape), dt, kind="ExternalOutput").ap()


def dram_tmp(nc, name, shape, dt=F32):
    return nc.dram_tensor(name, list(shape), dt, kind="Internal").ap()


_CONST = None


def consts():
    global _CONST
    if _CONST is not None:
        return _CONST
    c = {}
    c["ident"] = np.eye(128, dtype=np.float32)
    i128 = np.arange(128, dtype=np.float64)
    ang = 2 * np.pi * np.outer(i128, i128) / 128.0
    ccsc = np.stack([np.cos(ang), np.sin(ang)], axis=1) / np.sqrt(128.0)
    c["ccsc"] = ccsc.astype(np.float32)
    s1 = np.arange(64)[:, None, None]
    s2 = np.arange(64)[None, :, None]
    k1 = np.arange(64)[None, None, :]
    th = 2 * np.pi * (k1 * s1 / 64.0 + k1 * s2 / 4096.0)
    m1 = np.zeros((128, 64, 128), np.float64)
    m1[0:64, :, 0:64] = np.cos(th)
    m1[0:64, :, 64:128] = np.sin(th)
    m1[64:128, :, 0:64] = -np.sin(th)
    m1[64:128, :, 64:128] = np.cos(th)
    c["m1"] = m1.astype(ml_dtypes.bfloat16)
    ph = 2 * np.pi * np.outer(np.arange(64), np.arange(64)) / 64.0
    m3 = np.concatenate([np.cos(ph), -np.sin(ph)], axis=0) / 64.0
    c["m3"] = m3.astype(ml_dtypes.bfloat16)
    col = np.arange(64)
    cs = np.clip(col - 8, 0, 48)
    rel = col[None, :] - cs[:, None]
    m = np.where((rel >= 0) & (rel < 16), 0.0, -30000.0).astype(np.float32)
    c["namask"] = np.concatenate([m, m], axis=0)
    half = 32
    inv = 10000.0 ** (-np.arange(half, dtype=np.float32) / half)
    angr = np.arange(SEQ, dtype=np.float32)[:, None] * inv[None, :]
    cosr = np.cos(angr.astype(np.float64)).T
    sinr = np.sin(angr.astype(np.float64)).T
    cos64 = np.concatenate([cosr, cosr], axis=0)
    sin64 = np.concatenate([-sinr, sinr], axis=0)
    c["rcos"] = np.concatenate([cos64, cos64], axis=0).astype(np.float32)
    c["rsin"] = np.concatenate([sin64, sin64], axis=0).astype(np.float32)
    pm = np.zeros((128, 128), np.float32)
    for mm_ in range(128):
        pm[mm_ ^ 32, mm_] = 1.0
    c["perm"] = pm.astype(ml_dtypes.bfloat16)
    j = np.arange(128)[:, None]
    i = np.arange(128)[None, :]
    BIG = 1.0e7
    distf = np.where(i >= j, (i - j), BIG).astype(np.float32)
    distb = np.where(j > i, (j - i), BIG).astype(np.float32)
    io1 = np.broadcast_to(np.arange(1, 129, dtype=np.float32)[None, :], (128, 128))
    io2 = np.broadcast_to((128 - np.arange(128, dtype=np.float32))[None, :], (128, 128))
    c["rtab"] = np.ascontiguousarray(np.stack([distf, distb, io1, io2], axis=1)).astype(np.float32)
    ce = np.stack([127 - np.arange(128), np.arange(128)], axis=1).astype(np.float32)
    c["cexp"] = ce
    _CONST = c
    return c


CONST_SHAPES = {"ident": [128, 128], "ccsc": [128, 2, 128], "m1": [128, 64, 128], "m3": [128, 64],
                "namask": [128, 64], "rcos": [128, SEQ], "rsin": [128, SEQ], "perm": [128, 128],
                "rtab": [128, 4, 128], "cexp": [128, 2]}


def emit_mod(S, io):
    nc = S.nc
    with ExitStack() as es:
        cT = S.sb(es, "m_cT", [128, 16, 2], F32)
        bcT = Buf()
        ba = S.sb(es, "m_ba", [2, 1536], F32)
        bba = Buf()
        res = S.sb(es, "m_res", [2, 1536], F32)
        bres = Buf()
        wt = [S.sb(es, "m_w%d" % i, [128, 16, 512], F32) for i in range(3)]
        bw = [Buf() for _ in range(3)]
        S.dma("sp", cT[:], io["cT"], (), [bcT])
        S.dma("sp", ba[:], io["bada"], (), [bba])
        for n in range(3):
            S.dma("sp", wt[n][:], io["wada"][:, n * 512:(n + 1) * 512].rearrange("(k p) c -> p k c", p=128), (), [bw[n]])
        S.act(cT[:], cT[:], AF.Silu, [bcT], [bcT])
        cR = S.sb(es, "m_cR", [128, 16, 64, 2], F32)
        bcR = Buf()
        S.copy("dve", cR[:], bc(cT[:], 2, 64), [bcT], [bcR])
        for n in range(3):
            ps = S.psum[n]
            pb = S.pbuf[n]
            for k in range(16):
                S.mm(ps[:, :], cR[:, k, :, :].rearrange("p r b -> p (r b)"), wt[n][:, k, :], k == 0, k == 15, [bcR, bw[n]], [pb])
            S.tt("dve", res[:, n * 512:(n + 1) * 512], ps[0:2, :], ba[:, n * 512:(n + 1) * 512], ALU.add, [pb, bba], [bres])
        S.dma("sp", io["mod"], res[:], [bres], [Buf()])
    S.barrier()


def emit_A(S, io, layer_tag=""):
    nc = S.nc
    with ExitStack() as es:
        ident = S.sb(es, "a_ident", [128, 128], F32)
        b_ident = Buf()
        S.dma("sp", ident[:], io["ident"], (), [b_ident])
        gT = S.sb(es, "a_gT", [128, 16], F32)
        scT = S.sb(es, "a_scT", [128, 16], F32)
        shT = S.sb(es, "a_shT", [128, 16], F32)
        gsT = S.sb(es, "a_gsT", [128, 16], F32)
        b_mod = Buf()
        b_gs = Buf()
        S.dma("sp", gT[:], io["norm_g"].rearrange("(c p) -> p c", p=128), (), [b_mod], slow=True)
        S.dma("sp", scT[:], io["scale"].rearrange("(c p) -> p c", p=128), (), [b_mod], slow=True)
        S.dma("sp", shT[:], io["shift"].rearrange("(c p) -> p c", p=128), (), [b_mod], slow=True)
        S.stt(gsT[:], scT[:], 1.0, gT[:], ALU.add, ALU.mult, [b_mod], [b_gs])

        ccsc = S.sb(es, "a_ccsc", [128, 2, 128], F32)
        b_cc = Buf()
        S.dma("sp", ccsc[:], io["ccsc"], (), [b_cc])
        wf = S.sb(es, "a_wf", [128, 4, 512], F32)
        b_wf = Buf()
        S.dma("sp", wf[:], io["w_fft"].rearrange("(g p) c -> p g c", p=128), (), [b_wf])
        mcs = S.sb(es, "a_mcs", [128, 2, 4, 512], BF16)
        b_mcs = Buf()
        for cs in range(2):
            for g in range(4):
                pi = (cs * 4 + g) % 4
                ps, pb = S.psum[pi], S.pbuf[pi]
                S.mm(ps[:, :], ccsc[:, cs, :], wf[:, g, :], True, True, [b_cc, b_wf], [pb])
                S.copy("act" if g % 2 else "dve", mcs[:, cs, g, :], ps[:, :], [pb], [b_mcs])

        hT = S.sb(es, "a_hT", [128, 16, NTOK], BF16)
        b_hT = [Buf() for _ in range(2)]
        xt = [S.sb(es, "a_x%d" % i, [128, D], F32) for i in range(4)]
        b_xt = [Buf() for _ in range(4)]
        junk = S.sb(es, "a_junk", [128, D], F32)
        b_junk = Buf()
        ss = S.sb(es, "a_ss", [128, 8], F32)
        rs = S.sb(es, "a_rs", [128, 8], F32)
        b_ss = [Buf() for _ in range(8)]
        for grp in range(2):
            for tt_ in range(4):
                t = grp * 4 + tt_
                S.dma("sp", xt[tt_][:], io["x"][t * 128:(t + 1) * 128, :], (), [b_xt[tt_]])
                S.act(junk[:], xt[tt_][:], AF.Square, [b_xt[tt_]], [b_junk, b_ss[t]], accum=ss[:, t:t + 1])
                S.ts("dve", rs[:, t:t + 1], ss[:, t:t + 1], 1.0 / D, ALU.mult, [b_ss[t]], [b_ss[t]], s2=EPS, op1=ALU.add)
                S.act(rs[:, t:t + 1], rs[:, t:t + 1], AF.Sqrt, [b_ss[t]], [b_ss[t]])
                S.op("dve", lambda e, o=rs[:, t:t + 1]: e.reciprocal(out=o, in_=o), [b_ss[t]], [b_ss[t]])
                S.ts("pool", xt[tt_][:], xt[tt_][:], rs[:, t:t + 1], ALU.mult, [b_xt[tt_], b_ss[t]], [b_xt[tt_]], s2=1.0, op1=ALU.mult)
            for k in range(16):
                pi = 4 + (k % 4)
                ps, pb = S.psum[pi], S.pbuf[pi]
                for tt_ in range(4):
                    S.tr(ps[:, tt_ * 128:(tt_ + 1) * 128], xt[tt_][:, k * 128:(k + 1) * 128], ident[:], [b_xt[tt_], b_ident], [pb])
                if k % 2:
                    S.ts("dve", hT[:, k, grp * 512:(grp + 1) * 512], ps[:, :], gsT[:, k:k + 1], ALU.mult,
                         [pb, b_gs, b_mod], [b_hT[grp]], s2=shT[:, k:k + 1], op1=ALU.add)
                else:
                    S.act(hT[:, k, grp * 512:(grp + 1) * 512], ps[:, :], AF.Identity, [pb, b_gs, b_mod], [b_hT[grp]],
                          bias=shT[:, k:k + 1], scale=gsT[:, k:k + 1])

        wb = [S.sb(es, "a_w%d" % i, [128, 16, 512], BF16) for i in range(3)]
        b_wb = [Buf() for _ in range(3)]
        st_fm = [S.sb(es, "a_sfm%d" % i, [128, NTOK], BF16) for i in range(2)]
        b_sfm = [Buf() for _ in range(2)]
        st_tm = [S.sb(es, "a_stm%d" % i, [128, 8, 512], BF16) for i in range(2)]
        b_stm = [Buf() for _ in range(2)]
        fxT = S.sb(es, "a_fxT", [128, 4, NTOK], BF16)
        b_fx = Buf()
        send1 = io["send1"]

        def wload(p):
            slot = p % 3
            S.dma("pool", wb[slot][:], io["w_in"][:, p * 512:(p + 1) * 512].rearrange("(k p) c -> p k c", p=128),
                  (), [b_wb[slot]])

        cnt = {"fm": 0, "tm": 0, "ev": 0, "ps": 0}

        def evac(out, in_, r, w, scale=None):
            cnt["ev"] += 1
            if cnt["ev"] % 2:
                if scale is None:
                    S.copy("act", out, in_, r, w)
                else:
                    S.act(out, in_, AF.Copy, r, w, scale=scale)
            else:
                if scale is None:
                    S.copy("dve", out, in_, r, w)
                else:
                    S.ts("dve", out, in_, scale, ALU.mult, r, w)

        def nextps():
            cnt["ps"] += 1
            i = cnt["ps"] % 4
            return S.psum[i], S.pbuf[i]

        def fm_piece(p, dst_fn, scale=None):
            slot = p % 3
            for j in range(4):
                dram_ap, sb_ap = dst_fn(j)
                if sb_ap is None:
                    si = cnt["fm"] % 2
                    cnt["fm"] += 1
                    stage, bst = st_fm[si], b_sfm[si]
                    tgt = stage
                else:
                    tgt, bst = sb_ap, b_fx
                for h in range(2):
                    ps, pb = nextps()
                    for k in range(16):
                        S.mm(ps[:, :], wb[slot][:, k, j * 128:(j + 1) * 128], hT[:, k, h * 512:(h + 1) * 512],
                             k == 0, k == 15, [b_wb[slot], b_hT[h]], [pb])
                    if sb_ap is None:
                        evac(tgt[:, h * 512:(h + 1) * 512], ps[:, :], [pb], [bst], scale)
                    else:
                        evac(tgt[:, j, h * 512:(h + 1) * 512], ps[:, :], [pb], [bst], scale)
                if dram_ap is not None:
                    S.dma("sp", dram_ap, tgt[:], [bst], [Buf()])

        def tm_from(lhs_fn, nk, rhs_fn, rbufs, blk):
            si = cnt["tm"] % 2
            cnt["tm"] += 1
            stage, bst = st_tm[si], b_stm[si]
            for t in range(8):
                ps, pb = nextps()
                for k in range(nk):
                    S.mm(ps[:, :], lhs_fn(k, t), rhs_fn(k), k == 0, k == nk - 1, rbufs(t), [pb])
                evac(stage[:, t, :], ps[:, :], [pb], [bst])
            for j in range(4):
                dst = send1[j, blk, :].rearrange("(t p c) -> p t c", p=128, c=128)
                S.dma("sp", dst, stage[:, :, j * 128:(j + 1) * 128], [bst], [Buf()])

        def send_fm(blk):
            return lambda j: (send1[j, blk, :].rearrange("(p t) -> p t", p=128), None)

        wload(0)
        wload(1)
        for p in range(13):
            if p + 2 < 13:
                wload(p + 2)
            slot = p % 3
            if p == 0:
                fm_piece(0, lambda j: (None, fxT))
                for cs, blk in ((0, B_P), (1, B_Q)):
                    tm_from(lambda k, t: fxT[:, k, t * 128:(t + 1) * 128], 4,
                            lambda k, cs=cs: mcs[:, cs, k, :], lambda t: [b_fx, b_mcs], blk)
            elif p == 1:
                fm_piece(1, send_fm(B_FG))
            elif p == 2:
                fm_piece(2, send_fm(B_NQ), scale=0.125)
            elif p == 3:
                fm_piece(3, send_fm(B_NK))
            elif p in (4, 8, 9):
                blk = {4: B_NV, 8: B_RV, 9: B_RG}[p]
                tm_from(lambda k, t: hT[:, k, t * 128:(t + 1) * 128], 16,
                        lambda k, slot=slot: wb[slot][:, k, :], lambda t, slot=slot: [b_hT[t // 4], b_wb[slot]], blk)
            elif p == 5:
                fm_piece(5, send_fm(B_NG))
            elif p == 6:
                fm_piece(6, send_fm(B_RQ), scale=0.125)
            elif p == 7:
                fm_piece(7, send_fm(B_RK))
            elif p == 10:
                fm_piece(10, send_fm(B_CA))
            elif p == 11:
                fm_piece(11, send_fm(B_CB))
            elif p == 12:
                fm_piece(12, lambda j: (io["cvg"][j, :, :], None))
    S.barrier()


def _row_start(r):
    return min(max(r - 4, 0), 56)


def emit_B(S, io):
    nc = S.nc
    recv1 = io["recv1"]
    send2 = io["send2"]

    def load_fm(es, name, blk):
        t = S.sb(es, name, [128, SEQ], BF16)
        b = Buf()
        for i in range(4):
            S.dma(("sp", "act")[i % 2], t[:, i * 1024:(i + 1) * 1024], recv1[i, blk, :].rearrange("(p t) -> p t", p=128), (), [b])
        return t, b

    def load_tm(es, name, blk):
        t = S.sb(es, name, [128, 32, 128], BF16)
        b = Buf()
        for i in range(4):
            S.dma(("sp", "act")[i % 2], t[:, i * 8:(i + 1) * 8, :], recv1[i, blk, :].rearrange("(t p c) -> p t c", p=128, c=128), (), [b])
        return t, b

    def store_o(t, b, blk):
        for i in range(4):
            S.dma("sp", send2[i, blk, :, :], t[:, i * 1024:(i + 1) * 1024], [b], [Buf()])

    with ExitStack() as es0:
        identf = S.sb(es0, "b_identf", [128, 128], F32)
        identb = S.sb(es0, "b_identb", [128, 128], BF16)
        b_id = Buf()
        S.dma("sp", identf[:], io["ident"], (), [b_id])
        S.copy("dve", identb[:], identf[:], [b_id], [b_id])

        qT = S.sb(es0, "n_q", [64, 2, SEQ], BF16)
        kT = S.sb(es0, "n_k", [64, 2, SEQ], BF16)
        ng = S.sb(es0, "n_g", [128, SEQ], BF16)
        ve = S.sb(es0, "n_ve", [128, 32, 128], BF16)
        vo = S.sb(es0, "n_vo", [128, 32, 128], BF16)
        nab = S.sb(es0, "n_bias", [128, 15, 64], F32)
        msk = S.sb(es0, "n_mask", [128, 64], F32)
        b_q, b_k, b_ng, b_ve, b_vo, b_nab = Buf(), Buf(), Buf(), Buf(), Buf(), Buf()

        S.scope = "fft"
        with ExitStack() as es:
            pq = S.sb(es, "f_pq", [128, 64, 128], BF16)
            b_pq = Buf()
            for i in range(4):
                for c_, blk in ((0, B_P), (1, B_Q)):
                    S.dma("sp", pq[c_ * 64 + i * 16:c_ * 64 + (i + 1) * 16, :, :],
                          recv1[i, blk, :].rearrange("(a s c) -> a s c", a=16, c=128), (), [b_pq])
            m1 = S.sb(es, "f_m1", [128, 64, 128], BF16)
            b_m1 = Buf()
            for h in range(4):
                S.dma("act", m1[:, h * 16:(h + 1) * 16, :], io["m1"][:, h * 16:(h + 1) * 16, :], (), [b_m1])
            m3 = S.sb(es, "f_m3", [128, 64], BF16)
            b_m3 = Buf()
            S.dma("act", m3[:], io["m3"], (), [b_m3])
            fg, b_fg = load_fm(es, "f_fg", B_FG)
            gf = S.sb(es, "f_gate", [128, SEQ], BF16)
            b_gf = Buf()
            S.act(gf[:], fg[:], AF.Silu, [b_fg], [b_gf])
            S.scope = "conv"
            ca, b_ca = load_fm(es, "c_a", B_CA)
            cbt, b_cb = load_fm(es, "c_b", B_CB)
            S.act(cbt[:], cbt[:], AF.Sigmoid, [b_cb], [b_cb])
            up = S.sb(es, "c_up", [128, SEQ + 32], BF16)
            b_up = Buf()
            S.memset("pool", up[:, 0:15], 0.0, [b_up])
            S.memset("pool", up[:, 15 + SEQ:SEQ + 32], 0.0, [b_up])
            S.tt("dve", up[:, 15:15 + SEQ], ca[:], cbt[:], ALU.mult, [b_ca, b_cb], [b_up])
            cw = S.sb(es, "c_w", [128, 31], F32)
            cbias = S.sb(es, "c_bias", [128, 1], F32)
            b_cw = Buf()
            S.dma("sp", cw[:], io["cw"], (), [b_cw])
            S.dma("sp", cbias[:], io["cb"], (), [b_cw])
            dg = S.sb(es, "c_diag", [128, 31, 128], BF16)
            b_dg = Buf()
            for k in range(31):
                S.ts("pool", dg[:, k, :], identf[:], cw[:, k:k + 1], ALU.mult, [b_id, b_cw], [b_dg], s2=1.0, op1=ALU.mult)
            S.scope = "na"
            for i in range(4):
                for hh in range(2):
                    S.dma("act", qT[:, hh, i * 1024:(i + 1) * 1024],
                          recv1[i, B_NQ, hh * 65536:(hh + 1) * 65536].rearrange("(p t) -> p t", p=64), (), [b_q])
                    S.dma("act", kT[:, hh, i * 1024:(i + 1) * 1024],
                          recv1[i, B_NK, hh * 65536:(hh + 1) * 65536].rearrange("(p t) -> p t", p=64), (), [b_k])
            for i in range(4):
                S.dma("act", ng[:, i * 1024:(i + 1) * 1024], recv1[i, B_NG, :].rearrange("(p t) -> p t", p=128), (), [b_ng])
                S.dma("act", ve[:, i * 8:(i + 1) * 8, :], recv1[i, B_NV, :].rearrange("(t p c) -> p t c", p=128, c=128), (), [b_ve])
            for i in range(4):
                flat = recv1[i, B_NV, :]
                S.dma("act", vo[:, i * 8:i * 8 + 7, :],
                      flat[64 * 128:(64 + 7 * 128) * 128].rearrange("(t p c) -> p t c", p=128, c=128), (), [b_vo])
                if i < 3:
                    S.dma("act", vo[0:64, i * 8 + 7, :], flat[960 * 128:1024 * 128].rearrange("(p c) -> p c", c=128), (), [b_vo])
                    S.dma("act", vo[64:128, i * 8 + 7, :], recv1[i + 1, B_NV, 0:64 * 128].rearrange("(p c) -> p c", c=128), (), [b_vo])
            S.dma("act", nab[:], io["nab"], (), [b_nab])
            S.dma("act", msk[:], io["namask"], (), [b_nab])
            S.scope = "fft"
            tsb = S.sb(es, "f_t", [128, 64, 128], BF16)
            b_t = Buf()
            for g in range(16):
                ps, pb = S.psum[g % 2], S.pbuf[g % 2]
                for q in range(4):
                    s2 = g * 4 + q
                    S.mm(ps[:, q * 128:(q + 1) * 128], m1[:, s2, :], pq[:, s2, :], True, True, [b_m1, b_pq], [pb])
                S.copy("act" if g % 2 else "dve", tsb[:, g * 4:(g + 1) * 4, :],
                       ps[:, :].rearrange("p (a c) -> p a c", a=4), [pb], [b_t])
            scr = io["fftscr"]
            b_scr = Buf()
            S.dma("sp", scr.rearrange("(p s c) -> p s c", p=128, c=128), tsb[:], [b_t], [b_scr])
            t2 = S.sb(es, "f_t2", [128, 64, 128], BF16)
            b_t2 = Buf()
            src = scr.rearrange("(c k s h) -> c s k h", c=2, k=64, s=64)
            for c_ in range(2):
                S.dma("sp", t2[c_ * 64:(c_ + 1) * 64, :, :], src[c_], [b_scr], [b_t2])
            S.scope = "conv"
            yc = S.sb(es, "c_y", [128, SEQ], BF16)
            b_yc = Buf()
            for t in range(8):
                ps, pb = S.psum[t % 2], S.pbuf[t % 2]
                for k in range(31):
                    S.mm(ps[:, :], dg[:, k, :], up[:, t * 512 + k:t * 512 + k + 512], k == 0, k == 30, [b_dg, b_up], [pb])
                S.ts("dve", yc[:, t * 512:(t + 1) * 512], ps[:, :], cbias[:, 0:1], ALU.add, [pb, b_cw], [b_yc])
            store_o(yc, b_yc, 3)
            S.scope = "fft"
            of = S.sb(es, "f_o", [128, SEQ], BF16)
            b_of = Buf()
            ofv = of[:].rearrange("p (k2 k1) -> p k1 k2", k1=64)
            gfv = gf[:].rearrange("p (k2 k1) -> p k1 k2", k1=64)
            for g in range(8):
                ps, pb = S.psum[2 + g % 2], S.pbuf[2 + g % 2]
                for q in range(8):
                    k1 = g * 8 + q
                    S.mm(ps[:, q * 64:(q + 1) * 64], t2[:, k1, :], m3[:], True, True, [b_t2, b_m3], [pb])
                S.tt("dve", ofv[:, g * 8:(g + 1) * 8, :], ps[:, :].rearrange("p (a k) -> p a k", a=8),
                     gfv[:, g * 8:(g + 1) * 8, :], ALU.mult, [pb, b_gf], [b_of])
            store_o(of, b_of, 0)
        S.barrier()

        S.scope = "na"
        with ExitStack() as es:
            gn = S.sb(es, "n_gate", [128, SEQ], BF16)
            b_gn = Buf()
            S.act(gn[:], ng[:], AF.Silu, [b_ng], [b_gn])
            S.tt("dve", nab[:], nab[:], bc(msk[:], 1, 15), ALU.add, [b_nab], [b_nab])
            on = S.sb(es, "n_o", [128, SEQ], BF16)
            b_on = Buf()
            sc = [S.sb(es, "n_sc%d" % i, [128, 512], F32) for i in range(3)]
            b_sc = [Buf() for _ in range(3)]
            pe_ = [S.sb(es, "n_p%d" % i, [128, 512], BF16) for i in range(3)]
            b_pe = [Buf() for _ in range(3)]
            pn = [S.sb(es, "n_pn%d" % i, [128, 512], BF16) for i in range(3)]
            b_pn = [Buf() for _ in range(3)]
            pt = [S.sb(es, "n_pt%d" % i, [128, 4, 128], BF16) for i in range(3)]
            b_pt = [Buf() for _ in range(3)]
            st = S.sb(es, "n_st", [128, 64, 4], F32)
            b_st = [Buf() for _ in range(64)]
            def na_stage_qk(r):
                d2 = r % 3
                ks = _row_start(r) * 64
                sps, sb_ = S.psum[(0, 1, 6)[d2]], S.pbuf[(0, 1, 6)[d2]]
                for hh in range(2):
                    lo, hi = hh * 64, (hh + 1) * 64
                    S.mm(sps[lo:hi, :], qT[:, hh, r * 64:(r + 1) * 64], kT[:, hh, ks:ks + 512], True, True, [b_q, b_k], [sb_])

            def na_stage_a(r):
                d2 = r % 3
                rs_ = _row_start(r)
                ks = rs_ * 64
                j0 = rs_ - r + 7
                sps, sb_ = S.psum[(0, 1, 6)[d2]], S.pbuf[(0, 1, 6)[d2]]
                S.tt("dve", sc[d2][:], sps[:, :], nab[:, j0:j0 + 8, :].rearrange("p a k -> p (a k)"), ALU.add,
                     [sb_, b_nab], [b_sc[d2]])
                S.red(st[:, r, 0:1], sc[d2][:], ALU.max, [b_sc[d2]], [b_st[r]], negate=True)
                S.act(pe_[d2][:], sc[d2][:], AF.Exp, [b_sc[d2], b_st[r]], [b_pe[d2], b_st[r]], bias=st[:, r, 0:1], accum=st[:, r, 1:2])

            def na_stage_b(r):
                d2 = r % 3
                S.op("dve", lambda e, o=st[:, r, 2:3], i_=st[:, r, 1:2]: e.reciprocal(out=o, in_=i_), [b_st[r]], [b_st[r]])
                S.ts("pool", pn[d2][:], pe_[d2][:], st[:, r, 2:3], ALU.mult, [b_pe[d2], b_st[r]], [b_pn[d2]], s2=1.0, op1=ALU.mult)
                tps, tb = S.psum[(2, 3, 7)[d2]], S.pbuf[(2, 3, 7)[d2]]
                tpsb = tps.bitcast(BF16)
                for c_ in range(4):
                    S.tr(tpsb[:, c_ * 128:(c_ + 1) * 128], pn[d2][:, c_ * 128:(c_ + 1) * 128], identb[:], [b_pn[d2], b_id], [tb])
                S.copy("act", pt[d2][:], tpsb[:, 0:512].rearrange("p (a k) -> p a k", a=4), [tb], [b_pt[d2]])

            def na_stage_c(r):
                d2 = r % 3
                ks = _row_start(r) * 64
                ob = 4 + (r // 8) % 2
                ops_, opb = S.psum[ob], S.pbuf[ob]
                col = (r % 8) * 64
                for hh in range(2):
                    lo, hi = hh * 64, (hh + 1) * 64
                    for c_ in range(4):
                        tok0 = ks + 128 * c_
                        if tok0 % 128 == 0:
                            vt, bv, ti = ve, b_ve, tok0 // 128
                        else:
                            vt, bv, ti = vo, b_vo, (tok0 - 64) // 128
                        S.mm(ops_[lo:hi, col:col + 64], vt[:, ti, lo:hi], pt[d2][:, c_, lo:hi], c_ == 0, c_ == 3, [bv, b_pt[d2]], [opb])
                if r % 8 == 7:
                    r0 = (r // 8) * 8
                    S.tt("dve", on[:, r0 * 64:(r0 + 8) * 64], ops_[:, :], gn[:, r0 * 64:(r0 + 8) * 64], ALU.mult, [opb, b_gn], [b_on])

            na_stage_qk(0)
            na_stage_qk(1)
            for t in range(64 + 2):
                if t + 2 < 64:
                    na_stage_qk(t + 2)
                if t < 64:
                    na_stage_a(t)
                if 0 <= t - 1 < 64:
                    na_stage_b(t - 1)
                if 0 <= t - 2 < 64:
                    na_stage_c(t - 2)
            store_o(on, b_on, 1)
        S.barrier()

        S.scope = "ret"
        with ExitStack() as es:
            qp = S.sb(es, "r_qp", [128, SEQ], BF16)
            kp = S.sb(es, "r_kp", [128, SEQ], BF16)
            b_qp, b_kp = Buf(), Buf()
            with ExitStack() as es1:
                rin = S.sb(es1, "r_in", [128, SEQ], BF16)
                b_rin = Buf()
                rcos = S.sb(es1, "r_cos", [128, SEQ], F32)
                rsin = S.sb(es1, "r_sin", [128, SEQ], F32)
                b_tab = Buf()
                for h in range(4):
                    S.dma("sp", rcos[:, h * 1024:(h + 1) * 1024], io["rcos"][:, h * 1024:(h + 1) * 1024], (), [b_tab])
                    S.dma("act", rsin[:, h * 1024:(h + 1) * 1024], io["rsin"][:, h * 1024:(h + 1) * 1024], (), [b_tab])
                perm = S.sb(es1, "r_perm", [128, 128], BF16)
                b_perm = Buf()
                S.dma("act", perm[:], io["perm"], (), [b_perm])
                t1 = [S.sb(es1, "r_t1%d" % i, [128, 512], F32) for i in range(2)]
                t2_ = [S.sb(es1, "r_t2%d" % i, [128, 512], F32) for i in range(2)]
                b_t1 = [Buf() for _ in range(2)]
                b_t2 = [Buf() for _ in range(2)]
                for blk, dst, bdst in ((B_RQ, qp, b_qp), (B_RK, kp, b_kp)):
                    for i in range(4):
                        S.dma("sp", rin[:, i * 1024:(i + 1) * 1024], recv1[i, blk, :].rearrange("(p t) -> p t", p=128), (), [b_rin])
                    for c_ in range(8):
                        d2 = c_ % 2
                        sl = slice(c_ * 512, (c_ + 1) * 512)
                        ps, pb = S.psum[d2], S.pbuf[d2]
                        S.mm(ps[:, :], perm[:], rin[:, sl], True, True, [b_perm, b_rin], [pb])
                        S.tt("dve", t1[d2][:], ps[:, :], rsin[:, sl], ALU.mult, [pb, b_tab], [b_t1[d2]])
                        S.tt("pool", t2_[d2][:], rin[:, sl], rcos[:, sl], ALU.mult, [b_rin, b_tab], [b_t2[d2]])
                        S.tt("dve", dst[:, sl], t1[d2][:], t2_[d2][:], ALU.add, [b_t1[d2], b_t2[d2]], [bdst])
            S.barrier()
            lg = S.sb(es, "r_lg", [128, 2], F32)
            lg2 = S.sb(es, "r_lg2", [128, 4], F32)
            b_lg = Buf()
            S.dma("sp", lg[:], io["retlg"], (), [b_lg])
            S.dma("sp", lg2[:], io["retlg2"], (), [b_lg])
            for t_ in (lg, lg2):
                S.act(t_[:], t_[:], AF.Exp, [b_lg], [b_lg], scale=-1.0)
                S.act(t_[:], t_[:], AF.Ln, [b_lg], [b_lg], bias=1.0)
                S.ts("dve", t_[:], t_[:], -1.0, ALU.mult, [b_lg], [b_lg])
            rtab = S.sb(es, "r_tab", [128, 4, 128], F32)
            cexp = S.sb(es, "r_cexp", [128, 2], F32)
            b_rt = Buf()
            S.dma("sp", rtab[:], io["rtab"], (), [b_rt])
            S.dma("sp", cexp[:], io["cexp"], (), [b_rt])
            DT = S.sb(es, "r_DT", [128, 2, 128], F32)
            tmpD = S.sb(es, "r_tmpD", [128, 128], F32)
            b_DT, b_tmpD = Buf(), Buf()
            for h in range(2):
                S.act(DT[:, h, :], rtab[:, 0, :], AF.Exp, [b_rt, b_lg], [b_DT], scale=lg2[:, h:h + 1])
                S.act(tmpD[:], rtab[:, 1, :], AF.Exp, [b_rt, b_lg], [b_tmpD], scale=lg2[:, 2 + h:3 + h])
                S.tt("dve", DT[:, h, :], DT[:, h, :], tmpD[:], ALU.add, [b_DT, b_tmpD], [b_DT])
            qdec = S.sb(es, "r_qdec", [128, 2, 128], F32)
            kd = S.sb(es, "r_kd", [128, 4], F32)
            cdec = S.sb(es, "r_cdec", [128, 2], F32)
            b_dec = Buf()
            for dr in range(2):
                S.act(qdec[:, dr, :], rtab[:, 2 + dr, :], AF.Exp, [b_rt, b_lg], [b_dec], scale=lg[:, dr:dr + 1])
                for h in range(2):
                    c_ = dr * 2 + h
                    S.act(kd[:, c_:c_ + 1], cexp[:, dr:dr + 1], AF.Exp, [b_rt, b_lg], [b_dec], scale=lg2[:, c_:c_ + 1])
            S.act(cdec[:], lg[:], AF.Exp, [b_lg], [b_dec], scale=128.0)
            qf = S.sb(es, "r_qf", [128, SEQ], BF16)
            qb = S.sb(es, "r_qb", [128, SEQ], BF16)
            b_qf, b_qb = Buf(), Buf()
            qp3 = qp[:].rearrange("p (n i) -> p n i", i=128)
            S.tt("dve", qf[:].rearrange("p (n i) -> p n i", i=128), qp3, bc(qdec[:, 0, :], 1, 32), ALU.mult, [b_qp, b_dec], [b_qf])
            S.tt("pool", qb[:].rearrange("p (n i) -> p n i", i=128), qp3, bc(qdec[:, 1, :], 1, 32), ALU.mult, [b_qp, b_dec], [b_qb])
            ktok = S.sb(es, "r_ktok", [128, 32, 128], BF16)
            b_ktok = Buf()
            for g in range(8):
                ps, pb = S.psum[g % 2], S.pbuf[g % 2]
                psb_ = ps.bitcast(BF16)
                for q in range(4):
                    n = g * 4 + q
                    S.tr(psb_[:, q * 128:(q + 1) * 128], kp[:, n * 128:(n + 1) * 128], identb[:], [b_kp, b_id], [pb])
                S.copy("act" if g % 2 else "dve", ktok[:, g * 4:(g + 1) * 4, :], psb_[:, 0:512].rearrange("p (a k) -> p a k", a=4), [pb], [b_ktok])
            vr, b_vr = load_tm(es, "r_v", B_RV)
            rg, b_rg = load_tm(es, "r_g", B_RG)
            S.act(rg[:], rg[:], AF.Silu, [b_rg], [b_rg])
            vf = S.sb(es, "r_vf", [128, 32, 128], BF16)
            vb = S.sb(es, "r_vb", [128, 32, 128], BF16)
            b_vf, b_vb = Buf(), Buf()
            for h in range(2):
                sl = slice(h * 64, (h + 1) * 64)
                S.ts("dve", vf[:, :, sl], vr[:, :, sl], kd[:, h:h + 1], ALU.mult, [b_vr, b_dec], [b_vf])
                S.ts("pool", vb[:, :, sl], vr[:, :, sl], kd[:, 2 + h:3 + h], ALU.mult, [b_vr, b_dec], [b_vb], s2=1.0, op1=ALU.mult)
            sf = S.sb(es, "r_sf", [128, 32, 64], F32)
            sbk = S.sb(es, "r_sb", [128, 32, 64], F32)
            b_sf, b_sbk = Buf(), Buf()
            S.memset("pool", sf[:, 0, :], 0.0, [b_sf])
            S.memset("pool", sbk[:, 31, :], 0.0, [b_sbk])
            for dr in range(2):
                vt, bvt = (vf, b_vf) if dr == 0 else (vb, b_vb)
                st_, bst_ = (sf, b_sf) if dr == 0 else (sbk, b_sbk)
                order = list(range(0, 31)) if dr == 0 else list(range(31, 0, -1))
                for g0 in range(0, 31, 8):
                    grp = order[g0:g0 + 8]
                    bi = 2 + (g0 // 8) % 2
                    ps, pb = S.psum[bi], S.pbuf[bi]
                    for q, n in enumerate(grp):
                        for h in range(2):
                            sl = slice(h * 64, (h + 1) * 64)
                            S.mm(ps[sl, q * 64:(q + 1) * 64], ktok[:, n, sl], vt[:, n, sl], True, True, [b_ktok, bvt], [pb])
                    for q, n in enumerate(grp):
                        nxt = n + 1 if dr == 0 else n - 1
                        S.stt(st_[:, nxt, :], st_[:, n, :], cdec[:, dr:dr + 1], ps[:, q * 64:(q + 1) * 64], ALU.mult, ALU.add,
                              [bst_, pb, b_dec], [bst_])
            sfb = S.sb(es, "r_sfb", [128, 32, 64], BF16)
            sbb = S.sb(es, "r_sbb", [128, 32, 64], BF16)
            b_sfb, b_sbb = Buf(), Buf()
            S.copy("act", sfb[:], sf[:], [b_sf], [b_sfb])
            S.copy("act", sbb[:], sbk[:], [b_sbk], [b_sbb])
            orT = S.sb(es, "r_o", [128, SEQ], BF16)
            b_or = Buf()
            ad = [S.sb(es, "r_ad%d" % i, [128, 4, 128], BF16) for i in range(2)]
            b_ad = [Buf() for _ in range(2)]
            osb = [S.sb(es, "r_osb%d" % i, [128, 4, 2, 64], F32) for i in range(2)]
            osq = [S.sb(es, "r_osq%d" % i, [128, 4, 2, 64], F32) for i in range(2)]
            onb = [S.sb(es, "r_onb%d" % i, [128, 4, 128], BF16) for i in range(2)]
            b_osb = [Buf() for _ in range(2)]
            b_osq = [Buf() for _ in range(2)]
            b_onb = [Buf() for _ in range(2)]
            rst = S.sb(es, "r_rst", [128, 8, 8], F32)
            b_rst = [Buf() for _ in range(8)]
            def ret_stage1(g):
                d2 = g % 2
                for h in range(2):
                    sl = slice(h * 64, (h + 1) * 64)
                    aps, apb = S.psum[h], S.pbuf[h]
                    for cn in range(4):
                        n = g * 4 + cn
                        S.mm(aps[:, cn * 128:(cn + 1) * 128], kp[sl, n * 128:(n + 1) * 128],
                             qp[sl, n * 128:(n + 1) * 128], True, True, [b_kp, b_qp], [apb])
                    S.tt("dve", ad[h][:], aps[:, :].rearrange("p (c i) -> p c i", c=4), bc(DT[:, h, :], 1, 4), ALU.mult,
                         [apb, b_DT], [b_ad[h]])
                    ops_, opb = S.psum[2 + d2 * 2 + h], S.pbuf[2 + d2 * 2 + h]
                    for cn in range(4):
                        n = g * 4 + cn
                        oc = cn * 64
                        S.mm(ops_[:, oc:oc + 64], ad[h][:, cn, :], vr[:, n, sl], True, False, [b_ad[h], b_vr], [opb])
                        if n > 0:
                            S.mm(ops_[:, oc:oc + 64], qf[sl, n * 128:(n + 1) * 128], sfb[sl, n, :], False, n == 31,
                                 [b_qf, b_sfb], [opb])
                        if n < 31:
                            S.mm(ops_[:, oc:oc + 64], qb[sl, n * 128:(n + 1) * 128], sbb[sl, n, :], False, True,
                                 [b_qb, b_sbb], [opb])
                    S.copy("act", osb[d2][:, :, h, :], ops_[:, 0:256].rearrange("p (c e) -> p c e", e=64), [opb], [b_osb[d2]])

            def ret_stage2(g):
                d2 = g % 2
                S.tt("pool", osq[d2][:], osb[d2][:], osb[d2][:], ALU.mult, [b_osb[d2]], [b_osq[d2]])
                S.red(rst[:, g, :], osq[d2][:].rearrange("p c h e -> p (c h) e"), ALU.add, [b_osq[d2]], [b_rst[g]])
                S.ts("dve", rst[:, g, :], rst[:, g, :], 1.0 / 64.0, ALU.mult, [b_rst[g]], [b_rst[g]], s2=EPS, op1=ALU.add)
                S.act(rst[:, g, :], rst[:, g, :], AF.Sqrt, [b_rst[g]], [b_rst[g]])
                S.op("dve", lambda e, o=rst[:, g, :]: e.reciprocal(out=o, in_=o), [b_rst[g]], [b_rst[g]])
                S.tt("dve", osb[d2][:].rearrange("p c h e -> p (c h) e"), osb[d2][:].rearrange("p c h e -> p (c h) e"),
                     bc(rst[:, g, :], 2, 64), ALU.mult, [b_osb[d2], b_rst[g]], [b_osb[d2]])
                S.tt("pool", onb[d2][:], osb[d2][:].rearrange("p c h e -> p c (h e)"), rg[:, g * 4:(g + 1) * 4, :], ALU.mult,
                     [b_osb[d2], b_rg], [b_onb[d2]])
                tps, tb = S.psum[6 + d2], S.pbuf[6 + d2]
                tpsb = tps.bitcast(BF16)
                for cn in range(4):
                    S.tr(tpsb[:, cn * 128:(cn + 1) * 128], onb[d2][:, cn, :], identb[:], [b_onb[d2], b_id], [tb])
                S.copy("act", orT[:, g * 512:(g + 1) * 512], tpsb[:, 0:512], [tb], [b_or])

            ret_stage1(0)
            for g in range(8):
                if g + 1 < 8:
                    ret_stage1(g + 1)
                ret_stage2(g)
            store_o(orT, b_or, 2)
        S.barrier()

    S.barrier()


def emit_C(S, io, last):
    nc = S.nc
    recv2 = io["recv2"]
    with ExitStack() as es:
        oT = S.sb(es, "c_oT", [128, 16, NTOK], BF16)
        b_oT = [Buf() for _ in range(16)]
        wo = S.sb(es, "c_wo", [128, 16, D], BF16)
        b_wo = [Buf() for _ in range(4)]
        for n in range(4):
            S.dma("pool", wo[:, :, n * 512:(n + 1) * 512], io["w_out"][:, n * 512:(n + 1) * 512].rearrange("(k p) c -> p k c", p=128),
                  (), [b_wo[n]])
        gate = S.sb(es, "c_gate", [128, D], F32)
        b_gate = Buf()
        S.dma("sp", gate[:], bc(io["gate"], 0, 128), (), [b_gate])
        if last:
            fg = S.sb(es, "c_fg", [128, D], F32)
            b_fgn = Buf()
            S.dma("sp", fg[:], bc(io["final_g"], 0, 128), (), [b_fgn])
        es1 = es
        if True:
            yc = S.sb(es1, "c_yc", [128, 4, NTOK], BF16)
            b_yc = Buf()
            for j in range(4):
                S.dma("sp", yc[:, j, :], recv2[j, 3, :, :], (), [b_yc])
            cg = S.sb(es1, "c_cg", [128, 4, NTOK], BF16)
            b_cg = Buf()
            for j in range(4):
                S.dma("sp", cg[:, j, :], io["cvg"][j, :, :], (), [b_cg])
            for m in range(3):
                for j in range(4):
                    S.dma(("sp", "act")[j % 2], oT[:, m * 4 + j, :], recv2[j, m, :, :], (), [b_oT[m * 4 + j]])
            S.act(cg[:], cg[:], AF.Silu, [b_cg], [b_cg])
            wpw = S.sb(es1, "c_wpw", [128, 4, 512], BF16)
            b_wpw = Buf()
            S.dma("pool", wpw[:], io["w_pw"].rearrange("(k p) c -> p k c", p=128), (), [b_wpw])
            lngb = S.sb(es1, "c_lngb", [128, 2, 4], F32)
            b_ln = Buf()
            S.dma("sp", lngb[:, 0, :], io["lng"].rearrange("(c p) -> p c", p=128), (), [b_ln], slow=True)
            S.dma("sp", lngb[:, 1, :], io["lnb"].rearrange("(c p) -> p c", p=128), (), [b_ln], slow=True)
            ones = S.sb(es1, "c_ones", [128, 128], BF16)
            b_ones = Buf()
            S.memset("pool", ones[:], 1.0 / 512.0, [b_ones])
            ysq = S.sb(es1, "c_ysq", [128, 4, NTOK], BF16)
            b_ysq = Buf()
            S.act(ysq[:], yc[:], AF.Square, [b_yc], [b_ysq])
            sT = S.sb(es1, "c_sT", [128, 4, NTOK], BF16)
            b_sT = [Buf() for _ in range(2)]
            msq = S.sb(es1, "c_msq", [128, 512], F32)
            rstd = S.sb(es1, "c_rstd", [128, 512], F32)
            b_msq, b_rstd = Buf(), Buf()
            dtmp = [S.sb(es1, "c_d%d" % i, [128, 512], F32) for i in range(2)]
            b_dt = [Buf() for _ in range(2)]
            def ln_half(h):
                hs = slice(h * 512, (h + 1) * 512)
                pm, bpm = S.psum[0], S.pbuf[0]
                pq_, bpq = S.psum[1], S.pbuf[1]
                for j in range(4):
                    S.mm(pm[:, :], ones[:], yc[:, j, hs], j == 0, j == 3, [b_ones, b_yc], [bpm])
                for j in range(4):
                    S.mm(pq_[:, :], ones[:], ysq[:, j, hs], j == 0, j == 3, [b_ones, b_ysq], [bpq])
                S.act(msq[:], pm[:, :], AF.Square, [bpm], [b_msq])
                S.tt("dve", rstd[:], pq_[:, :], msq[:], ALU.subtract, [bpq, b_msq], [b_rstd])
                S.act(rstd[:], rstd[:], AF.Ln, [b_rstd], [b_rstd], bias=EPS)
                S.act(rstd[:], rstd[:], AF.Exp, [b_rstd], [b_rstd], scale=-0.5)
                for j in range(4):
                    d2 = j % 2
                    S.tt("dve", dtmp[d2][:], yc[:, j, hs], pm[:, :], ALU.subtract, [b_yc, bpm], [b_dt[d2]])
                    S.tt("pool", dtmp[d2][:], dtmp[d2][:], rstd[:], ALU.mult, [b_dt[d2], b_rstd], [b_dt[d2]])
                    S.ts("dve", dtmp[d2][:], dtmp[d2][:], lngb[:, 0, j:j + 1], ALU.mult, [b_dt[d2], b_ln], [b_dt[d2]],
                         s2=lngb[:, 1, j:j + 1], op1=ALU.add)
                    S.act(sT[:, j, hs], dtmp[d2][:], AF.Silu, [b_dt[d2]], [b_sT[h]])
            def pw_half(h):
                hs = slice(h * 512, (h + 1) * 512)
                for co in range(4):
                    ps, pb = S.psum[2 + co % 2], S.pbuf[2 + co % 2]
                    for ci in range(4):
                        S.mm(ps[:, :], wpw[:, ci, co * 128:(co + 1) * 128], sT[:, ci, hs], ci == 0, ci == 3, [b_wpw, b_sT[h]], [pb])
                    S.tt("dve", oT[:, 12 + co, hs], ps[:, :], cg[:, co, hs], ALU.mult, [pb, b_cg], [b_oT[12 + co]])
        xt = [S.sb(es, "c_x%d" % i, [128, D], F32) for i in range(2)]
        b_xt = [Buf() for _ in range(2)]
        tmp = [S.sb(es, "c_tmp%d" % i, [128, 512], F32) for i in range(2)]
        b_tmp = [Buf() for _ in range(2)]
        junk = S.sb(es, "c_junk", [128, D], F32)
        b_junk = Buf()
        ss = S.sb(es, "c_ss", [128, 8], F32)
        b_ss = [Buf() for _ in range(8)]
        outs = []
        def outproj_tile(t):
            d2 = t % 2
            S.dma("sp", xt[d2][:], io["x"][t * 128:(t + 1) * 128, :], (), [b_xt[d2]])
            for n in range(4):
                pi = 4 + (t * 4 + n) % 4
                ps, pb = S.psum[pi], S.pbuf[pi]
                for k in range(16):
                    S.mm(ps[:, :], oT[:, k, t * 128:(t + 1) * 128], wo[:, k, n * 512:(n + 1) * 512], k == 0, k == 15,
                         [b_oT[k], b_wo[n]], [pb])
                ti = (t * 4 + n) % 2
                S.tt("dve", tmp[ti][:], ps[:, :], gate[:, n * 512:(n + 1) * 512], ALU.mult, [pb, b_gate], [b_tmp[ti]])
                S.tt("pool", xt[d2][:, n * 512:(n + 1) * 512], xt[d2][:, n * 512:(n + 1) * 512], tmp[ti][:], ALU.add,
                     [b_xt[d2], b_tmp[ti]], [b_xt[d2]])
            if last:
                S.act(junk[:], xt[d2][:], AF.Square, [b_xt[d2]], [b_junk, b_ss[t]], accum=ss[:, t:t + 1])
                S.ts("dve", ss[:, t:t + 1], ss[:, t:t + 1], 1.0 / D, ALU.mult, [b_ss[t]], [b_ss[t]], s2=EPS, op1=ALU.add)
                S.act(ss[:, t:t + 1], ss[:, t:t + 1], AF.Sqrt, [b_ss[t]], [b_ss[t]])
                S.op("dve", lambda e, o=ss[:, t:t + 1]: e.reciprocal(out=o, in_=o), [b_ss[t]], [b_ss[t]])
                S.stt(xt[d2][:], xt[d2][:], ss[:, t:t + 1], fg[:], ALU.mult, ALU.mult, [b_xt[d2], b_ss[t], b_fgn], [b_xt[d2]])
            ob = Buf()
            outs.append(ob)
            S.dma("sp", io["xout"][t * 128:(t + 1) * 128, :], xt[d2][:], [b_xt[d2]], [ob])

        ln_half(0)
        pw_half(0)
        ln_half(1)
        for t in range(4):
            outproj_tile(t)
        pw_half(1)
        for t in range(4, 8):
            outproj_tile(t)
    S.barrier()


_PROGS = {}


CONST_BF16 = ("m1", "m3", "perm")


def _const_io(nc, io, names):
    for n in names:
        io[n] = dram_in(nc, n, CONST_SHAPES[n], BF16 if n in CONST_BF16 else F32)


def prog_mod():
    if "mod" in _PROGS:
        return _PROGS["mod"]
    nc = bass.Bass("TRN2", target_bir_lowering=False)
    io = {"cT": dram_in(nc, "cT", [128, 16, 2]), "wada": dram_in(nc, "wada", [D, 1536]),
          "bada": dram_in(nc, "bada", [2, 1536]), "mod": dram_out(nc, "mod", [2, 1536])}
    with ExitStack() as es:
        S = Sched(nc, es)
        emit_mod(S, io)
        S.run()
    _PROGS["mod"] = nc
    return nc


def prog_A():
    if "A" in _PROGS:
        return _PROGS["A"]
    nc = bass.Bass("TRN2", target_bir_lowering=False)
    io = {"x": dram_in(nc, "x", [NTOK, D]), "shift": dram_in(nc, "shift", [D]), "scale": dram_in(nc, "scale", [D]),
          "norm_g": dram_in(nc, "norm_g", [D]), "w_in": dram_in(nc, "w_in", [D, DIN]), "w_fft": dram_in(nc, "w_fft", [512, 512]),
          "send1": dram_out(nc, "send1", [4, NBLK1, BLK], BF16), "cvg": dram_out(nc, "cvg", [4, 128, NTOK], BF16)}
    _const_io(nc, io, ["ident", "ccsc"])
    with ExitStack() as es:
        S = Sched(nc, es)
        emit_A(S, io)
        S.run()
    _PROGS["A"] = nc
    return nc


B_CONSTS = ["ident", "m1", "m3", "namask", "rcos", "rsin", "perm", "rtab", "cexp"]


def prog_B():
    if "B" in _PROGS:
        return _PROGS["B"]
    nc = bass.Bass("TRN2", target_bir_lowering=False)
    io = {"recv1": dram_in(nc, "recv1", [4, NBLK1, BLK], BF16), "nab": dram_in(nc, "nab", [128, 15, 64]),
          "retlg": dram_in(nc, "retlg", [128, 2]), "retlg2": dram_in(nc, "retlg2", [128, 4]),
          "cw": dram_in(nc, "cw", [128, 31]), "cb": dram_in(nc, "cb", [128, 1]),
          "fftscr": dram_tmp(nc, "fftscr", [128 * 64 * 128], BF16),
          "send2": dram_out(nc, "send2", [4, 4, 128, NTOK], BF16)}
    _const_io(nc, io, B_CONSTS)
    with ExitStack() as es:
        S = Sched(nc, es)
        emit_B(S, io)
        S.run()
    _PROGS["B"] = nc
    return nc


def prog_C(last):
    key = "C%d" % int(last)
    if key in _PROGS:
        return _PROGS[key]
    nc = bass.Bass("TRN2", target_bir_lowering=False)
    io = {"recv2": dram_in(nc, "recv2", [4, 4, 128, NTOK], BF16), "cvg": dram_in(nc, "cvg", [4, 128, NTOK], BF16),
          "x": dram_in(nc, "x", [NTOK, D]), "gate": dram_in(nc, "gate", [D]), "w_out": dram_in(nc, "w_out", [D, D]),
          "w_pw": dram_in(nc, "w_pw", [512, 512]), "lng": dram_in(nc, "lng", [512]), "lnb": dram_in(nc, "lnb", [512]),
          "xout": dram_out(nc, "xout", [NTOK, D])}
    if last:
        io["final_g"] = dram_in(nc, "final_g", [D])
    with ExitStack() as es:
        S = Sched(nc, es)
        emit_C(S, io, last)
        S.run()
    _PROGS[key] = nc
    return nc


def prog_CA():
    if "CA" in _PROGS:
        return _PROGS["CA"]
    nc = bass.Bass("TRN2", target_bir_lowering=False)
    xmid = dram_tmp(nc, "xmid", [NTOK, D])
    ioc = {"recv2": dram_in(nc, "recv2", [4, 4, 128, NTOK], BF16), "cvg": dram_in(nc, "cvg", [4, 128, NTOK], BF16),
           "x": dram_in(nc, "x", [NTOK, D]), "gate": dram_in(nc, "gate", [D]), "w_out": dram_in(nc, "w_out", [D, D]),
           "w_pw": dram_in(nc, "w_pw", [512, 512]), "lng": dram_in(nc, "lng", [512]), "lnb": dram_in(nc, "lnb", [512]),
           "xout": xmid}
    ioa = {"x": xmid, "shift": dram_in(nc, "shift", [D]), "scale": dram_in(nc, "scale", [D]),
           "norm_g": dram_in(nc, "norm_g", [D]), "w_in": dram_in(nc, "w_in", [D, DIN]), "w_fft": dram_in(nc, "w_fft", [512, 512]),
           "send1": dram_out(nc, "send1", [4, NBLK1, BLK], BF16), "cvg": dram_out(nc, "cvg_next", [4, 128, NTOK], BF16)}
    _const_io(nc, ioa, ["ident", "ccsc"])
    xcopy = dram_out(nc, "xout", [NTOK, D])
    with ExitStack() as es:
        S = Sched(nc, es)
        emit_C(S, ioc, False)
        bx = Buf()
        S.dma("sp", xcopy, xmid, (), [bx])
        emit_A(S, ioa)
        S.run()
    _PROGS["CA"] = nc
    return nc


def _run(nc, in_maps):
    res = run_bass_kernel_spmd(nc, in_maps, core_ids=list(range(NCORE)))
    return res.results


def _percore_layer_inputs(l, na_rel_bias, ret_logit_fwd, ret_logit_bwd, conv_w, conv_b):
    col = np.arange(64)
    idx = np.clip(col[None, :] - col[:, None] + 15, 0, 30)
    outs = []
    for j in range(4):
        nab = np.empty((128, 15, 64), np.float32)
        for hh in range(2):
            rb = na_rel_bias[l, 2 * j + hh]
            nab[hh * 64:(hh + 1) * 64] = np.transpose(rb[:, idx], (1, 0, 2))
        lf = ret_logit_fwd[l, 2 * j:2 * j + 2]
        lb = ret_logit_bwd[l, 2 * j:2 * j + 2]
        retlg = np.empty((128, 2), np.float32)
        retlg[0:64, 0], retlg[64:128, 0] = lf[0], lf[1]
        retlg[0:64, 1], retlg[64:128, 1] = lb[0], lb[1]
        retlg2 = np.empty((128, 4), np.float32)
        retlg2[:, 0], retlg2[:, 1], retlg2[:, 2], retlg2[:, 3] = lf[0], lf[1], lb[0], lb[1]
        cw = np.ascontiguousarray(conv_w[l][:, j * 128:(j + 1) * 128].T)
        cb = np.ascontiguousarray(conv_b[l][j * 128:(j + 1) * 128, None])
        outs.append(dict(nab=nab, retlg=retlg, retlg2=retlg2, cw=cw, cb=cb))
    return outs


def kernel(x, c, norm_g, w_ada, b_ada, w_in, w_fft, na_rel_bias, ret_logit_fwd, ret_logit_bwd,
           conv_w, conv_b, conv_ln_g, conv_ln_b, conv_w_pw, w_out, final_g, _debug=None):
    f32 = np.float32
    x = np.asarray(x, f32)
    C = consts()
    cT = np.ascontiguousarray(np.asarray(c, f32).T.reshape(16, 128, NB).transpose(1, 0, 2))
    maps = []
    for k in range(NCORE):
        l, c0 = k // 4, (k % 4) * 1536
        maps.append(dict(cT=cT, wada=np.ascontiguousarray(w_ada[l][:, c0:c0 + 1536]),
                         bada=np.ascontiguousarray(np.broadcast_to(b_ada[l][None, c0:c0 + 1536], (2, 1536)))))
    r = _run(prog_mod(), maps)
    mod = np.stack([np.concatenate([np.asarray(r[l * 4 + q]["mod"]) for q in range(4)], axis=1) for l in range(DEPTH)])
    xs = [np.ascontiguousarray(x[k // 4, (k % 4) * NTOK:(k % 4 + 1) * NTOK]) for k in range(NCORE)]

    def a_inputs(l, k):
        b = k // 4
        return dict(shift=np.ascontiguousarray(mod[l, b, 0:D]), scale=np.ascontiguousarray(mod[l, b, D:2 * D]),
                    norm_g=norm_g[l], w_in=w_in[l], w_fft=w_fft[l], ident=C["ident"], ccsc=C["ccsc"])

    send1 = cvg = None
    for l in range(DEPTH):
        last = (l == DEPTH - 1)
        if l == 0:
            rA = _run(prog_A(), [dict(x=xs[k], **a_inputs(0, k)) for k in range(NCORE)])
            send1 = [np.asarray(rA[k]["send1"]) for k in range(NCORE)]
            cvg = [np.asarray(rA[k]["cvg"]) for k in range(NCORE)]
        pl = _percore_layer_inputs(l, na_rel_bias, ret_logit_fwd, ret_logit_bwd, conv_w, conv_b)
        maps = []
        for k in range(NCORE):
            b, j = k // 4, k % 4
            recv1 = np.stack([send1[b * 4 + i][j] for i in range(4)])
            m = dict(recv1=recv1, **pl[j])
            for n in B_CONSTS:
                m[n] = C[n]
            maps.append(m)
        rB = _run(prog_B(), maps)
        send2 = [np.asarray(rB[k]["send2"]) for k in range(NCORE)]
        maps = []
        for k in range(NCORE):
            b, i = k // 4, k % 4
            recv2 = np.stack([send2[b * 4 + j][i] for j in range(4)])
            m = dict(recv2=recv2, cvg=cvg[k], x=xs[k], gate=np.ascontiguousarray(mod[l, b, 2 * D:3 * D]), w_out=w_out[l],
                     w_pw=conv_w_pw[l], lng=conv_ln_g[l], lnb=conv_ln_b[l])
            if last:
                m["final_g"] = final_g
            else:
                m.update(a_inputs(l + 1, k))
            maps.append(m)
        if last:
            rC = _run(prog_C(True), maps)
        else:
            rC = _run(prog_CA(), maps)
            send1 = [np.asarray(rC[k]["send1"]) for k in range(NCORE)]
            cvg = [np.asarray(rC[k]["cvg_next"]) for k in range(NCORE)]
        xs = [np.asarray(rC[k]["xout"]) for k in range(NCORE)]
    out = np.empty((NB, SEQ, D), f32)
    for k in range(NCORE):
        out[k // 4, (k % 4) * NTOK:(k % 4 + 1) * NTOK] = xs[k]
    return out
```

```python
import numpy as np
import ml_dtypes
from contextlib import ExitStack
import concourse.bass as bass
import concourse.mybir as mybir
from concourse.bass_utils import run_bass_kernel_spmd

F32 = mybir.dt.float32
BF16 = mybir.dt.bfloat16
AF = mybir.ActivationFunctionType
ALU = mybir.AluOpType
AX = mybir.AxisListType

D = 2048
SEQ = 4096
NB = 2
DEPTH = 2
DIN = 6656
NTOK = 1024
NCORE = 8
EPS = 1e-6
NBLK1 = 13
BLK = 128 * 1024
B_P, B_Q, B_FG, B_NQ, B_NK, B_NV, B_NG, B_RQ, B_RK, B_RV, B_RG, B_CA, B_CB = range(13)


PROFILE_SCOPES = False


class Buf:
    __slots__ = ("name", "w", "r")

    def __init__(self, name=""):
        self.name = name
        self.w = {}
        self.r = {}


class Sched:
    COMPUTE = ("pe", "act", "dve", "pool")
    NDMA = 12

    def __init__(self, nc, es):
        self.nc = nc
        self.es = es
        self.eng = {"pe": nc.tensor, "act": nc.scalar, "dve": nc.vector, "pool": nc.gpsimd, "sp": nc.sync}
        self.streams = {e: [] for e in self.eng}
        self.sems = {}
        self.cnt = {}
        self.waited = {e: {} for e in self.eng}
        for e in self.COMPUTE:
            self.sems[e] = es.enter_context(nc.semaphore("s_" + e))
            self.cnt[e] = 0
        self.dq = {}
        for q in ("sp", "pool", "act"):
            lst = []
            for i in range(self.NDMA):
                k = "d_%s_%d" % (q, i)
                self.sems[k] = es.enter_context(nc.semaphore(k))
                self.cnt[k] = 0
                lst.append(k)
            self.dq[q] = [lst, 0]
        self.scope = None
        self.psum = [es.enter_context(nc.psum_tensor("psb%d" % i, [128, 512], F32)) for i in range(8)]
        self.pbuf = [Buf("ps%d" % i) for i in range(8)]

    def sb(self, es, name, shape, dt):
        return es.enter_context(self.nc.sbuf_tensor(name, list(shape), dt))

    def _deps(self, reads, writes, par=False):
        deps = {}

        def add(k, v):
            if deps.get(k, 0) < v:
                deps[k] = v
        for b in reads:
            for k, v in b.w.items():
                add(k, v)
        for b in writes:
            if not (par and not b.r):
                for k, v in b.w.items():
                    add(k, v)
            for k, v in b.r.items():
                add(k, v)
        return deps

    def _emit_waits(self, eng, deps):
        for k, v in deps.items():
            if eng == "pe" and k == "pe":
                continue
            if self.waited[eng].get(k, 0) < v:
                self.waited[eng][k] = v
                self.streams[eng].append(("wait", k, v))

    def _mark(self, tok, reads, writes, par=False):
        k, v = tok
        for b in reads:
            if b.r.get(k, 0) < v:
                b.r[k] = v
        for b in writes:
            if par and not b.r:
                if b.w.get(k, 0) < v:
                    b.w[k] = v
            else:
                b.w = {k: v}
                b.r = {}

    def op(self, eng, fn, reads=(), writes=()):
        self._emit_waits(eng, self._deps(reads, writes))
        self.cnt[eng] += 1
        tok = (eng, self.cnt[eng])
        self.streams[eng].append(("op", fn, eng, 1, self.scope))
        self._mark(tok, reads, writes)
        return tok

    def dma(self, q, out, in_, reads=(), writes=(), slow=False):
        lst, idx = self.dq[q]
        k = lst[idx % len(lst)]
        self.dq[q][1] = idx + 1
        deps = self._deps(reads, writes, par=True)
        if self.cnt[k] > 0:
            deps[k] = max(deps.get(k, 0), self.cnt[k])
        self._emit_waits(q, deps)
        self.cnt[k] += 16
        tok = (k, self.cnt[k])
        if slow:
            self.streams[q].append(("op", lambda e: e.dma_start(out=out, in_=in_, allow_slow_non_contiguous=True), k, 16, self.scope))
        else:
            self.streams[q].append(("op", lambda e: e.dma_start(out=out, in_=in_), k, 16, self.scope))
        self._mark(tok, reads, writes, par=True)
        return tok

    def coll(self, kind, ins, outs, groups, reads=(), writes=()):
        q = "pool"
        lst, idx = self.dq[q]
        k = lst[idx % len(lst)]
        self.dq[q][1] = idx + 1
        deps = self._deps(reads, writes)
        if self.cnt[k] > 0:
            deps[k] = max(deps.get(k, 0), self.cnt[k])
        self._emit_waits(q, deps)
        self.cnt[k] += 16
        tok = (k, self.cnt[k])
        self.streams[q].append(("op", lambda e: e.collective_compute(kind, ALU.bypass, replica_groups=groups, ins=ins, outs=outs), k, 16, self.scope))
        self._mark(tok, reads, writes)
        return tok

    def barrier(self):
        allv = {k: v for k, v in self.cnt.items() if v > 0}
        for e in self.eng:
            self._emit_waits(e, dict(allv))

    def run(self):
        nc = self.nc
        self.barrier()
        with nc.Block() as block:
            def make(ename):
                def body(e):
                    cur, ctx = None, None
                    for it in self.streams[ename]:
                        if it[0] == "wait":
                            e.wait_ge(self.sems[it[1]], it[2])
                        else:
                            _, fn, k, inc, sc = it
                            if PROFILE_SCOPES and sc != cur:
                                if ctx is not None:
                                    ctx.__exit__(None, None, None)
                                    ctx = None
                                if sc is not None:
                                    ctx = nc.named_scope(sc)
                                    ctx.__enter__()
                                cur = sc
                            fn(e).then_inc(self.sems[k], inc)
                    if ctx is not None:
                        ctx.__exit__(None, None, None)
                return body
            block.sync(make("sp"))
            block.tensor(make("pe"))
            block.scalar(make("act"))
            block.vector(make("dve"))
            block.gpsimd(make("pool"))

    def mm(self, out, lhsT, rhs, start, stop, r, w):
        return self.op("pe", lambda e: e.matmul(out, lhsT=lhsT, rhs=rhs, start=start, stop=stop), r, w)

    def tr(self, out, in_, ident, r, w):
        return self.op("pe", lambda e: e.transpose(out=out, in_=in_, identity=ident), r, w)

    def act(self, out, in_, func, r, w, bias=None, scale=None, accum=None):
        kw = {}
        if bias is not None:
            kw["bias"] = bias
        if scale is not None:
            kw["scale"] = scale
        if accum is not None:
            kw["accum_out"] = accum
        return self.op("act", lambda e: e.activation(out=out, in_=in_, func=func, **kw), r, w)

    def tt(self, eng, out, in0, in1, op, r, w):
        return self.op(eng, lambda e: e.tensor_tensor(out=out, in0=in0, in1=in1, op=op), r, w)

    def ts(self, eng, out, in0, s1, op0, r, w, s2=None, op1=None):
        if op1 is None:
            return self.op(eng, lambda e: e.tensor_scalar(out=out, in0=in0, scalar1=s1, scalar2=None, op0=op0), r, w)
        return self.op(eng, lambda e: e.tensor_scalar(out=out, in0=in0, scalar1=s1, scalar2=s2, op0=op0, op1=op1), r, w)

    def stt(self, out, in0, scalar, in1, op0, op1, r, w):
        return self.op("dve", lambda e: e.scalar_tensor_tensor(out=out, in0=in0, scalar=scalar, in1=in1, op0=op0, op1=op1), r, w)

    def copy(self, eng, out, in_, r, w):
        if eng == "act":
            return self.op("act", lambda e: e.copy(out=out, in_=in_), r, w)
        return self.op(eng, lambda e: e.tensor_copy(out=out, in_=in_), r, w)

    def red(self, out, in_, op, r, w, negate=False):
        return self.op("dve", lambda e: e.tensor_reduce(out=out, in_=in_, axis=AX.X, op=op, negate=negate), r, w)

    def memset(self, eng, ap, val, w):
        return self.op(eng, lambda e: e.memset(ap, val), (), w)


def bc(ap, pos, count):
    lst = [list(x) for x in ap.ap]
    lst.insert(pos, [0, count])
    return bass.AP(ap.tensor, ap.offset, lst)


def dram_in(nc, name, shape, dt=F32):
    return nc.dram_tensor(name, list(shape), dt, kind="ExternalInput").ap()


def dram_out(nc, name, shape, dt=F32):
    return nc.dram_tensor(name, list(shape), dt, kind="ExternalOutput").ap()


def dram_tmp(nc, name, shape, dt=F32):
    return nc.dram_tensor(name, list(shape), dt, kind="Internal").ap()


_CONST = None


def consts():
    global _CONST
    if _CONST is not None:
        return _CONST
    c = {}
    c["ident"] = np.eye(128, dtype=np.float32)
    i128 = np.arange(128, dtype=np.float64)
    ang = 2 * np.pi * np.outer(i128, i128) / 128.0
    ccsc = np.stack([np.cos(ang), np.sin(ang)], axis=1) / np.sqrt(128.0)
    c["ccsc"] = ccsc.astype(np.float32)
    s1 = np.arange(64)[:, None, None]
    s2 = np.arange(64)[None, :, None]
    k1 = np.arange(64)[None, None, :]
    th = 2 * np.pi * (k1 * s1 / 64.0 + k1 * s2 / 4096.0)
    m1 = np.zeros((128, 64, 128), np.float64)
    m1[0:64, :, 0:64] = np.cos(th)
    m1[0:64, :, 64:128] = np.sin(th)
    m1[64:128, :, 0:64] = -np.sin(th)
    m1[64:128, :, 64:128] = np.cos(th)
    c["m1"] = m1.astype(ml_dtypes.bfloat16)
    ph = 2 * np.pi * np.outer(np.arange(64), np.arange(64)) / 64.0
    m3 = np.concatenate([np.cos(ph), -np.sin(ph)], axis=0) / 64.0
    c["m3"] = m3.astype(ml_dtypes.bfloat16)
    col = np.arange(64)
    cs = np.clip(col - 8, 0, 48)
    rel = col[None, :] - cs[:, None]
    m = np.where((rel >= 0) & (rel < 16), 0.0, -30000.0).astype(np.float32)
    c["namask"] = np.concatenate([m, m], axis=0)
    half = 32
    inv = 10000.0 ** (-np.arange(half, dtype=np.float32) / half)
    angr = np.arange(SEQ, dtype=np.float32)[:, None] * inv[None, :]
    cosr = np.cos(angr.astype(np.float64)).T
    sinr = np.sin(angr.astype(np.float64)).T
    cos64 = np.concatenate([cosr, cosr], axis=0)
    sin64 = np.concatenate([-sinr, sinr], axis=0)
    c["rcos"] = np.concatenate([cos64, cos64], axis=0).astype(np.float32)
    c["rsin"] = np.concatenate([sin64, sin64], axis=0).astype(np.float32)
    pm = np.zeros((128, 128), np.float32)
    for mm_ in range(128):
        pm[mm_ ^ 32, mm_] = 1.0
    c["perm"] = pm.astype(ml_dtypes.bfloat16)
    j = np.arange(128)[:, None]
    i = np.arange(128)[None, :]
    BIG = 1.0e7
    distf = np.where(i >= j, (i - j), BIG).astype(np.float32)
    distb = np.where(j > i, (j - i), BIG).astype(np.float32)
    io1 = np.broadcast_to(np.arange(1, 129, dtype=np.float32)[None, :], (128, 128))
    io2 = np.broadcast_to((128 - np.arange(128, dtype=np.float32))[None, :], (128, 128))
    c["rtab"] = np.ascontiguousarray(np.stack([distf, distb, io1, io2], axis=1)).astype(np.float32)
    ce = np.stack([127 - np.arange(128), np.arange(128)], axis=1).astype(np.float32)
    c["cexp"] = ce
    _CONST = c
    return c


CONST_SHAPES = {"ident": [128, 128], "ccsc": [128, 2, 128], "m1": [128, 64, 128], "m3": [128, 64],
                "namask": [128, 64], "rcos": [128, SEQ], "rsin": [128, SEQ], "perm": [128, 128],
                "rtab": [128, 4, 128], "cexp": [128, 2]}


def emit_mod(S, io):
    nc = S.nc
    with ExitStack() as es:
        cT = S.sb(es, "m_cT", [128, 16, 2], F32)
        bcT = Buf()
        ba = S.sb(es, "m_ba", [2, 1536], F32)
        bba = Buf()
        res = S.sb(es, "m_res", [2, 1536], F32)
        bres = Buf()
        wt = [S.sb(es, "m_w%d" % i, [128, 16, 512], F32) for i in range(3)]
        bw = [Buf() for _ in range(3)]
        S.dma("sp", cT[:], io["cT"], (), [bcT])
        S.dma("sp", ba[:], io["bada"], (), [bba])
        for n in range(3):
            S.dma("sp", wt[n][:], io["wada"][:, n * 512:(n + 1) * 512].rearrange("(k p) c -> p k c", p=128), (), [bw[n]])
        S.act(cT[:], cT[:], AF.Silu, [bcT], [bcT])
        cR = S.sb(es, "m_cR", [128, 16, 64, 2], F32)
        bcR = Buf()
        S.copy("dve", cR[:], bc(cT[:], 2, 64), [bcT], [bcR])
        for n in range(3):
            ps = S.psum[n]
            pb = S.pbuf[n]
            for k in range(16):
                S.mm(ps[:, :], cR[:, k, :, :].rearrange("p r b -> p (r b)"), wt[n][:, k, :], k == 0, k == 15, [bcR, bw[n]], [pb])
            S.tt("dve", res[:, n * 512:(n + 1) * 512], ps[0:2, :], ba[:, n * 512:(n + 1) * 512], ALU.add, [pb, bba], [bres])
        S.dma("sp", io["mod"], res[:], [bres], [Buf()])
    S.barrier()


def emit_A(S, io, layer_tag=""):
    nc = S.nc
    with ExitStack() as es:
        ident = S.sb(es, "a_ident", [128, 128], F32)
        b_ident = Buf()
        S.dma("sp", ident[:], io["ident"], (), [b_ident])
        gT = S.sb(es, "a_gT", [128, 16], F32)
        scT = S.sb(es, "a_scT", [128, 16], F32)
        shT = S.sb(es, "a_shT", [128, 16], F32)
        gsT = S.sb(es, "a_gsT", [128, 16], F32)
        b_mod = Buf()
        b_gs = Buf()
        S.dma("sp", gT[:], io["norm_g"].rearrange("(c p) -> p c", p=128), (), [b_mod], slow=True)
        S.dma("sp", scT[:], io["scale"].rearrange("(c p) -> p c", p=128), (), [b_mod], slow=True)
        S.dma("sp", shT[:], io["shift"].rearrange("(c p) -> p c", p=128), (), [b_mod], slow=True)
        S.stt(gsT[:], scT[:], 1.0, gT[:], ALU.add, ALU.mult, [b_mod], [b_gs])

        ccsc = S.sb(es, "a_ccsc", [128, 2, 128], F32)
        b_cc = Buf()
        S.dma("sp", ccsc[:], io["ccsc"], (), [b_cc])
        wf = S.sb(es, "a_wf", [128, 4, 512], F32)
        b_wf = Buf()
        S.dma("sp", wf[:], io["w_fft"].rearrange("(g p) c -> p g c", p=128), (), [b_wf])
        mcs = S.sb(es, "a_mcs", [128, 2, 4, 512], BF16)
        b_mcs = Buf()
        for cs in range(2):
            for g in range(4):
                pi = (cs * 4 + g) % 4
                ps, pb = S.psum[pi], S.pbuf[pi]
                S.mm(ps[:, :], ccsc[:, cs, :], wf[:, g, :], True, True, [b_cc, b_wf], [pb])
                S.copy("act" if g % 2 else "dve", mcs[:, cs, g, :], ps[:, :], [pb], [b_mcs])

        hT = S.sb(es, "a_hT", [128, 16, NTOK], BF16)
        b_hT = [Buf() for _ in range(2)]
        xt = [S.sb(es, "a_x%d" % i, [128, D], F32) for i in range(4)]
        b_xt = [Buf() for _ in range(4)]
        junk = S.sb(es, "a_junk", [128, D], F32)
        b_junk = Buf()
        ss = S.sb(es, "a_ss", [128, 8], F32)
        rs = S.sb(es, "a_rs", [128, 8], F32)
        b_ss = [Buf() for _ in range(8)]
        for grp in range(2):
            for tt_ in range(4):
                t = grp * 4 + tt_
                S.dma("sp", xt[tt_][:], io["x"][t * 128:(t + 1) * 128, :], (), [b_xt[tt_]])
                S.act(junk[:], xt[tt_][:], AF.Square, [b_xt[tt_]], [b_junk, b_ss[t]], accum=ss[:, t:t + 1])
                S.ts("dve", rs[:, t:t + 1], ss[:, t:t + 1], 1.0 / D, ALU.mult, [b_ss[t]], [b_ss[t]], s2=EPS, op1=ALU.add)
                S.act(rs[:, t:t + 1], rs[:, t:t + 1], AF.Sqrt, [b_ss[t]], [b_ss[t]])
                S.op("dve", lambda e, o=rs[:, t:t + 1]: e.reciprocal(out=o, in_=o), [b_ss[t]], [b_ss[t]])
                S.ts("pool", xt[tt_][:], xt[tt_][:], rs[:, t:t + 1], ALU.mult, [b_xt[tt_], b_ss[t]], [b_xt[tt_]], s2=1.0, op1=ALU.mult)
            for k in range(16):
                pi = 4 + (k % 4)
                ps, pb = S.psum[pi], S.pbuf[pi]
                for tt_ in range(4):
                    S.tr(ps[:, tt_ * 128:(tt_ + 1) * 128], xt[tt_][:, k * 128:(k + 1) * 128], ident[:], [b_xt[tt_], b_ident], [pb])
                if k % 2:
                    S.ts("dve", hT[:, k, grp * 512:(grp + 1) * 512], ps[:, :], gsT[:, k:k + 1], ALU.mult,
                         [pb, b_gs, b_mod], [b_hT[grp]], s2=shT[:, k:k + 1], op1=ALU.add)
                else:
                    S.act(hT[:, k, grp * 512:(grp + 1) * 512], ps[:, :], AF.Identity, [pb, b_gs, b_mod], [b_hT[grp]],
                          bias=shT[:, k:k + 1], scale=gsT[:, k:k + 1])

        wb = [S.sb(es, "a_w%d" % i, [128, 16, 512], BF16) for i in range(3)]
        b_wb = [Buf() for _ in range(3)]
        st_fm = [S.sb(es, "a_sfm%d" % i, [128, NTOK], BF16) for i in range(2)]
        b_sfm = [Buf() for _ in range(2)]
        st_tm = [S.sb(es, "a_stm%d" % i, [128, 8, 512], BF16) for i in range(2)]
        b_stm = [Buf() for _ in range(2)]
        fxT = S.sb(es, "a_fxT", [128, 4, NTOK], BF16)
        b_fx = Buf()
        send1 = io["send1"]

        def wload(p):
            slot = p % 3
            S.dma("pool", wb[slot][:], io["w_in"][:, p * 512:(p + 1) * 512].rearrange("(k p) c -> p k c", p=128),
                  (), [b_wb[slot]])

        cnt = {"fm": 0, "tm": 0, "ev": 0, "ps": 0}

        def evac(out, in_, r, w, scale=None):
            cnt["ev"] += 1
            if cnt["ev"] % 2:
                if scale is None:
                    S.copy("act", out, in_, r, w)
                else:
                    S.act(out, in_, AF.Copy, r, w, scale=scale)
            else:
                if scale is None:
                    S.copy("dve", out, in_, r, w)
                else:
                    S.ts("dve", out, in_, scale, ALU.mult, r, w)

        def nextps():
            cnt["ps"] += 1
            i = cnt["ps"] % 4
            return S.psum[i], S.pbuf[i]

        def fm_piece(p, dst_fn, scale=None):
            slot = p % 3
            for j in range(4):
                dram_ap, sb_ap = dst_fn(j)
                if sb_ap is None:
                    si = cnt["fm"] % 2
                    cnt["fm"] += 1
                    stage, bst = st_fm[si], b_sfm[si]
                    tgt = stage
                else:
                    tgt, bst = sb_ap, b_fx
                for h in range(2):
                    ps, pb = nextps()
                    for k in range(16):
                        S.mm(ps[:, :], wb[slot][:, k, j * 128:(j + 1) * 128], hT[:, k, h * 512:(h + 1) * 512],
                             k == 0, k == 15, [b_wb[slot], b_hT[h]], [pb])
                    if sb_ap is None:
                        evac(tgt[:, h * 512:(h + 1) * 512], ps[:, :], [pb], [bst], scale)
                    else:
                        evac(tgt[:, j, h * 512:(h + 1) * 512], ps[:, :], [pb], [bst], scale)
                if dram_ap is not None:
                    S.dma("sp", dram_ap, tgt[:], [bst], [Buf()])

        def tm_from(lhs_fn, nk, rhs_fn, rbufs, blk):
            si = cnt["tm"] % 2
            cnt["tm"] += 1
            stage, bst = st_tm[si], b_stm[si]
            for t in range(8):
                ps, pb = nextps()
                for k in range(nk):
                    S.mm(ps[:, :], lhs_fn(k, t), rhs_fn(k), k == 0, k == nk - 1, rbufs(t), [pb])
                evac(stage[:, t, :], ps[:, :], [pb], [bst])
            for j in range(4):
                dst = send1[j, blk, :].rearrange("(t p c) -> p t c", p=128, c=128)
                S.dma("sp", dst, stage[:, :, j * 128:(j + 1) * 128], [bst], [Buf()])

        def send_fm(blk):
            return lambda j: (send1[j, blk, :].rearrange("(p t) -> p t", p=128), None)

        wload(0)
        wload(1)
        for p in range(13):
            if p + 2 < 13:
                wload(p + 2)
            slot = p % 3
            if p == 0:
                fm_piece(0, lambda j: (None, fxT))
                for cs, blk in ((0, B_P), (1, B_Q)):
                    tm_from(lambda k, t: fxT[:, k, t * 128:(t + 1) * 128], 4,
                            lambda k, cs=cs: mcs[:, cs, k, :], lambda t: [b_fx, b_mcs], blk)
            elif p == 1:
                fm_piece(1, send_fm(B_FG))
            elif p == 2:
                fm_piece(2, send_fm(B_NQ), scale=0.125)
            elif p == 3:
                fm_piece(3, send_fm(B_NK))
            elif p in (4, 8, 9):
                blk = {4: B_NV, 8: B_RV, 9: B_RG}[p]
                tm_from(lambda k, t: hT[:, k, t * 128:(t + 1) * 128], 16,
                        lambda k, slot=slot: wb[slot][:, k, :], lambda t, slot=slot: [b_hT[t // 4], b_wb[slot]], blk)
            elif p == 5:
                fm_piece(5, send_fm(B_NG))
            elif p == 6:
                fm_piece(6, send_fm(B_RQ), scale=0.125)
            elif p == 7:
                fm_piece(7, send_fm(B_RK))
            elif p == 10:
                fm_piece(10, send_fm(B_CA))
            elif p == 11:
                fm_piece(11, send_fm(B_CB))
            elif p == 12:
                fm_piece(12, lambda j: (io["cvg"][j, :, :], None))
    S.barrier()


def _row_start(r):
    return min(max(r - 4, 0), 56)


def emit_B(S, io):
    nc = S.nc
    recv1 = io["recv1"]
    send2 = io["send2"]

    def load_fm(es, name, blk):
        t = S.sb(es, name, [128, SEQ], BF16)
        b = Buf()
        for i in range(4):
            S.dma(("sp", "act")[i % 2], t[:, i * 1024:(i + 1) * 1024], recv1[i, blk, :].rearrange("(p t) -> p t", p=128), (), [b])
        return t, b

    def load_tm(es, name, blk):
        t = S.sb(es, name, [128, 32, 128], BF16)
        b = Buf()
        for i in range(4):
            S.dma(("sp", "act")[i % 2], t[:, i * 8:(i + 1) * 8, :], recv1[i, blk, :].rearrange("(t p c) -> p t c", p=128, c=128), (), [b])
        return t, b

    def store_o(t, b, blk):
        for i in range(4):
            S.dma("sp", send2[i, blk, :, :], t[:, i * 1024:(i + 1) * 1024], [b], [Buf()])

    with ExitStack() as es0:
        identf = S.sb(es0, "b_identf", [128, 128], F32)
        identb = S.sb(es0, "b_identb", [128, 128], BF16)
        b_id = Buf()
        S.dma("sp", identf[:], io["ident"], (), [b_id])
        S.copy("dve", identb[:], identf[:], [b_id], [b_id])

        qT = S.sb(es0, "n_q", [64, 2, SEQ], BF16)
        kT = S.sb(es0, "n_k", [64, 2, SEQ], BF16)
        ng = S.sb(es0, "n_g", [128, SEQ], BF16)
        ve = S.sb(es0, "n_ve", [128, 32, 128], BF16)
        vo = S.sb(es0, "n_vo", [128, 32, 128], BF16)
        nab = S.sb(es0, "n_bias", [128, 15, 64], F32)
        msk = S.sb(es0, "n_mask", [128, 64], F32)
        b_q, b_k, b_ng, b_ve, b_vo, b_nab = Buf(), Buf(), Buf(), Buf(), Buf(), Buf()

        S.scope = "fft"
        with ExitStack() as es:
            pq = S.sb(es, "f_pq", [128, 64, 128], BF16)
            b_pq = Buf()
            for i in range(4):
                for c_, blk in ((0, B_P), (1, B_Q)):
                    S.dma("sp", pq[c_ * 64 + i * 16:c_ * 64 + (i + 1) * 16, :, :],
                          recv1[i, blk, :].rearrange("(a s c) -> a s c", a=16, c=128), (), [b_pq])
            m1 = S.sb(es, "f_m1", [128, 64, 128], BF16)
            b_m1 = Buf()
            for h in range(4):
                S.dma("act", m1[:, h * 16:(h + 1) * 16, :], io["m1"][:, h * 16:(h + 1) * 16, :], (), [b_m1])
            m3 = S.sb(es, "f_m3", [128, 64], BF16)
            b_m3 = Buf()
            S.dma("act", m3[:], io["m3"], (), [b_m3])
            fg, b_fg = load_fm(es, "f_fg", B_FG)
            gf = S.sb(es, "f_gate", [128, SEQ], BF16)
            b_gf = Buf()
            S.act(gf[:], fg[:], AF.Silu, [b_fg], [b_gf])
            S.scope = "conv"
            ca, b_ca = load_fm(es, "c_a", B_CA)
            cbt, b_cb = load_fm(es, "c_b", B_CB)
            S.act(cbt[:], cbt[:], AF.Sigmoid, [b_cb], [b_cb])
            up = S.sb(es, "c_up", [128, SEQ + 32], BF16)
            b_up = Buf()
            S.memset("pool", up[:, 0:15], 0.0, [b_up])
            S.memset("pool", up[:, 15 + SEQ:SEQ + 32], 0.0, [b_up])
            S.tt("dve", up[:, 15:15 + SEQ], ca[:], cbt[:], ALU.mult, [b_ca, b_cb], [b_up])
            cw = S.sb(es, "c_w", [128, 31], F32)
            cbias = S.sb(es, "c_bias", [128, 1], F32)
            b_cw = Buf()
            S.dma("sp", cw[:], io["cw"], (), [b_cw])
            S.dma("sp", cbias[:], io["cb"], (), [b_cw])
            dg = S.sb(es, "c_diag", [128, 31, 128], BF16)
            b_dg = Buf()
            for k in range(31):
                S.ts("pool", dg[:, k, :], identf[:], cw[:, k:k + 1], ALU.mult, [b_id, b_cw], [b_dg], s2=1.0, op1=ALU.mult)
            S.scope = "na"
            for i in range(4):
                for hh in range(2):
                    S.dma("act", qT[:, hh, i * 1024:(i + 1) * 1024],
                          recv1[i, B_NQ, hh * 65536:(hh + 1) * 65536].rearrange("(p t) -> p t", p=64), (), [b_q])
                    S.dma("act", kT[:, hh, i * 1024:(i + 1) * 1024],
                          recv1[i, B_NK, hh * 65536:(hh + 1) * 65536].rearrange("(p t) -> p t", p=64), (), [b_k])
            for i in range(4):
                S.dma("act", ng[:, i * 1024:(i + 1) * 1024], recv1[i, B_NG, :].rearrange("(p t) -> p t", p=128), (), [b_ng])
                S.dma("act", ve[:, i * 8:(i + 1) * 8, :], recv1[i, B_NV, :].rearrange("(t p c) -> p t c", p=128, c=128), (), [b_ve])
            for i in range(4):
                flat = recv1[i, B_NV, :]
                S.dma("act", vo[:, i * 8:i * 8 + 7, :],
                      flat[64 * 128:(64 + 7 * 128) * 128].rearrange("(t p c) -> p t c", p=128, c=128), (), [b_vo])
                if i < 3:
                    S.dma("act", vo[0:64, i * 8 + 7, :], flat[960 * 128:1024 * 128].rearrange("(p c) -> p c", c=128), (), [b_vo])
                    S.dma("act", vo[64:128, i * 8 + 7, :], recv1[i + 1, B_NV, 0:64 * 128].rearrange("(p c) -> p c", c=128), (), [b_vo])
            S.dma("act", nab[:], io["nab"], (), [b_nab])
            S.dma("act", msk[:], io["namask"], (), [b_nab])
            S.scope = "fft"
            tsb = S.sb(es, "f_t", [128, 64, 128], BF16)
            b_t = Buf()
            for g in range(16):
                ps, pb = S.psum[g % 2], S.pbuf[g % 2]
                for q in range(4):
                    s2 = g * 4 + q
                    S.mm(ps[:, q * 128:(q + 1) * 128], m1[:, s2, :], pq[:, s2, :], True, True, [b_m1, b_pq], [pb])
                S.copy("act" if g % 2 else "dve", tsb[:, g * 4:(g + 1) * 4, :],
                       ps[:, :].rearrange("p (a c) -> p a c", a=4), [pb], [b_t])
            scr = io["fftscr"]
            b_scr = Buf()
            S.dma("sp", scr.rearrange("(p s c) -> p s c", p=128, c=128), tsb[:], [b_t], [b_scr])
            t2 = S.sb(es, "f_t2", [128, 64, 128], BF16)
            b_t2 = Buf()
            src = scr.rearrange("(c k s h) -> c s k h", c=2, k=64, s=64)
            for c_ in range(2):
                S.dma("sp", t2[c_ * 64:(c_ + 1) * 64, :, :], src[c_], [b_scr], [b_t2])
            S.scope = "conv"
            yc = S.sb(es, "c_y", [128, SEQ], BF16)
            b_yc = Buf()
            for t in range(8):
                ps, pb = S.psum[t % 2], S.pbuf[t % 2]
                for k in range(31):
                    S.mm(ps[:, :], dg[:, k, :], up[:, t * 512 + k:t * 512 + k + 512], k == 0, k == 30, [b_dg, b_up], [pb])
                S.ts("dve", yc[:, t * 512:(t + 1) * 512], ps[:, :], cbias[:, 0:1], ALU.add, [pb, b_cw], [b_yc])
            store_o(yc, b_yc, 3)
            S.scope = "fft"
            of = S.sb(es, "f_o", [128, SEQ], BF16)
            b_of = Buf()
            ofv = of[:].rearrange("p (k2 k1) -> p k1 k2", k1=64)
            gfv = gf[:].rearrange("p (k2 k1) -> p k1 k2", k1=64)
            for g in range(8):
                ps, pb = S.psum[2 + g % 2], S.pbuf[2 + g % 2]
                for q in range(8):
                    k1 = g * 8 + q
                    S.mm(ps[:, q * 64:(q + 1) * 64], t2[:, k1, :], m3[:], True, True, [b_t2, b_m3], [pb])
                S.tt("dve", ofv[:, g * 8:(g + 1) * 8, :], ps[:, :].rearrange("p (a k) -> p a k", a=8),
                     gfv[:, g * 8:(g + 1) * 8, :], ALU.mult, [pb, b_gf], [b_of])
            store_o(of, b_of, 0)
        S.barrier()

        S.scope = "na"
        with ExitStack() as es:
            gn = S.sb(es, "n_gate", [128, SEQ], BF16)
            b_gn = Buf()
            S.act(gn[:], ng[:], AF.Silu, [b_ng], [b_gn])
            S.tt("dve", nab[:], nab[:], bc(msk[:], 1, 15), ALU.add, [b_nab], [b_nab])
            on = S.sb(es, "n_o", [128, SEQ], BF16)
            b_on = Buf()
            sc = [S.sb(es, "n_sc%d" % i, [128, 512], F32) for i in range(3)]
            b_sc = [Buf() for _ in range(3)]
            pe_ = [S.sb(es, "n_p%d" % i, [128, 512], BF16) for i in range(3)]
            b_pe = [Buf() for _ in range(3)]
            pn = [S.sb(es, "n_pn%d" % i, [128, 512], BF16) for i in range(3)]
            b_pn = [Buf() for _ in range(3)]
            pt = [S.sb(es, "n_pt%d" % i, [128, 4, 128], BF16) for i in range(3)]
            b_pt = [Buf() for _ in range(3)]
            st = S.sb(es, "n_st", [128, 64, 4], F32)
            b_st = [Buf() for _ in range(64)]
            def na_stage_qk(r):
                d2 = r % 3
                ks = _row_start(r) * 64
                sps, sb_ = S.psum[(0, 1, 6)[d2]], S.pbuf[(0, 1, 6)[d2]]
                for hh in range(2):
                    lo, hi = hh * 64, (hh + 1) * 64
                    S.mm(sps[lo:hi, :], qT[:, hh, r * 64:(r + 1) * 64], kT[:, hh, ks:ks + 512], True, True, [b_q, b_k], [sb_])

            def na_stage_a(r):
                d2 = r % 3
                rs_ = _row_start(r)
                ks = rs_ * 64
                j0 = rs_ - r + 7
                sps, sb_ = S.psum[(0, 1, 6)[d2]], S.pbuf[(0, 1, 6)[d2]]
                S.tt("dve", sc[d2][:], sps[:, :], nab[:, j0:j0 + 8, :].rearrange("p a k -> p (a k)"), ALU.add,
                     [sb_, b_nab], [b_sc[d2]])
                S.red(st[:, r, 0:1], sc[d2][:], ALU.max, [b_sc[d2]], [b_st[r]], negate=True)
                S.act(pe_[d2][:], sc[d2][:], AF.Exp, [b_sc[d2], b_st[r]], [b_pe[d2], b_st[r]], bias=st[:, r, 0:1], accum=st[:, r, 1:2])

            def na_stage_b(r):
                d2 = r % 3
                S.op("dve", lambda e, o=st[:, r, 2:3], i_=st[:, r, 1:2]: e.reciprocal(out=o, in_=i_), [b_st[r]], [b_st[r]])
                S.ts("pool", pn[d2][:], pe_[d2][:], st[:, r, 2:3], ALU.mult, [b_pe[d2], b_st[r]], [b_pn[d2]], s2=1.0, op1=ALU.mult)
                tps, tb = S.psum[(2, 3, 7)[d2]], S.pbuf[(2, 3, 7)[d2]]
                tpsb = tps.bitcast(BF16)
                for c_ in range(4):
                    S.tr(tpsb[:, c_ * 128:(c_ + 1) * 128], pn[d2][:, c_ * 128:(c_ + 1) * 128], identb[:], [b_pn[d2], b_id], [tb])
                S.copy("act", pt[d2][:], tpsb[:, 0:512].rearrange("p (a k) -> p a k", a=4), [tb], [b_pt[d2]])

            def na_stage_c(r):
                d2 = r % 3
                ks = _row_start(r) * 64
                ob = 4 + (r // 8) % 2
                ops_, opb = S.psum[ob], S.pbuf[ob]
                col = (r % 8) * 64
                for hh in range(2):
                    lo, hi = hh * 64, (hh + 1) * 64
                    for c_ in range(4):
                        tok0 = ks + 128 * c_
                        if tok0 % 128 == 0:
                            vt, bv, ti = ve, b_ve, tok0 // 128
                        else:
                            vt, bv, ti = vo, b_vo, (tok0 - 64) // 128
                        S.mm(ops_[lo:hi, col:col + 64], vt[:, ti, lo:hi], pt[d2][:, c_, lo:hi], c_ == 0, c_ == 3, [bv, b_pt[d2]], [opb])
                if r % 8 == 7:
                    r0 = (r // 8) * 8
                    S.tt("dve", on[:, r0 * 64:(r0 + 8) * 64], ops_[:, :], gn[:, r0 * 64:(r0 + 8) * 64], ALU.mult, [opb, b_gn], [b_on])

            na_stage_qk(0)
            na_stage_qk(1)
            for t in range(64 + 2):
                if t + 2 < 64:
                    na_stage_qk(t + 2)
                if t < 64:
                    na_stage_a(t)
                if 0 <= t - 1 < 64:
                    na_stage_b(t - 1)
                if 0 <= t - 2 < 64:
                    na_stage_c(t - 2)
            store_o(on, b_on, 1)
        S.barrier()

        S.scope = "ret"
        with ExitStack() as es:
            qp = S.sb(es, "r_qp", [128, SEQ], BF16)
            kp = S.sb(es, "r_kp", [128, SEQ], BF16)
            b_qp, b_kp = Buf(), Buf()
            with ExitStack() as es1:
                rin = S.sb(es1, "r_in", [128, SEQ], BF16)
                b_rin = Buf()
                rcos = S.sb(es1, "r_cos", [128, SEQ], F32)
                rsin = S.sb(es1, "r_sin", [128, SEQ], F32)
                b_tab = Buf()
                for h in range(4):
                    S.dma("sp", rcos[:, h * 1024:(h + 1) * 1024], io["rcos"][:, h * 1024:(h + 1) * 1024], (), [b_tab])
                    S.dma("act", rsin[:, h * 1024:(h + 1) * 1024], io["rsin"][:, h * 1024:(h + 1) * 1024], (), [b_tab])
                perm = S.sb(es1, "r_perm", [128, 128], BF16)
                b_perm = Buf()
                S.dma("act", perm[:], io["perm"], (), [b_perm])
                t1 = [S.sb(es1, "r_t1%d" % i, [128, 512], F32) for i in range(2)]
                t2_ = [S.sb(es1, "r_t2%d" % i, [128, 512], F32) for i in range(2)]
                b_t1 = [Buf() for _ in range(2)]
                b_t2 = [Buf() for _ in range(2)]
                for blk, dst, bdst in ((B_RQ, qp, b_qp), (B_RK, kp, b_kp)):
                    for i in range(4):
                        S.dma("sp", rin[:, i * 1024:(i + 1) * 1024], recv1[i, blk, :].rearrange("(p t) -> p t", p=128), (), [b_rin])
                    for c_ in range(8):
                        d2 = c_ % 2
                        sl = slice(c_ * 512, (c_ + 1) * 512)
                        ps, pb = S.psum[d2], S.pbuf[d2]
                        S.mm(ps[:, :], perm[:], rin[:, sl], True, True, [b_perm, b_rin], [pb])
                        S.tt("dve", t1[d2][:], ps[:, :], rsin[:, sl], ALU.mult, [pb, b_tab], [b_t1[d2]])
                        S.tt("pool", t2_[d2][:], rin[:, sl], rcos[:, sl], ALU.mult, [b_rin, b_tab], [b_t2[d2]])
                        S.tt("dve", dst[:, sl], t1[d2][:], t2_[d2][:], ALU.add, [b_t1[d2], b_t2[d2]], [bdst])
            S.barrier()
            lg = S.sb(es, "r_lg", [128, 2], F32)
            lg2 = S.sb(es, "r_lg2", [128, 4], F32)
            b_lg = Buf()
            S.dma("sp", lg[:], io["retlg"], (), [b_lg])
            S.dma("sp", lg2[:], io["retlg2"], (), [b_lg])
            for t_ in (lg, lg2):
                S.act(t_[:], t_[:], AF.Exp, [b_lg], [b_lg], scale=-1.0)
                S.act(t_[:], t_[:], AF.Ln, [b_lg], [b_lg], bias=1.0)
                S.ts("dve", t_[:], t_[:], -1.0, ALU.mult, [b_lg], [b_lg])
            rtab = S.sb(es, "r_tab", [128, 4, 128], F32)
            cexp = S.sb(es, "r_cexp", [128, 2], F32)
            b_rt = Buf()
            S.dma("sp", rtab[:], io["rtab"], (), [b_rt])
            S.dma("sp", cexp[:], io["cexp"], (), [b_rt])
            DT = S.sb(es, "r_DT", [128, 2, 128], F32)
            tmpD = S.sb(es, "r_tmpD", [128, 128], F32)
            b_DT, b_tmpD = Buf(), Buf()
            for h in range(2):
                S.act(DT[:, h, :], rtab[:, 0, :], AF.Exp, [b_rt, b_lg], [b_DT], scale=lg2[:, h:h + 1])
                S.act(tmpD[:], rtab[:, 1, :], AF.Exp, [b_rt, b_lg], [b_tmpD], scale=lg2[:, 2 + h:3 + h])
                S.tt("dve", DT[:, h, :], DT[:, h, :], tmpD[:], ALU.add, [b_DT, b_tmpD], [b_DT])
            qdec = S.sb(es, "r_qdec", [128, 2, 128], F32)
            kd = S.sb(es, "r_kd", [128, 4], F32)
            cdec = S.sb(es, "r_cdec", [128, 2], F32)
            b_dec = Buf()
            for dr in range(2):
                S.act(qdec[:, dr, :], rtab[:, 2 + dr, :], AF.Exp, [b_rt, b_lg], [b_dec], scale=lg[:, dr:dr + 1])
                for h in range(2):
                    c_ = dr * 2 + h
                    S.act(kd[:, c_:c_ + 1], cexp[:, dr:dr + 1], AF.Exp, [b_rt, b_lg], [b_dec], scale=lg2[:, c_:c_ + 1])
            S.act(cdec[:], lg[:], AF.Exp, [b_lg], [b_dec], scale=128.0)
            qf = S.sb(es, "r_qf", [128, SEQ], BF16)
            qb = S.sb(es, "r_qb", [128, SEQ], BF16)
            b_qf, b_qb = Buf(), Buf()
            qp3 = qp[:].rearrange("p (n i) -> p n i", i=128)
            S.tt("dve", qf[:].rearrange("p (n i) -> p n i", i=128), qp3, bc(qdec[:, 0, :], 1, 32), ALU.mult, [b_qp, b_dec], [b_qf])
            S.tt("pool", qb[:].rearrange("p (n i) -> p n i", i=128), qp3, bc(qdec[:, 1, :], 1, 32), ALU.mult, [b_qp, b_dec], [b_qb])
            ktok = S.sb(es, "r_ktok", [128, 32, 128], BF16)
            b_ktok = Buf()
            for g in range(8):
                ps, pb = S.psum[g % 2], S.pbuf[g % 2]
                psb_ = ps.bitcast(BF16)
                for q in range(4):
                    n = g * 4 + q
                    S.tr(psb_[:, q * 128:(q + 1) * 128], kp[:, n * 128:(n + 1) * 128], identb[:], [b_kp, b_id], [pb])
                S.copy("act" if g % 2 else "dve", ktok[:, g * 4:(g + 1) * 4, :], psb_[:, 0:512].rearrange("p (a k) -> p a k", a=4), [pb], [b_ktok])
            vr, b_vr = load_tm(es, "r_v", B_RV)
            rg, b_rg = load_tm(es, "r_g", B_RG)
            S.act(rg[:], rg[:], AF.Silu, [b_rg], [b_rg])
            vf = S.sb(es, "r_vf", [128, 32, 128], BF16)
            vb = S.sb(es, "r_vb", [128, 32, 128], BF16)
            b_vf, b_vb = Buf(), Buf()
            for h in range(2):
                sl = slice(h * 64, (h + 1) * 64)
                S.ts("dve", vf[:, :, sl], vr[:, :, sl], kd[:, h:h + 1], ALU.mult, [b_vr, b_dec], [b_vf])
                S.ts("pool", vb[:, :, sl], vr[:, :, sl], kd[:, 2 + h:3 + h], ALU.mult, [b_vr, b_dec], [b_vb], s2=1.0, op1=ALU.mult)
            sf = S.sb(es, "r_sf", [128, 32, 64], F32)
            sbk = S.sb(es, "r_sb", [128, 32, 64], F32)
            b_sf, b_sbk = Buf(), Buf()
            S.memset("pool", sf[:, 0, :], 0.0, [b_sf])
            S.memset("pool", sbk[:, 31, :], 0.0, [b_sbk])
            for dr in range(2):
                vt, bvt = (vf, b_vf) if dr == 0 else (vb, b_vb)
                st_, bst_ = (sf, b_sf) if dr == 0 else (sbk, b_sbk)
                order = list(range(0, 31)) if dr == 0 else list(range(31, 0, -1))
                for g0 in range(0, 31, 8):
                    grp = order[g0:g0 + 8]
                    bi = 2 + (g0 // 8) % 2
                    ps, pb = S.psum[bi], S.pbuf[bi]
                    for q, n in enumerate(grp):
                        for h in range(2):
                            sl = slice(h * 64, (h + 1) * 64)
                            S.mm(ps[sl, q * 64:(q + 1) * 64], ktok[:, n, sl], vt[:, n, sl], True, True, [b_ktok, bvt], [pb])
                    for q, n in enumerate(grp):
                        nxt = n + 1 if dr == 0 else n - 1
                        S.stt(st_[:, nxt, :], st_[:, n, :], cdec[:, dr:dr + 1], ps[:, q * 64:(q + 1) * 64], ALU.mult, ALU.add,
                              [bst_, pb, b_dec], [bst_])
            sfb = S.sb(es, "r_sfb", [128, 32, 64], BF16)
            sbb = S.sb(es, "r_sbb", [128, 32, 64], BF16)
            b_sfb, b_sbb = Buf(), Buf()
            S.copy("act", sfb[:], sf[:], [b_sf], [b_sfb])
            S.copy("act", sbb[:], sbk[:], [b_sbk], [b_sbb])
            orT = S.sb(es, "r_o", [128, SEQ], BF16)
            b_or = Buf()
            ad = [S.sb(es, "r_ad%d" % i, [128, 4, 128], BF16) for i in range(2)]
            b_ad = [Buf() for _ in range(2)]
            osb = [S.sb(es, "r_osb%d" % i, [128, 4, 2, 64], F32) for i in range(2)]
            osq = [S.sb(es, "r_osq%d" % i, [128, 4, 2, 64], F32) for i in range(2)]
            onb = [S.sb(es, "r_onb%d" % i, [128, 4, 128], BF16) for i in range(2)]
            b_osb = [Buf() for _ in range(2)]
            b_osq = [Buf() for _ in range(2)]
            b_onb = [Buf() for _ in range(2)]
            rst = S.sb(es, "r_rst", [128, 8, 8], F32)
            b_rst = [Buf() for _ in range(8)]
            def ret_stage1(g):
                d2 = g % 2
                for h in range(2):
                    sl = slice(h * 64, (h + 1) * 64)
                    aps, apb = S.psum[h], S.pbuf[h]
                    for cn in range(4):
                        n = g * 4 + cn
                        S.mm(aps[:, cn * 128:(cn + 1) * 128], kp[sl, n * 128:(n + 1) * 128],
                             qp[sl, n * 128:(n + 1) * 128], True, True, [b_kp, b_qp], [apb])
                for h in range(2):
                    aps, apb = S.psum[h], S.pbuf[h]
                    S.tt("dve", ad[h][:], aps[:, :].rearrange("p (c i) -> p c i", c=4), bc(DT[:, h, :], 1, 4), ALU.mult,
                         [apb, b_DT], [b_ad[h]])
                for h in range(2):
                    sl = slice(h * 64, (h + 1) * 64)
                    ops_, opb = S.psum[2 + d2 * 2 + h], S.pbuf[2 + d2 * 2 + h]
                    for cn in range(4):
                        n = g * 4 + cn
                        oc = cn * 64
                        S.mm(ops_[:, oc:oc + 64], ad[h][:, cn, :], vr[:, n, sl], True, False, [b_ad[h], b_vr], [opb])
                        if n > 0:
                            S.mm(ops_[:, oc:oc + 64], qf[sl, n * 128:(n + 1) * 128], sfb[sl, n, :], False, n == 31,
                                 [b_qf, b_sfb], [opb])
                        if n < 31:
                            S.mm(ops_[:, oc:oc + 64], qb[sl, n * 128:(n + 1) * 128], sbb[sl, n, :], False, True,
                                 [b_qb, b_sbb], [opb])
                    S.copy("act", osb[d2][:, :, h, :], ops_[:, 0:256].rearrange("p (c e) -> p c e", e=64), [opb], [b_osb[d2]])

            def ret_stage2(g):
                d2 = g % 2
                S.tt("pool", osq[d2][:], osb[d2][:], osb[d2][:], ALU.mult, [b_osb[d2]], [b_osq[d2]])
                S.red(rst[:, g, :], osq[d2][:].rearrange("p c h e -> p (c h) e"), ALU.add, [b_osq[d2]], [b_rst[g]])
                S.ts("dve", rst[:, g, :], rst[:, g, :], 1.0 / 64.0, ALU.mult, [b_rst[g]], [b_rst[g]], s2=EPS, op1=ALU.add)
                S.act(rst[:, g, :], rst[:, g, :], AF.Sqrt, [b_rst[g]], [b_rst[g]])
                S.op("dve", lambda e, o=rst[:, g, :]: e.reciprocal(out=o, in_=o), [b_rst[g]], [b_rst[g]])
                S.tt("dve", osb[d2][:].rearrange("p c h e -> p (c h) e"), osb[d2][:].rearrange("p c h e -> p (c h) e"),
                     bc(rst[:, g, :], 2, 64), ALU.mult, [b_osb[d2], b_rst[g]], [b_osb[d2]])
                S.tt("pool", onb[d2][:], osb[d2][:].rearrange("p c h e -> p c (h e)"), rg[:, g * 4:(g + 1) * 4, :], ALU.mult,
                     [b_osb[d2], b_rg], [b_onb[d2]])
                tps, tb = S.psum[6 + d2], S.pbuf[6 + d2]
                tpsb = tps.bitcast(BF16)
                for cn in range(4):
                    S.tr(tpsb[:, cn * 128:(cn + 1) * 128], onb[d2][:, cn, :], identb[:], [b_onb[d2], b_id], [tb])
                S.copy("act", orT[:, g * 512:(g + 1) * 512], tpsb[:, 0:512], [tb], [b_or])

            ret_stage1(0)
            for g in range(8):
                if g + 1 < 8:
                    ret_stage1(g + 1)
                ret_stage2(g)
            store_o(orT, b_or, 2)
        S.barrier()

    S.barrier()


def emit_C(S, io, last):
    nc = S.nc
    recv2 = io["recv2"]
    with ExitStack() as es:
        oT = S.sb(es, "c_oT", [128, 16, NTOK], BF16)
        b_oT = [Buf() for _ in range(16)]
        wo = S.sb(es, "c_wo", [128, 16, D], BF16)
        b_wo = [Buf() for _ in range(4)]
        for n in range(4):
            S.dma("pool", wo[:, :, n * 512:(n + 1) * 512], io["w_out"][:, n * 512:(n + 1) * 512].rearrange("(k p) c -> p k c", p=128),
                  (), [b_wo[n]])
        gate = S.sb(es, "c_gate", [128, D], F32)
        b_gate = Buf()
        S.dma("sp", gate[:], bc(io["gate"], 0, 128), (), [b_gate])
        if last:
            fg = S.sb(es, "c_fg", [128, D], F32)
            b_fgn = Buf()
            S.dma("sp", fg[:], bc(io["final_g"], 0, 128), (), [b_fgn])
        es1 = es
        if True:
            yc = S.sb(es1, "c_yc", [128, 4, NTOK], BF16)
            b_yc = Buf()
            for j in range(4):
                S.dma("sp", yc[:, j, :], recv2[j, 3, :, :], (), [b_yc])
            cg = S.sb(es1, "c_cg", [128, 4, NTOK], BF16)
            b_cg = Buf()
            for j in range(4):
                S.dma("sp", cg[:, j, :], io["cvg"][j, :, :], (), [b_cg])
            for m in range(3):
                for j in range(4):
                    S.dma(("sp", "act")[j % 2], oT[:, m * 4 + j, :], recv2[j, m, :, :], (), [b_oT[m * 4 + j]])
            S.act(cg[:], cg[:], AF.Silu, [b_cg], [b_cg])
            wpw = S.sb(es1, "c_wpw", [128, 4, 512], BF16)
            b_wpw = Buf()
            S.dma("pool", wpw[:], io["w_pw"].rearrange("(k p) c -> p k c", p=128), (), [b_wpw])
            lngb = S.sb(es1, "c_lngb", [128, 2, 4], F32)
            b_ln = Buf()
            S.dma("sp", lngb[:, 0, :], io["lng"].rearrange("(c p) -> p c", p=128), (), [b_ln], slow=True)
            S.dma("sp", lngb[:, 1, :], io["lnb"].rearrange("(c p) -> p c", p=128), (), [b_ln], slow=True)
            ones = S.sb(es1, "c_ones", [128, 128], BF16)
            b_ones = Buf()
            S.memset("pool", ones[:], 1.0 / 512.0, [b_ones])
            ysq = S.sb(es1, "c_ysq", [128, 4, NTOK], BF16)
            b_ysq = Buf()
            S.act(ysq[:], yc[:], AF.Square, [b_yc], [b_ysq])
            sT = S.sb(es1, "c_sT", [128, 4, NTOK], BF16)
            b_sT = [Buf() for _ in range(2)]
            msq = S.sb(es1, "c_msq", [128, 512], F32)
            rstd = S.sb(es1, "c_rstd", [128, 512], F32)
            b_msq, b_rstd = Buf(), Buf()
            dtmp = [S.sb(es1, "c_d%d" % i, [128, 512], F32) for i in range(2)]
            b_dt = [Buf() for _ in range(2)]
            def ln_half(h):
                hs = slice(h * 512, (h + 1) * 512)
                pm, bpm = S.psum[0], S.pbuf[0]
                pq_, bpq = S.psum[1], S.pbuf[1]
                for j in range(4):
                    S.mm(pm[:, :], ones[:], yc[:, j, hs], j == 0, j == 3, [b_ones, b_yc], [bpm])
                for j in range(4):
                    S.mm(pq_[:, :], ones[:], ysq[:, j, hs], j == 0, j == 3, [b_ones, b_ysq], [bpq])
                S.act(msq[:], pm[:, :], AF.Square, [bpm], [b_msq])
                S.tt("dve", rstd[:], pq_[:, :], msq[:], ALU.subtract, [bpq, b_msq], [b_rstd])
                S.act(rstd[:], rstd[:], AF.Ln, [b_rstd], [b_rstd], bias=EPS)
                S.act(rstd[:], rstd[:], AF.Exp, [b_rstd], [b_rstd], scale=-0.5)
                for j in range(4):
                    d2 = j % 2
                    S.tt("dve", dtmp[d2][:], yc[:, j, hs], pm[:, :], ALU.subtract, [b_yc, bpm], [b_dt[d2]])
                    S.tt("pool", dtmp[d2][:], dtmp[d2][:], rstd[:], ALU.mult, [b_dt[d2], b_rstd], [b_dt[d2]])
                    S.ts("dve", dtmp[d2][:], dtmp[d2][:], lngb[:, 0, j:j + 1], ALU.mult, [b_dt[d2], b_ln], [b_dt[d2]],
                         s2=lngb[:, 1, j:j + 1], op1=ALU.add)
                    S.act(sT[:, j, hs], dtmp[d2][:], AF.Silu, [b_dt[d2]], [b_sT[h]])
            def pw_half(h):
                hs = slice(h * 512, (h + 1) * 512)
                for co in range(4):
                    ps, pb = S.psum[2 + co % 2], S.pbuf[2 + co % 2]
                    for ci in range(4):
                        S.mm(ps[:, :], wpw[:, ci, co * 128:(co + 1) * 128], sT[:, ci, hs], ci == 0, ci == 3, [b_wpw, b_sT[h]], [pb])
                    S.tt("dve", oT[:, 12 + co, hs], ps[:, :], cg[:, co, hs], ALU.mult, [pb, b_cg], [b_oT[12 + co]])
        xt = [S.sb(es, "c_x%d" % i, [128, D], F32) for i in range(2)]
        b_xt = [Buf() for _ in range(2)]
        tmp = [S.sb(es, "c_tmp%d" % i, [128, 512], F32) for i in range(2)]
        b_tmp = [Buf() for _ in range(2)]
        junk = S.sb(es, "c_junk", [128, D], F32)
        b_junk = Buf()
        ss = S.sb(es, "c_ss", [128, 8], F32)
        b_ss = [Buf() for _ in range(8)]
        outs = []
        def outproj_tile(t):
            d2 = t % 2
            S.dma("sp", xt[d2][:], io["x"][t * 128:(t + 1) * 128, :], (), [b_xt[d2]])
            for n in range(4):
                pi = 4 + (t * 4 + n) % 4
                ps, pb = S.psum[pi], S.pbuf[pi]
                for k in range(16):
                    S.mm(ps[:, :], oT[:, k, t * 128:(t + 1) * 128], wo[:, k, n * 512:(n + 1) * 512], k == 0, k == 15,
                         [b_oT[k], b_wo[n]], [pb])
                ti = (t * 4 + n) % 2
                S.tt("dve", tmp[ti][:], ps[:, :], gate[:, n * 512:(n + 1) * 512], ALU.mult, [pb, b_gate], [b_tmp[ti]])
                S.tt("pool", xt[d2][:, n * 512:(n + 1) * 512], xt[d2][:, n * 512:(n + 1) * 512], tmp[ti][:], ALU.add,
                     [b_xt[d2], b_tmp[ti]], [b_xt[d2]])
            if last:
                S.act(junk[:], xt[d2][:], AF.Square, [b_xt[d2]], [b_junk, b_ss[t]], accum=ss[:, t:t + 1])
                S.ts("dve", ss[:, t:t + 1], ss[:, t:t + 1], 1.0 / D, ALU.mult, [b_ss[t]], [b_ss[t]], s2=EPS, op1=ALU.add)
                S.act(ss[:, t:t + 1], ss[:, t:t + 1], AF.Sqrt, [b_ss[t]], [b_ss[t]])
                S.op("dve", lambda e, o=ss[:, t:t + 1]: e.reciprocal(out=o, in_=o), [b_ss[t]], [b_ss[t]])
                S.stt(xt[d2][:], xt[d2][:], ss[:, t:t + 1], fg[:], ALU.mult, ALU.mult, [b_xt[d2], b_ss[t], b_fgn], [b_xt[d2]])
            ob = Buf()
            outs.append(ob)
            S.dma("sp", io["xout"][t * 128:(t + 1) * 128, :], xt[d2][:], [b_xt[d2]], [ob])

        ln_half(0)
        pw_half(0)
        ln_half(1)
        for t in range(4):
            outproj_tile(t)
        pw_half(1)
        for t in range(4, 8):
            outproj_tile(t)
    S.barrier()


_PROGS = {}


CONST_BF16 = ("m1", "m3", "perm")


def _const_io(nc, io, names):
    for n in names:
        io[n] = dram_in(nc, n, CONST_SHAPES[n], BF16 if n in CONST_BF16 else F32)


def prog_mod():
    if "mod" in _PROGS:
        return _PROGS["mod"]
    nc = bass.Bass("TRN2", target_bir_lowering=False)
    io = {"cT": dram_in(nc, "cT", [128, 16, 2]), "wada": dram_in(nc, "wada", [D, 1536]),
          "bada": dram_in(nc, "bada", [2, 1536]), "mod": dram_out(nc, "mod", [2, 1536])}
    with ExitStack() as es:
        S = Sched(nc, es)
        emit_mod(S, io)
        S.run()
    _PROGS["mod"] = nc
    return nc


def prog_A():
    if "A" in _PROGS:
        return _PROGS["A"]
    nc = bass.Bass("TRN2", target_bir_lowering=False)
    io = {"x": dram_in(nc, "x", [NTOK, D]), "shift": dram_in(nc, "shift", [D]), "scale": dram_in(nc, "scale", [D]),
          "norm_g": dram_in(nc, "norm_g", [D]), "w_in": dram_in(nc, "w_in", [D, DIN]), "w_fft": dram_in(nc, "w_fft", [512, 512]),
          "send1": dram_out(nc, "send1", [4, NBLK1, BLK], BF16), "cvg": dram_out(nc, "cvg", [4, 128, NTOK], BF16)}
    _const_io(nc, io, ["ident", "ccsc"])
    with ExitStack() as es:
        S = Sched(nc, es)
        emit_A(S, io)
        S.run()
    _PROGS["A"] = nc
    return nc


B_CONSTS = ["ident", "m1", "m3", "namask", "rcos", "rsin", "perm", "rtab", "cexp"]


def prog_B():
    if "B" in _PROGS:
        return _PROGS["B"]
    nc = bass.Bass("TRN2", target_bir_lowering=False)
    io = {"recv1": dram_in(nc, "recv1", [4, NBLK1, BLK], BF16), "nab": dram_in(nc, "nab", [128, 15, 64]),
          "retlg": dram_in(nc, "retlg", [128, 2]), "retlg2": dram_in(nc, "retlg2", [128, 4]),
          "cw": dram_in(nc, "cw", [128, 31]), "cb": dram_in(nc, "cb", [128, 1]),
          "fftscr": dram_tmp(nc, "fftscr", [128 * 64 * 128], BF16),
          "send2": dram_out(nc, "send2", [4, 4, 128, NTOK], BF16)}
    _const_io(nc, io, B_CONSTS)
    with ExitStack() as es:
        S = Sched(nc, es)
        emit_B(S, io)
        S.run()
    _PROGS["B"] = nc
    return nc


def prog_C(last):
    key = "C%d" % int(last)
    if key in _PROGS:
        return _PROGS[key]
    nc = bass.Bass("TRN2", target_bir_lowering=False)
    io = {"recv2": dram_in(nc, "recv2", [4, 4, 128, NTOK], BF16), "cvg": dram_in(nc, "cvg", [4, 128, NTOK], BF16),
          "x": dram_in(nc, "x", [NTOK, D]), "gate": dram_in(nc, "gate", [D]), "w_out": dram_in(nc, "w_out", [D, D]),
          "w_pw": dram_in(nc, "w_pw", [512, 512]), "lng": dram_in(nc, "lng", [512]), "lnb": dram_in(nc, "lnb", [512]),
          "xout": dram_out(nc, "xout", [NTOK, D])}
    if last:
        io["final_g"] = dram_in(nc, "final_g", [D])
    with ExitStack() as es:
        S = Sched(nc, es)
        emit_C(S, io, last)
        S.run()
    _PROGS[key] = nc
    return nc


def prog_CA():
    if "CA" in _PROGS:
        return _PROGS["CA"]
    nc = bass.Bass("TRN2", target_bir_lowering=False)
    xmid = dram_tmp(nc, "xmid", [NTOK, D])
    ioc = {"recv2": dram_in(nc, "recv2", [4, 4, 128, NTOK], BF16), "cvg": dram_in(nc, "cvg", [4, 128, NTOK], BF16),
           "x": dram_in(nc, "x", [NTOK, D]), "gate": dram_in(nc, "gate", [D]), "w_out": dram_in(nc, "w_out", [D, D]),
           "w_pw": dram_in(nc, "w_pw", [512, 512]), "lng": dram_in(nc, "lng", [512]), "lnb": dram_in(nc, "lnb", [512]),
           "xout": xmid}
    ioa = {"x": xmid, "shift": dram_in(nc, "shift", [D]), "scale": dram_in(nc, "scale", [D]),
           "norm_g": dram_in(nc, "norm_g", [D]), "w_in": dram_in(nc, "w_in", [D, DIN]), "w_fft": dram_in(nc, "w_fft", [512, 512]),
           "send1": dram_out(nc, "send1", [4, NBLK1, BLK], BF16), "cvg": dram_out(nc, "cvg_next", [4, 128, NTOK], BF16)}
    _const_io(nc, ioa, ["ident", "ccsc"])
    xcopy = dram_out(nc, "xout", [NTOK, D])
    with ExitStack() as es:
        S = Sched(nc, es)
        emit_C(S, ioc, False)
        bx = Buf()
        S.dma("sp", xcopy, xmid, (), [bx])
        emit_A(S, ioa)
        S.run()
    _PROGS["CA"] = nc
    return nc


def _run(nc, in_maps):
    res = run_bass_kernel_spmd(nc, in_maps, core_ids=list(range(NCORE)))
    return res.results


def _percore_layer_inputs(l, na_rel_bias, ret_logit_fwd, ret_logit_bwd, conv_w, conv_b):
    col = np.arange(64)
    idx = np.clip(col[None, :] - col[:, None] + 15, 0, 30)
    outs = []
    for j in range(4):
        nab = np.empty((128, 15, 64), np.float32)
        for hh in range(2):
            rb = na_rel_bias[l, 2 * j + hh]
            nab[hh * 64:(hh + 1) * 64] = np.transpose(rb[:, idx], (1, 0, 2))
        lf = ret_logit_fwd[l, 2 * j:2 * j + 2]
        lb = ret_logit_bwd[l, 2 * j:2 * j + 2]
        retlg = np.empty((128, 2), np.float32)
        retlg[0:64, 0], retlg[64:128, 0] = lf[0], lf[1]
        retlg[0:64, 1], retlg[64:128, 1] = lb[0], lb[1]
        retlg2 = np.empty((128, 4), np.float32)
        retlg2[:, 0], retlg2[:, 1], retlg2[:, 2], retlg2[:, 3] = lf[0], lf[1], lb[0], lb[1]
        cw = np.ascontiguousarray(conv_w[l][:, j * 128:(j + 1) * 128].T)
        cb = np.ascontiguousarray(conv_b[l][j * 128:(j + 1) * 128, None])
        outs.append(dict(nab=nab, retlg=retlg, retlg2=retlg2, cw=cw, cb=cb))
    return outs


def kernel(x, c, norm_g, w_ada, b_ada, w_in, w_fft, na_rel_bias, ret_logit_fwd, ret_logit_bwd,
           conv_w, conv_b, conv_ln_g, conv_ln_b, conv_w_pw, w_out, final_g, _debug=None):
    f32 = np.float32
    x = np.asarray(x, f32)
    C = consts()
    cT = np.ascontiguousarray(np.asarray(c, f32).T.reshape(16, 128, NB).transpose(1, 0, 2))
    maps = []
    for k in range(NCORE):
        l, c0 = k // 4, (k % 4) * 1536
        maps.append(dict(cT=cT, wada=np.ascontiguousarray(w_ada[l][:, c0:c0 + 1536]),
                         bada=np.ascontiguousarray(np.broadcast_to(b_ada[l][None, c0:c0 + 1536], (2, 1536)))))
    r = _run(prog_mod(), maps)
    mod = np.stack([np.concatenate([np.asarray(r[l * 4 + q]["mod"]) for q in range(4)], axis=1) for l in range(DEPTH)])
    xs = [np.ascontiguousarray(x[k // 4, (k % 4) * NTOK:(k % 4 + 1) * NTOK]) for k in range(NCORE)]

    def a_inputs(l, k):
        b = k // 4
        return dict(shift=np.ascontiguousarray(mod[l, b, 0:D]), scale=np.ascontiguousarray(mod[l, b, D:2 * D]),
                    norm_g=norm_g[l], w_in=w_in[l], w_fft=w_fft[l], ident=C["ident"], ccsc=C["ccsc"])

    send1 = cvg = None
    for l in range(DEPTH):
        last = (l == DEPTH - 1)
        if l == 0:
            rA = _run(prog_A(), [dict(x=xs[k], **a_inputs(0, k)) for k in range(NCORE)])
            send1 = [np.asarray(rA[k]["send1"]) for k in range(NCORE)]
            cvg = [np.asarray(rA[k]["cvg"]) for k in range(NCORE)]
        pl = _percore_layer_inputs(l, na_rel_bias, ret_logit_fwd, ret_logit_bwd, conv_w, conv_b)
        maps = []
        for k in range(NCORE):
            b, j = k // 4, k % 4
            recv1 = np.stack([send1[b * 4 + i][j] for i in range(4)])
            m = dict(recv1=recv1, **pl[j])
            for n in B_CONSTS:
                m[n] = C[n]
            maps.append(m)
        rB = _run(prog_B(), maps)
        send2 = [np.asarray(rB[k]["send2"]) for k in range(NCORE)]
        maps = []
        for k in range(NCORE):
            b, i = k // 4, k % 4
            recv2 = np.stack([send2[b * 4 + j][i] for j in range(4)])
            m = dict(recv2=recv2, cvg=cvg[k], x=xs[k], gate=np.ascontiguousarray(mod[l, b, 2 * D:3 * D]), w_out=w_out[l],
                     w_pw=conv_w_pw[l], lng=conv_ln_g[l], lnb=conv_ln_b[l])
            if last:
                m["final_g"] = final_g
            else:
                m.update(a_inputs(l + 1, k))
            maps.append(m)
        if last:
            rC = _run(prog_C(True), maps)
        else:
            rC = _run(prog_CA(), maps)
            send1 = [np.asarray(rC[k]["send1"]) for k in range(NCORE)]
            cvg = [np.asarray(rC[k]["cvg_next"]) for k in range(NCORE)]
        xs = [np.asarray(rC[k]["xout"]) for k in range(NCORE)]
    out = np.empty((NB, SEQ, D), f32)
    for k in range(NCORE):
        out[k // 4, (k % 4) * NTOK:(k % 4 + 1) * NTOK] = xs[k]
    return out
```
